# Optimizing a Trainium2 kernel written in Bass

```python
import math
import jax, jax.numpy as jnp
from jax import lax
import numpy as np


D_MODEL = 2048
BATCH = 2
SEQ = 4096
DEPTH = 2
DEC_BATCH = 32
DEC_SEQ = 32
PAST_LEN = 4096

CHUNK = 64
N_EVEN = (DEPTH + 1) // 2
N_ODD = DEPTH // 2
A_WIDTH = D_MODEL // 2
A_HEADS = 8
A_HEAD_DIM = A_WIDTH // A_HEADS
A_BLOCK = 128
B_WIDTH = D_MODEL - A_WIDTH
B_HEADS = 4
B_KEY_DIM = 128
B_KEY_WIDTH = B_HEADS * B_KEY_DIM
B_VAL_DIM = B_WIDTH // B_HEADS
B_GATE_RANK = 16
B_GATE_TAU = 16.0
IN_EVEN = 2 * A_WIDTH + 2 * B_KEY_WIDTH + 2 * B_WIDTH + B_GATE_RANK
C_WIDTH = D_MODEL // 2
POOL_WINDOWS = (2, 4, 8, 16)
C_GROUPS = 4
C_GROUP_DIM = C_WIDTH // C_GROUPS
POOL_BUF = max(POOL_WINDOWS) - 1
D_WIDTH = D_MODEL - C_WIDTH
S5_GROUP_DIM = 16
S5_GROUPS = D_WIDTH // S5_GROUP_DIM
S5_STATE = 64
IN_ODD = C_WIDTH + D_WIDTH
D_FF = 5632
CONV_W = 3
EPS = 1e-6

kernel_name = "hybrid_streaming_encoder_step"


def rms_norm(x, g):
    xf = x.astype(jnp.float32)
    y = xf * lax.rsqrt(jnp.mean(xf * xf, axis=-1, keepdims=True) + EPS)
    return (y * g.astype(jnp.float32)).astype(x.dtype)


def layer_norm(x, g):
    xf = x.astype(jnp.float32)
    mu = jnp.mean(xf, axis=-1, keepdims=True)
    xc = xf - mu
    y = xc * lax.rsqrt(jnp.mean(xc * xc, axis=-1, keepdims=True) + EPS)
    return (y * g.astype(jnp.float32)).astype(x.dtype)


def block_causal_mask(n):
    i = jnp.arange(n)
    return (i[None, :] // CHUNK) <= (i[:, None] // CHUNK)


def spatial_gate(u, v, w_s, b_s):
    L = u.shape[2]
    w = jnp.where(block_causal_mask(L)[None], w_s[:, :L, :L], 0.0)
    s = jnp.einsum('hij,bnjhd->bnihd', w, v) + b_s[:, :L].T[None, None, :, :, None]
    return u * s


def gla_chunks(q, k, v, log_g, s0):
    def step(s, inp):
        qc, kc, vc, gc = inp
        cum = jnp.cumsum(gc, axis=1)
        tot = cum[:, -1]
        kd = kc * jnp.exp(tot[:, None] - cum)
        s = jnp.exp(tot)[..., None] * s + jnp.einsum('blhk,blhv->bhkv', kd, vc)
        o = jnp.einsum('blhk,bhkv->blhv', qc, s)
        return s, o
    xs = (jnp.moveaxis(q, 1, 0), jnp.moveaxis(k, 1, 0), jnp.moveaxis(v, 1, 0), jnp.moveaxis(log_g, 1, 0))
    s, o = lax.scan(step, s0, xs)
    return jnp.moveaxis(o, 0, 1), s


def even_mixer(h, s0, w_in, w_s, b_s, v_gain, w_gate, gate_bias, o_gain, w_out, a_len, b_len):
    bsz, t, _ = h.shape
    f32 = jnp.float32
    proj = h @ w_in
    o1 = A_WIDTH; o2 = o1 + A_WIDTH; o3 = o2 + B_KEY_WIDTH; o4 = o3 + B_KEY_WIDTH
    o5 = o4 + B_WIDTH; o6 = o5 + B_WIDTH
    a_u, a_v, q, k, v, r, g_lr = jnp.split(proj, [o1, o2, o3, o4, o5, o6], axis=-1)
    a_u = jax.nn.gelu(a_u)
    a_v = layer_norm(jax.nn.gelu(a_v), v_gain)
    shp = (bsz, t // a_len, a_len, A_HEADS, A_HEAD_DIM)
    a_out = spatial_gate(a_u.reshape(shp), a_v.reshape(shp), w_s, b_s).reshape(bsz, t, A_WIDTH)
    log_g = jax.nn.log_sigmoid((g_lr @ w_gate + gate_bias).astype(f32)) / B_GATE_TAU
    nc = t // b_len
    def heads(z, d):
        return z.astype(f32).reshape(bsz, nc, b_len, B_HEADS, d)
    o, s_new = gla_chunks(heads(q, B_KEY_DIM) * (B_KEY_DIM ** -0.5), heads(k, B_KEY_DIM),
                          heads(v, B_VAL_DIM), heads(log_g, B_KEY_DIM), s0.astype(f32))
    o = o * lax.rsqrt(jnp.mean(o * o, axis=-1, keepdims=True) + EPS)
    o = o.reshape(bsz, t, B_WIDTH) * o_gain.astype(f32) * jax.nn.silu(r.astype(f32))
    y = jnp.concatenate([a_out, o.astype(h.dtype)], axis=-1) @ w_out
    return y, s_new.astype(s0.dtype), a_v


def pool_mix(c, buf, pos0, c_map, c_scale):
    bsz, t, _ = c.shape
    f32 = jnp.float32
    xp = jnp.concatenate([buf.astype(c.dtype), c], axis=1)
    new_buf = xp[:, -POOL_BUF:]
    xf = xp.astype(f32)
    cs = jnp.concatenate([jnp.zeros((bsz, 1, C_WIDTH), f32), jnp.cumsum(xf, axis=1)], axis=1)
    pos = pos0 + jnp.arange(t)
    groups = []
    for g, w in enumerate(POOL_WINDOWS):
        lo, hi = g * C_GROUP_DIM, (g + 1) * C_GROUP_DIM
        total = cs[:, POOL_BUF + 1:POOL_BUF + 1 + t, lo:hi] - cs[:, POOL_BUF + 1 - w:POOL_BUF + 1 - w + t, lo:hi]
        cnt = jnp.minimum(pos + 1, w).astype(f32)
        groups.append(total / cnt[None, :, None])
    pooled = jnp.stack(groups, axis=2)
    delta = pooled - xf[:, POOL_BUF:].reshape(bsz, t, C_GROUPS, C_GROUP_DIM)
    y = jnp.einsum('btgc,gcd->btgd', delta, c_map.astype(f32)).reshape(bsz, t, C_WIDTH) * c_scale.astype(f32)
    return y.astype(c.dtype), new_buf


def s5_mix(u, s_re0, s_im0, a_re, a_im, log_dt, b_re, b_im, c_re, c_im, d_skip, w_glu):
    bsz, t, _ = u.shape
    f32 = jnp.float32
    ug = u.astype(f32).reshape(bsz, t, S5_GROUPS, S5_GROUP_DIM)
    lam = lax.complex(a_re.astype(f32), a_im.astype(f32))
    dt = jnp.exp(log_dt.astype(f32))[:, None]
    lam_bar = jnp.exp(lam * dt)
    b_bar = ((lam_bar - 1.0) / lam)[..., None] * lax.complex(b_re.astype(f32), b_im.astype(f32))
    bu = lax.complex(jnp.einsum('btgh,gph->btgp', ug, jnp.real(b_bar)),
                     jnp.einsum('btgh,gph->btgp', ug, jnp.imag(b_bar)))
    a_seq = jnp.broadcast_to(lam_bar, bu.shape)
    def combine(left, right):
        a1, x1 = left
        a2, x2 = right
        return a1 * a2, a2 * x1 + x2
    a_cum, s = lax.associative_scan(combine, (a_seq, bu), axis=1)
    s0 = lax.complex(s_re0.astype(f32), s_im0.astype(f32))
    s = s + a_cum * s0[:, None]
    y = (jnp.einsum('btgp,ghp->btgh', jnp.real(s), c_re.astype(f32))
         - jnp.einsum('btgp,ghp->btgh', jnp.imag(s), c_im.astype(f32))
         + d_skip.astype(f32).reshape(S5_GROUPS, S5_GROUP_DIM) * ug)
    y = jax.nn.gelu(y).reshape(bsz, t, D_WIDTH).astype(u.dtype)
    g_a, g_b = jnp.split(y @ w_glu, 2, axis=-1)
    out = g_a * jax.nn.sigmoid(g_b)
    s_last = s[:, -1]
    return out, jnp.real(s_last).astype(s_re0.dtype), jnp.imag(s_last).astype(s_im0.dtype)


def odd_mixer(h, pool_buf, s_re0, s_im0, pos0, w_in, c_map, c_scale, a_re, a_im, log_dt,
              b_re, b_im, c_re, c_im, d_skip, w_glu, w_out):
    proj = h @ w_in
    c_in, d_in = jnp.split(proj, [C_WIDTH], axis=-1)
    c_out, pool_new = pool_mix(c_in, pool_buf, pos0, c_map, c_scale)
    d_out, re_new, im_new = s5_mix(d_in, s_re0, s_im0, a_re, a_im, log_dt, b_re, b_im, c_re, c_im, d_skip, w_glu)
    y = jnp.concatenate([c_out, d_out], axis=-1) @ w_out
    return y, pool_new, re_new, im_new


def conv_ffn(h, buf, w_up, conv_w, conv_b, w_down):
    t = h.shape[1]
    gate, val = jnp.split(h @ w_up, 2, axis=-1)
    gp = jnp.concatenate([buf.astype(gate.dtype), gate], axis=1)
    conv = conv_b
    for k in range(CONV_W):
        conv = conv + conv_w[k] * gp[:, k:k + t]
    out = (jax.nn.gelu(conv) * val) @ w_down
    return out, gp[:, -(CONV_W - 1):]


def trunk(x, gla0, pool0, s5re0, s5im0, ffn0, pos0, a_len, b_len, p):
    new_gla, new_av, new_pool, new_re, new_im, new_ffn = [], [], [], [], [], []
    for layer in range(DEPTH):
        i = layer // 2
        h = rms_norm(x, p['norm_mix_pre'][layer])
        if layer % 2 == 0:
            m, s_new, av = even_mixer(h, gla0[i], p['w_in_even'][i], p['a_w_s'][i], p['a_b_s'][i],
                                      p['a_v_norm'][i], p['b_w_gate'][i], p['b_gate_bias'][i],
                                      p['b_out_norm'][i], p['w_out_even'][i], a_len, b_len)
            new_gla.append(s_new)
            new_av.append(av)
        else:
            m, pb, sre, sim = odd_mixer(h, pool0[i], s5re0[i], s5im0[i], pos0, p['w_in_odd'][i],
                                        p['c_map'][i], p['c_scale'][i], p['s5_a_re'][i], p['s5_a_im'][i],
                                        p['s5_log_dt'][i], p['s5_b_re'][i], p['s5_b_im'][i],
                                        p['s5_c_re'][i], p['s5_c_im'][i], p['s5_d'][i],
                                        p['s5_w_glu'][i], p['w_out_odd'][i])
            new_pool.append(pb)
            new_re.append(sre)
            new_im.append(sim)
        x = x + rms_norm(m, p['norm_mix_post'][layer])
        h = rms_norm(x, p['norm_ffn_pre'][layer])
        f, fb = conv_ffn(h, ffn0[layer], p['ffn_w_up'][layer], p['ffn_conv_w'][layer],
                         p['ffn_conv_b'][layer], p['ffn_w_down'][layer])
        new_ffn.append(fb)
        x = x + rms_norm(f, p['norm_ffn_post'][layer])
    return (x, jnp.stack(new_gla), jnp.stack(new_av), jnp.stack(new_pool),
            jnp.stack(new_re), jnp.stack(new_im), jnp.stack(new_ffn))


def setup_inputs(seed: int = 0) -> dict:
    key = jax.random.key(seed)
    keys = jax.random.split(key, 64)
    it = iter(range(64))

    def nrm(shape, scale):
        return scale * jax.random.normal(keys[next(it)], shape, jnp.float32)

    n_idx = jnp.arange(S5_STATE, dtype=jnp.float32)
    return {
        'x_prompt': nrm((BATCH, SEQ, D_MODEL), 1.0),
        'x_sample': nrm((DEC_BATCH, DEC_SEQ, D_MODEL), 1.0),
        'state_gla': nrm((N_EVEN, DEC_BATCH, B_HEADS, B_KEY_DIM, B_VAL_DIM), 1.0),
        'state_pool': nrm((N_ODD, DEC_BATCH, POOL_BUF, C_WIDTH), 1.0),
        'state_s5_re': nrm((N_ODD, DEC_BATCH, S5_GROUPS, S5_STATE), 0.3),
        'state_s5_im': nrm((N_ODD, DEC_BATCH, S5_GROUPS, S5_STATE), 0.3),
        'state_ffn_conv': nrm((DEPTH, DEC_BATCH, CONV_W - 1, D_FF), 1.0),
        'norm_mix_pre': 1.0 + nrm((DEPTH, D_MODEL), 0.05),
        'norm_mix_post': 1.0 + nrm((DEPTH, D_MODEL), 0.05),
        'norm_ffn_pre': 1.0 + nrm((DEPTH, D_MODEL), 0.05),
        'norm_ffn_post': 1.0 + nrm((DEPTH, D_MODEL), 0.05),
        'w_in_even': nrm((N_EVEN, D_MODEL, IN_EVEN), D_MODEL ** -0.5),
        'a_w_s': nrm((N_EVEN, A_HEADS, A_BLOCK, A_BLOCK), A_BLOCK ** -0.5),
        'a_b_s': 1.0 + nrm((N_EVEN, A_HEADS, A_BLOCK), 0.1),
        'a_v_norm': 1.0 + nrm((N_EVEN, A_WIDTH), 0.05),
        'b_w_gate': nrm((N_EVEN, B_GATE_RANK, B_KEY_WIDTH), B_GATE_RANK ** -0.5),
        'b_gate_bias': 2.0 + nrm((N_EVEN, B_KEY_WIDTH), 0.5),
        'b_out_norm': 1.0 + nrm((N_EVEN, B_WIDTH), 0.05),
        'w_out_even': nrm((N_EVEN, A_WIDTH + B_WIDTH, D_MODEL), (A_WIDTH + B_WIDTH) ** -0.5),
        'w_in_odd': nrm((N_ODD, D_MODEL, IN_ODD), D_MODEL ** -0.5),
        'c_map': nrm((N_ODD, C_GROUPS, C_GROUP_DIM, C_GROUP_DIM), C_GROUP_DIM ** -0.5),
        'c_scale': 1.0 + nrm((N_ODD, C_WIDTH), 0.1),
        's5_a_re': -0.5 + nrm((N_ODD, S5_GROUPS, S5_STATE), 0.01),
        's5_a_im': math.pi * n_idx + nrm((N_ODD, S5_GROUPS, S5_STATE), 0.01),
        's5_log_dt': jax.random.uniform(keys[next(it)], (N_ODD, S5_GROUPS), jnp.float32,
                                        minval=math.log(1e-3), maxval=math.log(1e-1)),
        's5_b_re': nrm((N_ODD, S5_GROUPS, S5_STATE, S5_GROUP_DIM), (2 * S5_GROUP_DIM) ** -0.5),
        's5_b_im': nrm((N_ODD, S5_GROUPS, S5_STATE, S5_GROUP_DIM), (2 * S5_GROUP_DIM) ** -0.5),
        's5_c_re': nrm((N_ODD, S5_GROUPS, S5_GROUP_DIM, S5_STATE), 0.5),
        's5_c_im': nrm((N_ODD, S5_GROUPS, S5_GROUP_DIM, S5_STATE), 0.5),
        's5_d': nrm((N_ODD, D_WIDTH), 0.5),
        's5_w_glu': nrm((N_ODD, D_WIDTH, 2 * D_WIDTH), D_WIDTH ** -0.5),
        'w_out_odd': nrm((N_ODD, C_WIDTH + D_WIDTH, D_MODEL), (C_WIDTH + D_WIDTH) ** -0.5),
        'ffn_w_up': nrm((DEPTH, D_MODEL, 2 * D_FF), D_MODEL ** -0.5),
        'ffn_conv_w': nrm((DEPTH, CONV_W, D_FF), CONV_W ** -0.5),
        'ffn_conv_b': nrm((DEPTH, D_FF), 0.02),
        'ffn_w_down': nrm((DEPTH, D_FF, D_MODEL), D_FF ** -0.5),
    }


def reference(x_prompt, x_sample, state_gla, state_pool, state_s5_re, state_s5_im, state_ffn_conv,
              norm_mix_pre, norm_mix_post, norm_ffn_pre, norm_ffn_post,
              w_in_even, a_w_s, a_b_s, a_v_norm, b_w_gate, b_gate_bias, b_out_norm, w_out_even,
              w_in_odd, c_map, c_scale, s5_a_re, s5_a_im, s5_log_dt, s5_b_re, s5_b_im, s5_c_re, s5_c_im,
              s5_d, s5_w_glu, w_out_odd, ffn_w_up, ffn_conv_w, ffn_conv_b, ffn_w_down):
    p = dict(norm_mix_pre=norm_mix_pre, norm_mix_post=norm_mix_post, norm_ffn_pre=norm_ffn_pre,
             norm_ffn_post=norm_ffn_post, w_in_even=w_in_even, a_w_s=a_w_s, a_b_s=a_b_s,
             a_v_norm=a_v_norm, b_w_gate=b_w_gate, b_gate_bias=b_gate_bias, b_out_norm=b_out_norm,
             w_out_even=w_out_even, w_in_odd=w_in_odd, c_map=c_map, c_scale=c_scale,
             s5_a_re=s5_a_re, s5_a_im=s5_a_im, s5_log_dt=s5_log_dt, s5_b_re=s5_b_re, s5_b_im=s5_b_im,
             s5_c_re=s5_c_re, s5_c_im=s5_c_im, s5_d=s5_d, s5_w_glu=s5_w_glu, w_out_odd=w_out_odd,
             ffn_w_up=ffn_w_up, ffn_conv_w=ffn_conv_w, ffn_conv_b=ffn_conv_b, ffn_w_down=ffn_w_down)
    bp = x_prompt.shape[0]
    dt = x_prompt.dtype
    (y_prompt, gla_prompt, _, pool_prompt, s5_re_prompt, s5_im_prompt, ffn_prompt) = trunk(
        x_prompt,
        jnp.zeros((N_EVEN, bp, B_HEADS, B_KEY_DIM, B_VAL_DIM), dt),
        jnp.zeros((N_ODD, bp, POOL_BUF, C_WIDTH), dt),
        jnp.zeros((N_ODD, bp, S5_GROUPS, S5_STATE), dt),
        jnp.zeros((N_ODD, bp, S5_GROUPS, S5_STATE), dt),
        jnp.zeros((DEPTH, bp, CONV_W - 1, D_FF), dt),
        0, A_BLOCK, CHUNK, p)
    t_s = x_sample.shape[1]
    (y_sample, gla_sample, av_sample, pool_sample, s5_re_sample, s5_im_sample, ffn_sample) = trunk(
        x_sample, state_gla, state_pool, state_s5_re, state_s5_im, state_ffn_conv,
        PAST_LEN, t_s, t_s, p)
    return (y_prompt, y_sample, gla_prompt, gla_sample, av_sample, pool_prompt, pool_sample,
            s5_re_prompt, s5_im_prompt, s5_re_sample, s5_im_sample, ffn_prompt, ffn_sample)
```

```python
import numpy as np
import concourse.bass as bass
import concourse.mybir as mybir
from concourse.bass_utils import run_bass_kernel_spmd

F32 = mybir.dt.float32
BF16 = mybir.dt.bfloat16
I32 = mybir.dt.int32
AF = mybir.ActivationFunctionType
ALU = mybir.AluOpType


class Res:
    __slots__ = ("w", "readers", "name")

    def __init__(self, name=""):
        self.w = None
        self.readers = {}
        self.name = name


class Op:
    __slots__ = ("eng", "fn", "deps", "signal", "val", "sem", "is_dma", "key")

    def __init__(self, eng, fn):
        self.eng = eng
        self.fn = fn
        self.deps = []
        self.signal = False
        self.val = None
        self.sem = None
        self.is_dma = False
        self.key = eng


class Sched:
    ENG = ("pe", "act", "dve", "pool", "sp")

    def __init__(self, nc):
        self.nc = nc
        self.ops = {e: [] for e in self.ENG}
        self.esem = {e: nc.alloc_semaphore("es_" + e) for e in self.ENG}
        self.dma_sems = {}
        self.dma_tot = {}
        self.final = []

    def dsem(self, name):
        if name not in self.dma_sems:
            self.dma_sems[name] = self.nc.alloc_semaphore("ds_" + name)
            self.dma_tot[name] = 0
        return name

    def op(self, eng, fn, reads=(), writes=(), extra=()):
        o = Op(eng, fn)
        deps = []
        for r in reads:
            if r.w is not None:
                deps.append(r.w)
        for w in writes:
            deps.extend(w.readers.values())
            if w.w is not None:
                deps.append(w.w)
        deps.extend(extra)
        seen = set()
        for d in deps:
            if d is o or id(d) in seen:
                continue
            seen.add(id(d))
            if d.eng == "pe" and eng == "pe" and not d.is_dma:
                continue
            if d.eng == eng and eng == "sp" and not d.is_dma:
                continue
            o.deps.append(d)
            d.signal = True
        for r in reads:
            r.readers[o.key] = o
        for w in writes:
            w.w = o
            w.readers = {}
        self.ops[eng].append(o)
        return o

    def dma(self, eng, semname, out, in_, reads=(), writes=(), extra=(), final=False):
        self.dsem(semname)
        self.dma_tot[semname] += 16
        val = self.dma_tot[semname]
        sem = self.dma_sems[semname]

        def fn(e, out=out, in_=in_, sem=sem):
            return e.dma_start(out=out, in_=in_).then_inc(sem, 16)

        o = Op(eng, fn)
        o.is_dma = True
        o.sem = sem
        o.val = val
        o.key = "dma_" + semname
        deps = []
        for r in reads:
            if r.w is not None:
                deps.append(r.w)
        for w in writes:
            deps.extend(w.readers.values())
            if w.w is not None:
                deps.append(w.w)
        deps.extend(extra)
        seen = set()
        for d in deps:
            if id(d) in seen:
                continue
            seen.add(id(d))
            o.deps.append(d)
            d.signal = True
        for r in reads:
            r.readers[o.key] = o
        for w in writes:
            w.w = o
            w.readers = {}
        self.ops[eng].append(o)
        if final:
            self.final.append(o)
        return o

    def finalize(self):
        for e in self.ENG:
            c = 0
            for o in self.ops[e]:
                if o.is_dma:
                    continue
                if o.signal:
                    c += 1
                    o.val = c
                    o.sem = self.esem[e]

    def emit(self):
        self.finalize()
        nc = self.nc
        sched = self

        def run(e, name):
            waited = {}
            for o in sched.ops[name]:
                for d in o.deps:
                    sid = id(d.sem)
                    if waited.get(sid, 0) >= d.val:
                        continue
                    waited[sid] = d.val
                    e.wait_ge(d.sem, d.val)
                ins = o.fn(e)
                if (not o.is_dma) and o.signal:
                    ins.then_inc(o.sem, 1)
            if name == "sp":
                for o in sched.final:
                    sid = id(o.sem)
                    if waited.get(sid, 0) >= o.val:
                        continue
                    waited[sid] = o.val
                    e.wait_ge(o.sem, o.val)

        with nc.Block() as block:
            @block.sync
            def _(e):
                run(e, "sp")

            @block.tensor
            def _(e):
                run(e, "pe")

            @block.scalar
            def _(e):
                run(e, "act")

            @block.vector
            def _(e):
                run(e, "dve")

            @block.gpsimd
            def _(e):
                run(e, "pool")


import math

D = 2048
NPT = 4096
NST = 128
TT = NPT + NST
DFF = 5632
NFC = 44
EPS = 1e-6
PI = math.pi
NSLOT = 2
SLOT_E = 8192
LB = 64

C_NORM = 0
C_OG = 128
C_CS = 136
C_SD = 144
C_CW = 152
C_CB = C_CW + 264
NCOL = C_CB + 88


class Ctx:
    pass


def build_program(debug=False):
    nc = bass.Bass("TRN2", target_bir_lowering=False)
    K = Sched(nc)
    X = Ctx()
    X.nc, X.K = nc, K

    def din(name, shape, dt=F32):
        return nc.dram_tensor(name, list(shape), dt, kind="ExternalInput").ap()

    def dout(name, shape, dt=F32):
        return nc.dram_tensor(name, list(shape), dt, kind="ExternalOutput").ap()

    I = {}
    I["xT"] = din("xT", [D, TT])
    I["w_in_even"] = din("w_in_even", [D, 5136])
    I["w_out_even"] = din("w_out_even", [D, D])
    I["w_in_odd"] = din("w_in_odd", [D, D])
    I["w_glu"] = din("w_glu", [1024, 2048])
    I["w_out_odd"] = din("w_out_odd", [D, D])
    I["w_up"] = din("w_up", [2, D, 2 * DFF])
    I["w_down"] = din("w_down", [2, DFF, D])
    I["c_map"] = din("c_map", [1024, 256])
    I["cols"] = din("cols", [128, NCOL])
    I["wsT_p"] = din("wsT_p", [128, 8, 128])
    I["wsT_s"] = din("wsT_s", [128, 8, 128])
    I["bs_p"] = din("bs_p", [1024])
    I["bs_s"] = din("bs_s", [1024])
    I["v_gain"] = din("v_gain", [1024])
    I["gbias"] = din("gbias", [512])
    I["w_gate"] = din("w_gate", [16, 512])
    I["cmask"] = din("cmask", [128, 2, 136])
    I["gla_s"] = din("gla_s", [4, 128, 1024])
    I["pool_s"] = din("pool_s", [128, 8, 4, 15])
    I["s5re_s"] = din("s5re_s", [128, 4, 32])
    I["s5im_s"] = din("s5im_s", [128, 4, 32])
    I["ffn_s"] = din("ffn_s", [128, 2, NFC, 4, 2])
    I["s5col"] = din("s5col", [128, 3, 32])
    I["s5row"] = din("s5row", [3, 4096])
    I["Bt"] = din("Bt", [2, 128, 8, 64])
    I["Ct"] = din("Ct", [2, 128, 32, 16])
    I["MB"] = din("MB", [128, 4, 2])
    I["MC"] = din("MC", [128, 2, 4, 8])
    I["invcnt"] = din("invcnt", [4 * TT])

    O = {}
    O["yT"] = dout("yT", [D, TT])
    O["gla_p"] = dout("gla_p", [128, 1024])
    O["gla_o"] = dout("gla_o", [4, 128, 1024])
    O["av_o"] = dout("av_o", [128, 1024])
    O["pool_p"] = dout("pool_p", [128, 8, 15])
    O["pool_o"] = dout("pool_o", [128, 8, 4, 15])
    O["s5_p"] = dout("s5_p", [128, 2, 32])
    O["s5_o"] = dout("s5_o", [128, 2, 4, 32])
    O["ffn_p"] = dout("ffn_p", [128, 2, NFC, 2])
    O["ffn_o"] = dout("ffn_o", [128, 2, NFC, 4, 2])
    X.I, X.O = I, O

    dk = dict(kind="ExternalOutput") if debug else {}
    X.ws5 = nc.dram_tensor("ws5", [4, 128, 8 * 4 * 128], BF16, **dk).ap()
    X.ptab = nc.dram_tensor("ptab", [128, 3 * 32 * LB], F32, **dk).ap()

    sb = lambda name, shape, dt: nc.alloc_sbuf_tensor('s_' + name, shape, dt)
    X.wslot = [sb(f"wslot{i}", [128, SLOT_E], BF16) for i in range(NSLOT)]
    X.wres = [Res(f"wslot{i}") for i in range(NSLOT)]
    X.wcnt = 0
    X.xT = sb("xTs", [128, 16, 512], F32)
    X.rx = Res("x")
    X.cols = sb("cols", [128, NCOL], F32)
    X.rcols = Res("cols")
    X.ones = sb("ones", [128, 128], BF16)
    X.rones = Res("ones")
    X.wmT = [sb("wmT_p", [128, 8, 128], BF16), sb("wmT_s", [128, 8, 128], BF16)]
    X.rwmT = Res("wmT")
    X.wgate = sb("wgate", [16, 512], BF16)
    X.cmask = sb("cmask", [128, 2, 136], F32)
    X.rconst = Res("const")
    X.S = sb("glaS", [128, 1024], F32)
    X.rS = Res("S")
    X.Sbf = sb("glaSbf", [128, 1024], BF16)
    X.rSbf = Res("Sbf")
    X.phalo = sb("phalo", [128, 8, 15], F32)
    X.rphalo = Res("phalo")
    X.chalo = sb("chalo", [128, 2, NFC, 2], F32)
    X.rchalo = [Res("chalo0"), Res("chalo1")]
    X.s5st = sb("s5st", [128, 2, 32], F32)
    X.rs5st = Res("s5st")
    X.rho = sb("rho", [128, 32], F32)
    X.pend = sb("pend", [128, 3, 32], F32)
    X.pend32 = sb("pend32", [128, 3, 32], F32)
    X.sqb = [sb(f"sqb{i}", [128, 512], BF16) for i in range(2)]
    X.rsqb = [Res("sqb0"), Res("sqb1")]
    X.sqc = 0
    X.nrm = sb("nrm", [128, 2, 512], F32)
    X.rnrm = Res("nrm")
    X.tmpx = [sb(f"tmpx{i}", [128, 512], F32) for i in range(2)]
    X.rtmpx = [Res("tmpx0"), Res("tmpx1")]
    ARENA_F = 27776
    X.arena = sb("arena", [128, ARENA_F], F32)
    X.ARENA_F = ARENA_F
    X.apos = 0
    X.psum = nc.alloc_psum_tensor("ps", [128, 8, 512], F32)
    X.rps = [Res(f"ps{i}") for i in range(8)]
    X.pscnt = {}
    return X


def arena_reset(X, keep=0):
    X.apos = keep


def aalloc(X, shape, dt):
    n = 1
    for s in shape[1:]:
        n *= s
    nf = n if dt == F32 else (n + 1) // 2
    nf = (nf + 7) // 8 * 8
    a0 = X.apos
    X.apos += nf
    assert X.apos <= X.ARENA_F, (X.apos, X.ARENA_F)
    v = X.arena[:, a0:a0 + nf]
    if dt != F32:
        v = v.bitcast(dt)
    v = v[:, 0:n]
    if len(shape) == 3:
        v = v.rearrange("p (a b) -> p a b", b=shape[2])
    elif len(shape) == 4:
        v = v.rearrange("p (a b c) -> p a b c", b=shape[2], c=shape[3])
    return v


def fence(X):
    K = X.K
    last = list(getattr(X, "arena_dmas", []))
    X.arena_dmas = []
    for e in ("pe", "act", "dve", "pool"):
        if K.ops[e]:
            last.append(K.ops[e][-1])
    for e in ("pe", "act", "dve", "pool"):
        K.op(e, (lambda en: (lambda eng: eng.nop()))(e), extra=last)
    X.fence_ops = [K.ops[e][-1] for e in ("pe", "act", "dve", "pool")]


def bank(X, group, banks):
    c = X.pscnt.get(group, 0)
    X.pscnt[group] = c + 1
    return banks[c % len(banks)]


def wslab(X, src_ap, kc, cols):
    K = X.K
    i = X.wcnt % NSLOT
    X.wcnt += 1
    view = X.wslot[i][:, 0:kc * cols].rearrange("p (k c) -> p k c", c=cols)
    K.dma("pool", f"w{i}", view, src_ap, writes=[X.wres[i]])
    return view, X.wres[i]


def col(X, idx):
    return X.cols[:, idx:idx + 1]


def norm_finish(X, n):
    K = X.K
    ps7 = X.psum[:, 7, 0:n]
    K.op("act", lambda e: e.activation(out=X.nrm[:, 1, 0:n], in_=ps7, func=AF.Sqrt, scale=1.0 / D, bias=X.epsc[:, 0:1]),
         reads=[X.rps[7], X.rconst], writes=[X.rnrm])
    K.op("dve", lambda e: e.reciprocal(out=X.nrm[:, 0, 0:n], in_=X.nrm[:, 1, 0:n]), reads=[X.rnrm], writes=[X.rnrm])


def sq_accum(X, src, rsrc, n, c, nchunks=16):
    K = X.K
    i = X.sqc % 2
    X.sqc += 1
    sq = X.sqb[i]
    K.op("act", lambda e: e.activation(out=sq[:, 0:n], in_=src, func=AF.Square), reads=[rsrc], writes=[X.rsqb[i]])
    K.op("pe", lambda e: e.matmul(X.psum[:, 7, 0:n], lhsT=X.ones[:, :], rhs=sq[:, 0:n], start=(c == 0), stop=(c == nchunks - 1)),
         reads=[X.rsqb[i], X.rones], writes=[X.rps[7]])


def pre_norm(X, layer, kind, n, hT, rh):
    K = X.K
    for c in range(16):
        sq_accum(X, X.xT[:, c, 0:n], X.rx, n, c)
    norm_finish(X, n)
    for c in range(16):
        g = col(X, C_NORM + (kind * 2 + layer) * 16 + c)
        K.op("dve", lambda e, c=c, g=g: e.scalar_tensor_tensor(out=hT[:, c, 0:n], in0=X.xT[:, c, 0:n], scalar=g, in1=X.nrm[:, 0, 0:n], op0=ALU.mult, op1=ALU.mult),
             reads=[X.rx, X.rnrm, X.rcols], writes=[rh])


def post_residual(X, layer, kind, n, fT, rf):
    K = X.K
    norm_finish(X, n)
    for c in range(16):
        g = col(X, C_NORM + (kind * 2 + layer) * 16 + c)
        i = c % 2
        t = X.tmpx[i]
        K.op("dve", lambda e, c=c, g=g, t=t: e.scalar_tensor_tensor(out=t[:, 0:n], in0=fT[:, c, 0:n], scalar=g, in1=X.nrm[:, 0, 0:n], op0=ALU.mult, op1=ALU.mult),
             reads=[rf, X.rnrm, X.rcols], writes=[X.rtmpx[i]])
        K.op("pool", lambda e, c=c, t=t: e.tensor_tensor(out=X.xT[:, c, 0:n], in0=X.xT[:, c, 0:n], in1=t[:, 0:n], op=ALU.add),
             reads=[X.rtmpx[i], X.rx], writes=[X.rx])


def mm_fm(X, b, slab, rslab, j, src, rsrc, n, kc, extra_reads=()):
    K = X.K
    for k in range(kc):
        K.op("pe", lambda e, k=k: e.matmul(X.psum[:, b, 0:n], lhsT=slab[:, k, j * 128:(j + 1) * 128], rhs=src[:, k, 0:n], start=(k == 0), stop=(k == kc - 1)),
             reads=[rslab, rsrc], writes=[X.rps[b]])


def out_proj(X, wsrc, n, catT, rcat, fT, rf, kc=16):
    K = X.K
    wv = wsrc.rearrange("(k p) c -> p k c", p=128)
    cols_per = 512 if kc == 16 else 128
    nsl = 2048 // cols_per
    c = 0
    for s in range(nsl):
        slab, rs = wslab(X, wv[:, :, s * cols_per:(s + 1) * cols_per], kc, cols_per)
        for j in range(cols_per // 128):
            b = bank(X, "main", [0, 1, 2, 3, 4, 5])
            mm_fm(X, b, slab, rs, j, catT, rcat, n, kc)
            K.op("act", lambda e, b=b, c=c: e.copy(out=fT[:, c, 0:n], in_=X.psum[:, b, 0:n]), reads=[X.rps[b]], writes=[rf])
            sq_accum(X, X.psum[:, b, 0:n], X.rps[b], n, c)
            c += 1


def setup(X):
    K, nc, I = X.K, X.nc, X.I
    arena_reset(X)
    X.epsc = nc.alloc_sbuf_tensor("s_epsc", [128, 4], F32)
    K.op("dve", lambda e: e.memset(X.epsc[:, 0:1], EPS), writes=[X.rconst])
    K.op("dve", lambda e: e.memset(X.epsc[:, 1:2], 1.0), writes=[X.rconst])
    K.op("dve", lambda e: e.memset(X.epsc[:, 2:3], 0.0), writes=[X.rconst])
    K.op("dve", lambda e: e.memset(X.ones[:, :], 1.0), writes=[X.rones])
    K.dma("sp", "cols", X.cols[:, :], I["cols"], writes=[X.rcols])
    K.dma("sp", "cmask", X.cmask[:, :, :], I["cmask"], writes=[X.rconst])
    t0 = aalloc(X, [128, 8, 128], F32)
    t1 = aalloc(X, [128, 8, 128], F32)
    tg = aalloc(X, [128, 512], F32)
    r0, r1, rg = Res(), Res(), Res()
    K.dma("sp", "su0", t0, I["wsT_p"], writes=[r0])
    K.dma("sp", "su1", t1, I["wsT_s"], writes=[r1])
    K.dma("sp", "su2", tg[0:16, :], I["w_gate"], writes=[rg])
    K.op("dve", lambda e: e.memset(t0[64:128, :, 0:64], 0.0), reads=[r0], writes=[r0])
    K.op("dve", lambda e: e.tensor_copy(out=X.wmT[0][:, :, :], in_=t0), reads=[r0], writes=[X.rwmT])
    K.op("dve", lambda e: e.tensor_copy(out=X.wmT[1][:, :, :], in_=t1), reads=[r1], writes=[X.rwmT])
    K.op("dve", lambda e: e.tensor_copy(out=X.wgate[:, :], in_=tg[0:16, :]), reads=[rg], writes=[X.rconst])
    K.op("dve", lambda e: e.memset(X.S[:, :], 0.0), writes=[X.rS])
    K.op("dve", lambda e: e.memset(X.phalo[:, :, :], 0.0), writes=[X.rphalo])
    K.op("dve", lambda e: e.memset(X.chalo[:, 0, :, :], 0.0), writes=[X.rchalo[0]])
    K.op("dve", lambda e: e.memset(X.chalo[:, 1, :, :], 0.0), writes=[X.rchalo[1]])
    K.op("dve", lambda e: e.memset(X.s5st[:, :, :], 0.0), writes=[X.rs5st])
    s5_setup(X)


def s5_params(X, are, aim, ldt, shape, need_f, R=None):
    K = X.K
    R = R if R is not None else Res()
    mk = lambda: aalloc(X, shape, F32)
    dt, rho, th, sn, cs, y, kf, m = mk(), mk(), mk(), mk(), mk(), mk(), mk(), mk()
    ki = aalloc(X, shape, F32).bitcast(I32)
    rd = [X.rconst]
    K.op("act", lambda e: e.activation(out=dt, in_=ldt, func=AF.Exp), reads=[R] + rd, writes=[R])
    K.op("dve", lambda e: e.tensor_tensor(out=rho, in0=are, in1=dt, op=ALU.mult), reads=[R], writes=[R])
    K.op("act", lambda e: e.activation(out=rho, in_=rho, func=AF.Exp), reads=[R], writes=[R])
    K.op("dve", lambda e: e.tensor_tensor(out=th, in0=aim, in1=dt, op=ALU.mult), reads=[R], writes=[R])

    def sin_of(dst, shift):
        K.op("dve", lambda e: e.tensor_scalar(out=y, in0=th, scalar1=shift, scalar2=None, op0=ALU.add), reads=[R], writes=[R])
        K.op("dve", lambda e: e.tensor_scalar(out=ki, in0=y, scalar1=1.0 / (2 * PI), scalar2=None, op0=ALU.mult), reads=[R], writes=[R])
        K.op("dve", lambda e: e.tensor_copy(out=kf, in_=ki), reads=[R], writes=[R])
        K.op("dve", lambda e: e.scalar_tensor_tensor(out=y, in0=kf, scalar=-2 * PI, in1=y, op0=ALU.mult, op1=ALU.add), reads=[R], writes=[R])
        K.op("dve", lambda e: e.tensor_scalar(out=m, in0=y, scalar1=PI, scalar2=None, op0=ALU.is_gt), reads=[R], writes=[R])
        K.op("dve", lambda e: e.scalar_tensor_tensor(out=y, in0=m, scalar=-2 * PI, in1=y, op0=ALU.mult, op1=ALU.add), reads=[R], writes=[R])
        K.op("dve", lambda e: e.tensor_scalar(out=m, in0=y, scalar1=-PI, scalar2=None, op0=ALU.is_lt), reads=[R], writes=[R])
        K.op("dve", lambda e: e.scalar_tensor_tensor(out=y, in0=m, scalar=2 * PI, in1=y, op0=ALU.mult, op1=ALU.add), reads=[R], writes=[R])
        K.op("dve", lambda e: e.tensor_scalar(out=y, in0=y, scalar1=PI, scalar2=-PI, op0=ALU.min, op1=ALU.max), reads=[R], writes=[R])
        K.op("act", lambda e: e.activation(out=dst, in_=y, func=AF.Sin), reads=[R], writes=[R])

    sin_of(sn, 0.0)
    sin_of(cs, PI / 2)
    out = dict(rho=rho, sn=sn, cs=cs, R=R)
    if need_f:
        lbr, lbi, fr, fi = mk(), mk(), mk(), mk()
        tt = lambda o, a, b, op: K.op("dve", lambda e: e.tensor_tensor(out=o, in0=a, in1=b, op=op), reads=[R], writes=[R])
        tt(lbr, rho, cs, ALU.mult)
        K.op("dve", lambda e: e.tensor_scalar(out=lbr, in0=lbr, scalar1=-1.0, scalar2=None, op0=ALU.add), reads=[R], writes=[R])
        tt(lbi, rho, sn, ALU.mult)
        tt(fr, lbr, are, ALU.mult)
        tt(y, lbi, aim, ALU.mult)
        tt(fr, fr, y, ALU.add)
        tt(fi, lbi, are, ALU.mult)
        tt(y, lbr, aim, ALU.mult)
        tt(fi, fi, y, ALU.subtract)
        tt(kf, are, are, ALU.mult)
        tt(m, aim, aim, ALU.mult)
        tt(kf, kf, m, ALU.add)
        K.op("dve", lambda e: e.reciprocal(out=kf, in_=kf), reads=[R], writes=[R])
        tt(fr, fr, kf, ALU.mult)
        tt(fi, fi, kf, ALU.mult)
        out.update(fr=fr, fi=fi)
    return out


def s5_setup(X):
    K, nc, I = X.K, X.nc, X.I
    arena_reset(X, 4096)
    pc = aalloc(X, [128, 3, 32], F32)
    rpc = Res()
    K.dma("sp", "su3", pc, I["s5col"], writes=[rpc])
    P = s5_params(X, pc[:, 0, :], pc[:, 1, :], pc[:, 2, :], [128, 32], False, rpc)
    K.op("dve", lambda e: e.tensor_copy(out=X.rho[:, :], in_=P["rho"]), reads=[P["R"], rpc], writes=[X.rconst])
    tab = aalloc(X, [128, 3, 32, LB], F32)
    rt = P["R"]
    K.op("dve", lambda e: e.tensor_copy(out=tab[:, 0, :, 0], in_=P["cs"]), reads=[rt], writes=[rt])
    K.op("dve", lambda e: e.tensor_copy(out=tab[:, 1, :, 0], in_=P["sn"]), reads=[rt], writes=[rt])
    ta = aalloc(X, [128, 32, 32], F32)
    tb = aalloc(X, [128, 32, 32], F32)
    j = 1
    while j < LB:
        cr = tab[:, 0, :, j - 1:j].to_broadcast([128, 32, j])
        ci = tab[:, 1, :, j - 1:j].to_broadcast([128, 32, j])
        pr, pi = tab[:, 0, :, 0:j], tab[:, 1, :, 0:j]
        tt = lambda o, a, b, op: K.op("dve", lambda e: e.tensor_tensor(out=o, in0=a, in1=b, op=op), reads=[rt], writes=[rt])
        tt(ta[:, :, 0:j], pr, cr, ALU.mult)
        tt(tb[:, :, 0:j], pi, ci, ALU.mult)
        tt(tab[:, 0, :, j:2 * j], ta[:, :, 0:j], tb[:, :, 0:j], ALU.subtract)
        tt(ta[:, :, 0:j], pr, ci, ALU.mult)
        tt(tb[:, :, 0:j], pi, cr, ALU.mult)
        tt(tab[:, 1, :, j:2 * j], ta[:, :, 0:j], tb[:, :, 0:j], ALU.add)
        j *= 2
    K.op("dve", lambda e: e.tensor_scalar(out=tab[:, 2, :, :], in0=tab[:, 1, :, :], scalar1=-1.0, scalar2=None, op0=ALU.mult), reads=[rt], writes=[rt])
    for a in range(3):
        K.op("dve", lambda e, a=a: e.tensor_copy(out=X.pend[:, a, :], in_=tab[:, a, :, LB - 1]), reads=[rt], writes=[X.rconst])
        K.op("dve", lambda e, a=a: e.tensor_copy(out=X.pend32[:, a, :], in_=tab[:, a, :, 31]), reads=[rt], writes=[X.rconst])
    X.ptab_w = K.dma("sp", "ptab", X.ptab, tab.rearrange("p a c l -> p (a c l)"), reads=[rt])
    X.arena_dmas = [X.ptab_w]
    fence(X)
    arena_reset(X, 4096)
    Bt = aalloc(X, [128, 2, 8, 64], F32)
    Ct = aalloc(X, [128, 2, 32, 16], F32)
    MB = aalloc(X, [128, 4, 2], F32)
    MC = aalloc(X, [128, 2, 4, 8], F32)
    rB = Res()
    K.dma("sp", "su4", Bt, I["Bt"].rearrange("a p f q -> p a f q"), writes=[rB], extra=X.fence_ops)
    K.dma("sp", "su5", Ct, I["Ct"].rearrange("a p c h -> p a c h"), writes=[rB], extra=X.fence_ops)
    K.dma("sp", "su6", MB, I["MB"], writes=[rB], extra=X.fence_ops)
    K.dma("sp", "su7", MC, I["MC"], writes=[rB], extra=X.fence_ops)
    base = X.apos
    X.ws5_w = []
    for slab_i in range(4):
        if slab_i > 0:
            X.arena_dmas.append(X.ws5_w[-1])
            fence(X)
        arena_reset(X, base)
        row = aalloc(X, [128, 3, 1024], F32)
        rrow = Res()
        for a in range(3):
            K.dma("sp", "su8", row[:, a, :], I["s5row"][a, slab_i * 1024:(slab_i + 1) * 1024].partition_broadcast(128), writes=[rrow],
                  extra=X.fence_ops)
        R0 = Res()
        K.op("dve", lambda e, row=row: e.tensor_copy(out=row[:, 2, 0:1], in_=row[:, 2, 0:1]), reads=[rrow, rB], writes=[R0])
        Pm = s5_params(X, row[:, 0, :], row[:, 1, :], row[:, 2, :], [128, 1024], True, R0)
        R = Pm["R"]
        fr = Pm["fr"].rearrange("p (c g q) -> p c g q", g=2, q=64)
        fi = Pm["fi"].rearrange("p (c g q) -> p c g q", g=2, q=64)
        stg = aalloc(X, [128, 8, 4, 128], BF16)
        t1 = aalloc(X, [128, 4, 2, 64], F32)
        t2 = aalloc(X, [128, 4, 2, 64], F32)
        rs = Res()
        for half in range(2):
            ccl0 = half * 4
            cc0 = slab_i * 8 + ccl0
            fc = cc0 // 4
            fre = fr[:, ccl0:ccl0 + 4, :, :]
            fim = fi[:, ccl0:ccl0 + 4, :, :]
            bre = Bt[:, 0, fc, :].unsqueeze(1).unsqueeze(1).to_broadcast([128, 4, 2, 64])
            bim = Bt[:, 1, fc, :].unsqueeze(1).unsqueeze(1).to_broadcast([128, 4, 2, 64])
            mb = MB[:, :, :].unsqueeze(3).to_broadcast([128, 4, 2, 64])
            tt = lambda o, a, b, op, R=R, rs=rs: K.op("dve", lambda e: e.tensor_tensor(out=o, in0=a, in1=b, op=op), reads=[R, rB, rs], writes=[rs])
            dst = lambda arr, half=half, stg=stg: stg[:, half * 4:half * 4 + 4, arr, :].rearrange("p c (g q) -> p c g q", g=2)
            tt(t1, fre, bre, ALU.mult)
            tt(t2, fim, bim, ALU.mult)
            tt(t1, t1, t2, ALU.subtract)
            tt(dst(0), t1, mb, ALU.mult)
            tt(t1, fre, bim, ALU.mult)
            tt(t2, fim, bre, ALU.mult)
            tt(t1, t1, t2, ALU.add)
            tt(dst(1), t1, mb, ALU.mult)
            for arr, (a, sgn) in enumerate([(0, 0), (1, 1)]):
                cin = Ct[:, a, cc0:cc0 + 4, :].unsqueeze(2).to_broadcast([128, 4, 8, 16])
                mc = MC[:, sgn, :, :].unsqueeze(3).to_broadcast([128, 4, 8, 16])
                d = stg[:, half * 4:half * 4 + 4, 2 + arr, :].rearrange("p c (g h) -> p c g h", g=8)
                tt(d, cin, mc, ALU.mult)
        X.ws5_w.append(K.dma("sp", "ws5", X.ws5[slab_i], stg.rearrange("p c a m -> p (c a m)"), reads=[rs]))
    X.arena_dmas.append(X.ws5_w[-1])
    fence(X)


def even_mixer(X, gi, t0, n, typ, last_prompt):
    K, nc, I, O = X.K, X.nc, X.I, X.O
    nt = n // 128
    nch = 2 if typ == 0 else 4
    Lc = 128 // nch
    fence(X)
    arena_reset(X)
    catT = aalloc(X, [128, 16, n], BF16); rcat = Res("cat")
    X.cat_keep = X.apos
    hT = aalloc(X, [128, 16, n], BF16); rh = Res("h")
    auT = aalloc(X, [128, 8, n], BF16); rau = Res()
    avbf = aalloc(X, [128, nt, 1024], BF16); ravbf = Res()
    avg = aalloc(X, [128, 1024], F32); ravg = Res()
    qT = aalloc(X, [128, 4, n], BF16); rq = Res()
    k32 = aalloc(X, [128, nt, 512], F32); rk = Res()
    vbf = aalloc(X, [128, nt, 1024], BF16); rv = Res()
    sr = aalloc(X, [128, 8, n], BF16); rsr = Res()
    glr = aalloc(X, [128, n], BF16); rglr = Res()
    sp = X.tmpx[0][:, :]; rsp = X.rtmpx[0]
    exr = X.tmpx[1][:, :]; rexr = X.rtmpx[1]
    kd = [aalloc(X, [128, 512], BF16) for _ in range(2)]; rkd = [Res(), Res()]
    osq = aalloc(X, [128, 8, 128], BF16); rosq = Res()
    otmp = aalloc(X, [128, 8, 128], F32); rot = Res()
    bst = aalloc(X, [128, 8, 128], F32)
    vgt = aalloc(X, [128, 1024], F32)
    gbt = aalloc(X, [128, 512], F32)
    st = aalloc(X, [128, 32], F32); rst = Res()
    dec = aalloc(X, [128, 4, 4], F32); rdec = Res()
    rnrm2 = X.rnrm
    nrm2 = X.nrm
    rbt = Res()
    fo = X.fence_ops
    K.dma("sp", "e0", bst.rearrange("p h i -> p (h i)"), I["bs_p" if typ == 0 else "bs_s"].partition_broadcast(128), writes=[rbt], extra=fo)
    K.dma("sp", "e1", vgt, I["v_gain"].partition_broadcast(128), writes=[rbt], extra=fo)
    K.dma("sp", "e2", gbt, I["gbias"].partition_broadcast(128), writes=[rbt], extra=fo)

    pre_norm(X, 0, 0, n, hT, rh)
    W = I["w_in_even"].rearrange("(k p) c -> p k c", p=128)
    MB_ = [0, 1, 2, 3, 4, 5]

    def fm_slab(c0, ncols, evac):
        slab, rs = wslab(X, W[:, :, c0:c0 + ncols], 16, ncols)
        for j in range(max(1, ncols // 128)):
            b = bank(X, "main", MB_)
            if ncols >= 128:
                mm_fm(X, b, slab, rs, j, hT, rh, n, 16)
            else:
                for k in range(16):
                    K.op("pe", lambda e, k=k, b=b: e.matmul(X.psum[0:ncols, b, 0:n], lhsT=slab[:, k, 0:ncols], rhs=hT[:, k, 0:n], start=(k == 0), stop=(k == 15)),
                         reads=[rs, rh], writes=[X.rps[b]])
            evac(b, c0 // 128 + j if ncols >= 128 else 0)

    def tm_group(b, slab, rs, tt):
        for k in range(16):
            K.op("pe", lambda e, k=k: e.matmul(X.psum[:, b, 0:512], lhsT=hT[:, k, tt * 128:(tt + 1) * 128], rhs=slab[:, k, 0:512], start=(k == 0), stop=(k == 15)),
                 reads=[rs, rh], writes=[X.rps[b]])

    for s in range(2):
        fm_slab(s * 512, 512, lambda b, c: K.op("act", lambda e: e.activation(out=auT[:, c, 0:n], in_=X.psum[:, b, 0:n], func=AF.Gelu_apprx_tanh), reads=[X.rps[b]], writes=[rau]))
    sl0, rs0 = wslab(X, W[:, :, 1024:1536], 16, 512)
    sl1, rs1 = wslab(X, W[:, :, 1536:2048], 16, 512)
    for tt in range(nt):
        for hh, (sl, rs) in enumerate([(sl0, rs0), (sl1, rs1)]):
            b = bank(X, "main", MB_)
            tm_group(b, sl, rs, tt)
            K.op("act", lambda e, b=b, hh=hh: e.activation(out=avg[:, hh * 512:(hh + 1) * 512], in_=X.psum[:, b, 0:512], func=AF.Gelu_apprx_tanh), reads=[X.rps[b]], writes=[ravg])
        for hh in range(2):
            K.op("dve", lambda e, hh=hh: e.bn_stats(out=st[:, hh * 6:(hh + 1) * 6], in_=avg[:, hh * 512:(hh + 1) * 512]), reads=[ravg], writes=[rst])
        K.op("dve", lambda e: e.bn_aggr(out=st[:, 12:14], in_=st[:, 0:12]), reads=[rst], writes=[rst])
        K.op("act", lambda e: e.activation(out=st[:, 14:15], in_=st[:, 13:14], func=AF.Sqrt, bias=X.epsc[:, 0:1]), reads=[rst, X.rconst], writes=[rst])
        K.op("dve", lambda e: e.reciprocal(out=st[:, 15:16], in_=st[:, 14:15]), reads=[rst], writes=[rst])
        K.op("dve", lambda e: e.tensor_scalar(out=avg[:, :], in0=avg[:, :], scalar1=st[:, 12:13], scalar2=st[:, 15:16], op0=ALU.subtract, op1=ALU.mult), reads=[rst, ravg], writes=[ravg])
        K.op("dve", lambda e: e.tensor_tensor(out=avg[:, :], in0=avg[:, :], in1=vgt, op=ALU.mult), reads=[ravg, rbt], writes=[ravg])
        K.op("act", lambda e, tt=tt: e.copy(out=avbf[:, tt, :], in_=avg[:, :]), reads=[ravg], writes=[ravbf])
        if typ == 1:
            X.arena_dmas.append(K.dma("sp", "avo", O["av_o"], avg[:, :], reads=[ravg], final=True))
    fm_slab(2048, 512, lambda b, c: K.op("act", lambda e: e.activation(out=qT[:, c - 16, 0:n], in_=X.psum[:, b, 0:n], func=AF.Copy, scale=128.0 ** -0.5), reads=[X.rps[b]], writes=[rq]))
    sl, rs = wslab(X, W[:, :, 2560:3072], 16, 512)
    for tt in range(nt):
        b = bank(X, "main", MB_)
        tm_group(b, sl, rs, tt)
        K.op("act", lambda e, b=b, tt=tt: e.copy(out=k32[:, tt, :], in_=X.psum[:, b, 0:512]), reads=[X.rps[b]], writes=[rk])
    for hh in range(2):
        sl, rs = wslab(X, W[:, :, 3072 + hh * 512:3584 + hh * 512], 16, 512)
        for tt in range(nt):
            b = bank(X, "main", MB_)
            tm_group(b, sl, rs, tt)
            K.op("act", lambda e, b=b, tt=tt, hh=hh: e.copy(out=vbf[:, tt, hh * 512:(hh + 1) * 512], in_=X.psum[:, b, 0:512]), reads=[X.rps[b]], writes=[rv])
    for s in range(2):
        fm_slab(4096 + s * 512, 512, lambda b, c: K.op("act", lambda e: e.activation(out=sr[:, c - 32, 0:n], in_=X.psum[:, b, 0:n], func=AF.Silu), reads=[X.rps[b]], writes=[rsr]))
    fm_slab(5120, 16, lambda b, c: K.op("act", lambda e: e.copy(out=glr[0:16, 0:n], in_=X.psum[0:16, b, 0:n]), reads=[X.rps[b]], writes=[rglr]))

    for tt in range(nt):
        for hp in range(2):
            b = bank(X, "main", MB_)
            for h4 in range(4):
                h = hp * 4 + h4
                K.op("pe", lambda e, b=b, h=h, h4=h4, tt=tt: e.matmul(X.psum[:, b, h4 * 128:(h4 + 1) * 128], lhsT=avbf[:, tt, h * 128:(h + 1) * 128], rhs=X.wmT[typ][:, h, :], start=True, stop=True),
                     reads=[ravbf, X.rwmT], writes=[X.rps[b]])
            K.op("dve", lambda e, b=b, hp=hp: e.tensor_tensor(out=otmp[:, 0:4, :], in0=X.psum[:, b, :].rearrange("p (a i) -> p a i", i=128), in1=bst[:, hp * 4:hp * 4 + 4, :], op=ALU.add),
                 reads=[X.rps[b], rbt], writes=[rot])
            K.op("dve", lambda e, hp=hp, tt=tt: e.tensor_tensor(out=catT[:, hp * 4:hp * 4 + 4, tt * 128:(tt + 1) * 128], in0=otmp[:, 0:4, :], in1=auT[:, hp * 4:hp * 4 + 4, tt * 128:(tt + 1) * 128], op=ALU.mult),
                 reads=[rot, rau], writes=[rcat])

    MBg = [0, 1, 2, 3]
    mrev = X.cmask[:, typ, 0:128]
    cind = X.cmask[:, typ, 128:132]
    cmk = X.cmask[:, typ, 132:136]
    for tt in range(nt):
        tsl = slice(tt * 128, (tt + 1) * 128)
        bz = bank(X, "gla", MBg)
        K.op("pe", lambda e, bz=bz, tsl=tsl: e.matmul(X.psum[:, bz, 0:512], lhsT=glr[0:16, tsl], rhs=X.wgate[0:16, :], start=True, stop=True),
             reads=[rglr, X.rconst], writes=[X.rps[bz]])
        K.op("dve", lambda e, bz=bz: e.tensor_tensor(out=sp, in0=X.psum[:, bz, 0:512], in1=gbt, op=ALU.add), reads=[X.rps[bz], rbt], writes=[rsp])
        K.op("act", lambda e: e.activation(out=sp, in_=sp, func=AF.Exp, scale=-1.0), reads=[rsp], writes=[rsp])
        K.op("act", lambda e: e.activation(out=sp, in_=sp, func=AF.Ln, bias=X.epsc[:, 1:2]), reads=[rsp, X.rconst], writes=[rsp])
        br = bank(X, "gla", MBg)
        K.op("pe", lambda e, br=br: e.matmul(X.psum[:, br, 0:512], lhsT=mrev, rhs=sp, start=True, stop=True), reads=[rsp, X.rconst], writes=[X.rps[br]])
        K.op("act", lambda e, br=br: e.activation(out=exr, in_=X.psum[:, br, 0:512], func=AF.Exp), reads=[X.rps[br]], writes=[rexr])
        bt_ = bank(X, "gla", MBg)
        for h in range(4):
            K.op("pe", lambda e, bt_=bt_, h=h: e.matmul(X.psum[:, bt_, h * 4:h * 4 + 4], lhsT=sp[:, h * 128:(h + 1) * 128], rhs=cind, start=True, stop=True),
                 reads=[rsp, X.rconst], writes=[X.rps[bt_]])
        K.op("act", lambda e, bt_=bt_: e.activation(out=dec.rearrange("p a b -> p (a b)"), in_=X.psum[:, bt_, 0:16], func=AF.Exp), reads=[X.rps[bt_]], writes=[rdec])
        bo = [4, 5]
        for c in range(nch):
            i = c % 2
            K.op("dve", lambda e, c=c, i=i, tt=tt: e.scalar_tensor_tensor(out=kd[i], in0=exr, scalar=cmk[:, c:c + 1], in1=k32[:, tt, :], op0=ALU.mult, op1=ALU.mult),
                 reads=[rexr, rk, X.rconst], writes=[rkd[i]])
            if typ == 1:
                seq = tt * 4 + c
                K.dma("sp", "gls", X.S[:, :], I["gla_s"][seq], writes=[X.rS])
            bs_ = [bank(X, "gla", MBg), bank(X, "gla", MBg)]
            for h in range(4):
                K.op("pe", lambda e, h=h, i=i, tt=tt, bs_=bs_: e.matmul(X.psum[:, bs_[h // 2], (h % 2) * 256:(h % 2 + 1) * 256], lhsT=kd[i][:, h * 128:(h + 1) * 128], rhs=vbf[:, tt, h * 256:(h + 1) * 256], start=True, stop=True),
                     reads=[rkd[i], rv], writes=[X.rps[bs_[h // 2]]])
            for h in range(4):
                K.op("dve", lambda e, h=h, c=c, bs_=bs_: e.scalar_tensor_tensor(out=X.S[:, h * 256:(h + 1) * 256], in0=X.S[:, h * 256:(h + 1) * 256], scalar=dec[:, h, c:c + 1], in1=X.psum[:, bs_[h // 2], (h % 2) * 256:(h % 2 + 1) * 256], op0=ALU.mult, op1=ALU.add),
                     reads=[X.rS, rdec, X.rps[bs_[h // 2]]], writes=[X.rS])
            K.op("act", lambda e: e.copy(out=X.Sbf[:, :], in_=X.S[:, :]), reads=[X.rS], writes=[X.rSbf])
            if typ == 1:
                K.dma("sp", "glo", O["gla_o"][seq], X.S[:, :], reads=[X.rS], final=True)
            for e8 in range(8):
                h = e8 // 2
                K.op("pe", lambda e, e8=e8, h=h, c=c, tt=tt: e.matmul(X.psum[:, bo[e8 // 4], (e8 % 4) * 128 + c * Lc:(e8 % 4) * 128 + (c + 1) * Lc], lhsT=X.Sbf[:, e8 * 128:(e8 + 1) * 128], rhs=qT[:, h, tt * 128 + c * Lc:tt * 128 + (c + 1) * Lc], start=True, stop=True),
                     reads=[X.rSbf, rq], writes=[X.rps[bo[e8 // 4]]])
        for a in range(2):
            K.op("act", lambda e, a=a: e.activation(out=osq[:, a * 4:(a + 1) * 4, :], in_=X.psum[:, bo[a], :].rearrange("p (a i) -> p a i", i=128), func=AF.Square), reads=[X.rps[bo[a]]], writes=[rosq])
        bn = 6
        for h in range(4):
            for u in range(2):
                K.op("pe", lambda e, h=h, u=u, bn=bn: e.matmul(X.psum[:, bn, h * 128:(h + 1) * 128], lhsT=X.ones[:, :], rhs=osq[:, 2 * h + u, :], start=(u == 0), stop=(u == 1)),
                     reads=[rosq, X.rones], writes=[X.rps[bn]])
        K.op("act", lambda e, bn=bn: e.activation(out=nrm2[:, 1, :], in_=X.psum[:, bn, 0:512], func=AF.Sqrt, scale=1.0 / 256, bias=X.epsc[:, 0:1]), reads=[X.rps[bn], X.rconst], writes=[rnrm2])
        K.op("dve", lambda e: e.reciprocal(out=nrm2[:, 0, :], in_=nrm2[:, 1, :]), reads=[rnrm2], writes=[rnrm2])
        for e8 in range(8):
            h = e8 // 2
            K.op("dve", lambda e, e8=e8, h=h: e.scalar_tensor_tensor(out=otmp[:, e8, :], in0=X.psum[:, bo[e8 // 4], (e8 % 4) * 128:(e8 % 4 + 1) * 128], scalar=col(X, C_OG + e8), in1=nrm2[:, 0, h * 128:(h + 1) * 128], op0=ALU.mult, op1=ALU.mult),
                 reads=[X.rps[bo[e8 // 4]], rnrm2, X.rcols], writes=[rot])
        K.op("dve", lambda e, tsl=tsl: e.tensor_tensor(out=catT[:, 8:16, tsl], in0=otmp[:, :, :], in1=sr[:, :, tsl], op=ALU.mult), reads=[rot, rsr], writes=[rcat])
    if last_prompt:
        K.dma("sp", "glp", O["gla_p"], X.S[:, :], reads=[X.rS], final=True)
    fence(X)
    arena_reset(X, X.cat_keep)
    fT = aalloc(X, [128, 16, n], F32); rf = Res("f")
    out_proj(X, I["w_out_even"], n, catT, rcat, fT, rf)
    post_residual(X, 0, 1, n, fT, rf)


def conv_ffn(X, layer, gi, n, typ, last_prompt):
    K, nc, I, O = X.K, X.nc, X.I, X.O
    nseg = 1 if typ == 0 else 4
    L = n // nseg
    fence(X)
    arena_reset(X)
    actT = aalloc(X, [128, NFC, n], BF16); ract = Res("act")
    keep = X.apos
    hT = aalloc(X, [128, 16, n], BF16); rh = Res("h")
    gb = [aalloc(X, [128, nseg, L + 2], F32) for _ in range(2)]; rgb = [Res(), Res()]
    tc_ = [aalloc(X, [128, nseg, L], F32) for _ in range(2)]; rtc = [Res(), Res()]
    fo = X.fence_ops
    if typ == 1:
        fst = aalloc(X, [128, NFC, 4, 2], F32); rfst = Res()
        K.dma("sp", "f0", fst, I["ffn_s"][:, layer], writes=[rfst], extra=fo)
    pre_norm(X, layer, 2, n, hT, rh)
    W = I["w_up"][layer].rearrange("(k p) c -> p k c", p=128)
    cw = lambda t, f: col(X, C_CW + (layer * 3 + t) * NFC + f)
    cb = lambda f: col(X, C_CB + layer * NFC + f)
    for s in range(11):
        sg, rsg = wslab(X, W[:, :, s * 512:(s + 1) * 512], 16, 512)
        sv, rsv = wslab(X, W[:, :, DFF + s * 512:DFF + (s + 1) * 512], 16, 512)
        for j in range(4):
            f = s * 4 + j
            i = f % 2
            bg = bank(X, "ffg", [0, 2, 4])
            bv = bg + 1
            mm_fm(X, bg, sg, rsg, j, hT, rh, n, 16)
            mm_fm(X, bv, sv, rsv, j, hT, rh, n, 16)
            g3 = gb[i]
            K.op("act", lambda e, bg=bg, g3=g3: e.copy(out=g3[:, :, 2:L + 2], in_=X.psum[:, bg, 0:n].rearrange("p (s l) -> p s l", l=L)), reads=[X.rps[bg]], writes=[rgb[i]])
            if typ == 0:
                K.op("pool", lambda e, g3=g3, f=f: e.tensor_copy(out=g3[:, 0, 0:2], in_=X.chalo[:, layer, f, :]), reads=[X.rchalo[layer]], writes=[rgb[i]])
                K.op("pool", lambda e, g3=g3, f=f: e.tensor_copy(out=X.chalo[:, layer, f, :], in_=g3[:, 0, L:L + 2]), reads=[rgb[i]], writes=[X.rchalo[layer]])
            else:
                K.op("pool", lambda e, g3=g3, f=f: e.tensor_copy(out=g3[:, :, 0:2], in_=fst[:, f, :, :]), reads=[rfst], writes=[rgb[i]])
                K.op("pool", lambda e, g3=g3, f=f: e.tensor_copy(out=X.ffno[:, layer, f, :, :], in_=g3[:, :, L:L + 2]), reads=[rgb[i]], writes=[X.rffno])
            t3 = tc_[i]
            K.op("act", lambda e, g3=g3, t3=t3, f=f: e.activation(out=t3, in_=g3[:, :, 0:L], func=AF.Identity, scale=cw(0, f), bias=cb(f)), reads=[rgb[i], X.rcols], writes=[rtc[i]])
            K.op("dve", lambda e, g3=g3, t3=t3, f=f: e.scalar_tensor_tensor(out=t3, in0=g3[:, :, 1:L + 1], scalar=cw(1, f), in1=t3, op0=ALU.mult, op1=ALU.add), reads=[rgb[i], rtc[i], X.rcols], writes=[rtc[i]])
            K.op("dve", lambda e, g3=g3, t3=t3, f=f: e.scalar_tensor_tensor(out=t3, in0=g3[:, :, 2:L + 2], scalar=cw(2, f), in1=t3, op0=ALU.mult, op1=ALU.add), reads=[rgb[i], rtc[i], X.rcols], writes=[rtc[i]])
            K.op("act", lambda e, t3=t3: e.activation(out=t3, in_=t3, func=AF.Gelu_apprx_tanh), reads=[rtc[i]], writes=[rtc[i]])
            K.op("dve", lambda e, t3=t3, f=f, bv=bv: e.tensor_tensor(out=actT[:, f, 0:n], in0=t3.rearrange("p s l -> p (s l)"), in1=X.psum[:, bv, 0:n], op=ALU.mult), reads=[rtc[i], X.rps[bv]], writes=[ract])
    if typ == 0 and last_prompt:
        K.dma("sp", "ffp", O["ffn_p"][:, layer], X.chalo[:, layer, :, :], reads=[X.rchalo[layer]], final=True)
    if typ == 1:
        K.dma("sp", "ffo", O["ffn_o"][:, layer], X.ffno[:, layer], reads=[X.rffno], final=True)
    fence(X)
    arena_reset(X, keep)
    fT = aalloc(X, [128, 16, n], F32); rf = Res("f")
    out_proj(X, I["w_down"][layer], n, actT, ract, fT, rf, kc=NFC)
    post_residual(X, layer, 3, n, fT, rf)


def odd_mixer(X, gi, t0, n, typ, last_prompt):
    K, nc, I, O = X.K, X.nc, X.I, X.O
    nseg = 1 if typ == 0 else 4
    L = n // nseg
    Lb = LB if typ == 0 else 32
    nblk = n // Lb
    Wd = 15 + L
    fence(X)
    arena_reset(X)
    catT = aalloc(X, [128, 16, n], BF16); rcat = Res("cat")
    keep = X.apos
    hT = aalloc(X, [128, 16, n], BF16); rh = Res("h")
    cbf = [aalloc(X, [128, nseg, Wd], F32) for _ in range(2)]; rcb = [Res(), Res()]
    pt = [aalloc(X, [128, nseg, Wd], F32) for _ in range(2)]; rpt = [Res(), Res()]
    dlt = aalloc(X, [128, 8, n], BF16); rdl = Res()
    uT = aalloc(X, [128, 8, n], BF16); ru = Res()
    tab = aalloc(X, [128, 3, 32, LB], F32); rtab = Res()
    tq = [X.tmpx[0][:, 0:n], X.tmpx[1][:, 0:n]] + [aalloc(X, [128, n], F32) for _ in range(2)]; rtq = [X.rtmpx[0], X.rtmpx[1], Res(), Res()]
    wz = [aalloc(X, [128, n], F32) for _ in range(4)]; rwz = [Res() for _ in range(4)]
    pr = [aalloc(X, [128, n], BF16) for _ in range(4)]; rpr = [Res() for _ in range(4)]
    yg = aalloc(X, [128, 8, n], BF16); ryg = Res()
    sgm = aalloc(X, [128, n], F32); rsg = Res()
    se = aalloc(X, [128, 8], F32); rse = Res()
    fo = X.fence_ops
    K.dma("sp", "o0", tab.rearrange("p a c l -> p (a c l)"), X.ptab, writes=[rtab], extra=fo + [X.ptab_w])
    if typ == 1:
        pst = aalloc(X, [128, 8, 4, 15], F32); rpst = Res()
        K.dma("sp", "o1", pst, I["pool_s"], writes=[rpst], extra=fo)
        s5i = aalloc(X, [128, 2, 4, 32], F32); rs5i = Res()
        K.dma("sp", "o2", s5i[:, 0], I["s5re_s"], writes=[rs5i], extra=fo)
        K.dma("sp", "o2", s5i[:, 1], I["s5im_s"], writes=[rs5i], extra=fo)
        s5o = aalloc(X, [128, 2, 4, 32], F32); rs5o = Res()
        pso = aalloc(X, [128, 8, 4, 15], F32); rpso = Res()
    pre_norm(X, 1, 0, n, hT, rh)
    W = I["w_in_odd"].rearrange("(k p) c -> p k c", p=128)
    MB_ = [0, 1, 2, 3]
    for s in range(2):
        slab, rs = wslab(X, W[:, :, s * 512:(s + 1) * 512], 16, 512)
        for j4 in range(4):
            j = s * 4 + j4
            g = j // 2
            b = bank(X, "odd", MB_)
            mm_fm(X, b, slab, rs, j4, hT, rh, n, 16)
            i = j % 2
            cb3 = cbf[i]
            K.op("act", lambda e, b=b, cb3=cb3: e.copy(out=cb3[:, :, 15:Wd], in_=X.psum[:, b, 0:n].rearrange("p (s l) -> p s l", l=L)), reads=[X.rps[b]], writes=[rcb[i]])
            if typ == 0:
                K.op("pool", lambda e, cb3=cb3, j=j: e.tensor_copy(out=cb3[:, 0, 0:15], in_=X.phalo[:, j, :]), reads=[X.rphalo], writes=[rcb[i]])
                K.op("pool", lambda e, cb3=cb3, j=j: e.tensor_copy(out=X.phalo[:, j, :], in_=cb3[:, 0, L:L + 15]), reads=[rcb[i]], writes=[X.rphalo])
            else:
                K.op("pool", lambda e, cb3=cb3, j=j: e.tensor_copy(out=cb3[:, :, 0:15], in_=pst[:, j, :, :]), reads=[rpst], writes=[rcb[i]])
                K.op("pool", lambda e, cb3=cb3, j=j: e.tensor_copy(out=pso[:, j, :, :], in_=cb3[:, :, L:L + 15]), reads=[rcb[i]], writes=[rpso])
            src, rsrc = cb3, rcb[i]
            lo = 0
            for step in range(g + 1):
                d = 1 << step
                dst, rdst = pt[step % 2], rpt[step % 2]
                K.op("pool", lambda e, src=src, dst=dst, d=d, lo=lo: e.tensor_tensor(out=dst[:, :, lo + d:Wd], in0=src[:, :, lo + d:Wd], in1=src[:, :, lo:Wd - d], op=ALU.add),
                     reads=[rsrc], writes=[rdst])
                src, rsrc = dst, rdst
                lo += d
            w = 2 << g
            if typ == 0 and t0 == 0:
                K.op("dve", lambda e, src=src, g=g: e.tensor_tensor(out=src[:, 0, 15:31], in0=src[:, 0, 15:31], in1=X.cmaskc[:, g, :], op=ALU.mult), reads=[rsrc, X.rconst], writes=[rsrc])
            K.op("dve", lambda e, src=src, cb3=cb3, j=j, w=w: e.scalar_tensor_tensor(out=dlt[:, j, 0:n].rearrange("p (s l) -> p s l", l=L), in0=src[:, :, 15:Wd], scalar=1.0 / w, in1=cb3[:, :, 15:Wd], op0=ALU.mult, op1=ALU.subtract),
                 reads=[rsrc, rcb[i]], writes=[rdl])
    cms, rcm = wslab(X, I["c_map"].rearrange("(j p) d -> p j d", p=128), 8, 256)
    for j in range(8):
        g, hf = j // 2, j % 2
        b = bank(X, "odd", MB_)
        for kk in range(2):
            K.op("pe", lambda e, b=b, g=g, kk=kk, hf=hf: e.matmul(X.psum[:, b, 0:n], lhsT=cms[:, g * 2 + kk, hf * 128:(hf + 1) * 128], rhs=dlt[:, g * 2 + kk, 0:n], start=(kk == 0), stop=(kk == 1)),
                 reads=[rcm, rdl], writes=[X.rps[b]])
        K.op("act", lambda e, b=b, j=j: e.activation(out=catT[:, j, 0:n], in_=X.psum[:, b, 0:n], func=AF.Identity, scale=col(X, C_CS + j)), reads=[X.rps[b], X.rcols], writes=[rcat])
    if typ == 0 and last_prompt:
        K.dma("sp", "plp", O["pool_p"], X.phalo[:, :, :], reads=[X.rphalo], final=True)
    if typ == 1:
        X.arena_dmas.append(K.dma("sp", "plo", O["pool_o"], pso, reads=[rpso], final=True))
    for s in range(2):
        slab, rs = wslab(X, W[:, :, 1024 + s * 512:1024 + (s + 1) * 512], 16, 512)
        for j4 in range(4):
            j = s * 4 + j4
            b = bank(X, "odd", MB_)
            mm_fm(X, b, slab, rs, j4, hT, rh, n, 16)
            K.op("act", lambda e, b=b, j=j: e.copy(out=uT[:, j, 0:n], in_=X.psum[:, b, 0:n]), reads=[X.rps[b]], writes=[ru])
    pend = X.pend if typ == 0 else X.pend32
    v3 = lambda ap: ap.rearrange("p (b l) -> p b l", l=Lb)
    for sl in range(4):
        ws, rws = wslab(X, X.ws5[sl].rearrange("p (k c) -> p k c", c=128), 32, 128)
        if sl == 0:
            pass
        for q8 in range(8):
            cc = sl * 8 + q8
            fc = cc // 4
            by = 4 + (fc % 2)
            bre, bim = (0, 1) if cc % 2 == 0 else (2, 3)
            K.op("pe", lambda e, bre=bre, q8=q8, fc=fc, ws=ws: e.matmul(X.psum[:, bre, 0:n], lhsT=ws[:, q8 * 4 + 0, :], rhs=uT[:, fc, 0:n], start=True, stop=True), reads=[rws, ru], writes=[X.rps[bre]], extra=X.ws5_w)
            K.op("pe", lambda e, bim=bim, q8=q8, fc=fc, ws=ws: e.matmul(X.psum[:, bim, 0:n], lhsT=ws[:, q8 * 4 + 1, :], rhs=uT[:, fc, 0:n], start=True, stop=True), reads=[rws, ru], writes=[X.rps[bim]])
            Pre = tab[:, 0, cc, 0:Lb].unsqueeze(1).to_broadcast([128, nblk, Lb])
            Pim = tab[:, 1, cc, 0:Lb].unsqueeze(1).to_broadcast([128, nblk, Lb])
            PimN = tab[:, 2, cc, 0:Lb].unsqueeze(1).to_broadcast([128, nblk, Lb])
            pre3 = v3(X.psum[:, bre, 0:n]); pim3 = v3(X.psum[:, bim, 0:n])
            K.op("dve", lambda e, pre3=pre3, Pre=Pre: e.tensor_tensor(out=v3(tq[0]), in0=pre3, in1=Pre, op=ALU.mult), reads=[X.rps[bre], rtab], writes=[rtq[0]])
            K.op("dve", lambda e, pim3=pim3, Pim=Pim: e.tensor_tensor(out=v3(tq[1]), in0=pim3, in1=Pim, op=ALU.mult), reads=[X.rps[bim], rtab], writes=[rtq[1]])
            K.op("pool", lambda e: e.tensor_tensor(out=wz[0], in0=tq[0], in1=tq[1], op=ALU.add), reads=[rtq[0], rtq[1]], writes=[rwz[0]])
            K.op("dve", lambda e, pim3=pim3, Pre=Pre: e.tensor_tensor(out=v3(tq[2]), in0=pim3, in1=Pre, op=ALU.mult), reads=[X.rps[bim], rtab], writes=[rtq[2]])
            K.op("dve", lambda e, pre3=pre3, Pim=Pim: e.tensor_tensor(out=v3(tq[3]), in0=pre3, in1=Pim, op=ALU.mult), reads=[X.rps[bre], rtab], writes=[rtq[3]])
            K.op("pool", lambda e: e.tensor_tensor(out=wz[1], in0=tq[2], in1=tq[3], op=ALU.subtract), reads=[rtq[2], rtq[3]], writes=[rwz[1]])
            if X.dbg and gi == 0 and cc == X.dbg_cc:
                for ii in range(4):
                    K.dma("sp", "dbgq", X.dbgo["tq"][ii], tq[ii], reads=[rtq[ii]], final=True)
                for ii in range(2):
                    K.dma("sp", "dbgw", X.dbgo["wz"][ii], wz[ii], reads=[rwz[ii]], final=True)
            rho_b = X.rho[:, cc:cc + 1].to_broadcast([128, Lb])
            for bl in range(nblk):
                c0 = bl * Lb
                if typ == 0:
                    ire, iim = X.s5st[:, 0, cc:cc + 1], X.s5st[:, 1, cc:cc + 1]
                    rinit = X.rs5st
                    ore, oim, rout = ire, iim, X.rs5st
                else:
                    ire, iim = s5i[:, 0, bl, cc:cc + 1], s5i[:, 1, bl, cc:cc + 1]
                    rinit = rs5i
                    ore, oim, rout = s5o[:, 0, bl, cc:cc + 1], s5o[:, 1, bl, cc:cc + 1], rs5o
                K.op("dve", lambda e, c0=c0, ire=ire, rho_b=rho_b: e.tensor_tensor_scan(out=wz[2][:, c0:c0 + Lb], data0=rho_b, data1=wz[0][:, c0:c0 + Lb], initial=ire, op0=ALU.mult, op1=ALU.add),
                     reads=[rwz[0], rinit, X.rconst], writes=[rwz[2]])
                K.op("dve", lambda e, c0=c0, iim=iim, rho_b=rho_b: e.tensor_tensor_scan(out=wz[3][:, c0:c0 + Lb], data0=rho_b, data1=wz[1][:, c0:c0 + Lb], initial=iim, op0=ALU.mult, op1=ALU.add),
                     reads=[rwz[1], rinit, X.rconst], writes=[rwz[3]])
                zre_e, zim_e = wz[2][:, c0 + Lb - 1:c0 + Lb], wz[3][:, c0 + Lb - 1:c0 + Lb]
                K.op("dve", lambda e, zre_e=zre_e, cc=cc: e.tensor_scalar(out=se[:, 0:1], in0=zre_e, scalar1=pend[:, 0, cc:cc + 1], scalar2=None, op0=ALU.mult), reads=[rwz[2], X.rconst], writes=[rse])
                K.op("dve", lambda e, zre_e=zre_e, cc=cc: e.tensor_scalar(out=se[:, 1:2], in0=zre_e, scalar1=pend[:, 1, cc:cc + 1], scalar2=None, op0=ALU.mult), reads=[rwz[2], X.rconst], writes=[rse])
                K.op("dve", lambda e, zim_e=zim_e, cc=cc, ore=ore: e.scalar_tensor_tensor(out=ore, in0=zim_e, scalar=pend[:, 2, cc:cc + 1], in1=se[:, 0:1], op0=ALU.mult, op1=ALU.add), reads=[rwz[3], rse, X.rconst], writes=[rout])
                K.op("dve", lambda e, zim_e=zim_e, cc=cc, oim=oim: e.scalar_tensor_tensor(out=oim, in0=zim_e, scalar=pend[:, 0, cc:cc + 1], in1=se[:, 1:2], op0=ALU.mult, op1=ALU.add), reads=[rwz[3], rse, X.rconst], writes=[rout])
            if X.dbg and gi == 0 and cc == X.dbg_cc:
                for ii in range(2, 4):
                    K.dma("sp", "dbgw", X.dbgo["wz"][ii], wz[ii], reads=[rwz[ii]], final=True)
                K.dma("sp", "dbgu", X.dbgo["uT"], uT, reads=[ru], final=True)
            K.op("dve", lambda e, Pre=Pre: e.tensor_tensor(out=v3(pr[0]), in0=v3(wz[2]), in1=Pre, op=ALU.mult), reads=[rwz[2], rtab], writes=[rpr[0]])
            K.op("dve", lambda e, PimN=PimN: e.tensor_tensor(out=v3(pr[1]), in0=v3(wz[3]), in1=PimN, op=ALU.mult), reads=[rwz[3], rtab], writes=[rpr[1]])
            K.op("pool", lambda e, Pim=Pim: e.tensor_tensor(out=v3(pr[2]), in0=v3(wz[2]), in1=Pim, op=ALU.mult), reads=[rwz[2], rtab], writes=[rpr[2]])
            K.op("pool", lambda e, Pre=Pre: e.tensor_tensor(out=v3(pr[3]), in0=v3(wz[3]), in1=Pre, op=ALU.mult), reads=[rwz[3], rtab], writes=[rpr[3]])
            for a in range(4):
                K.op("pe", lambda e, a=a, q8=q8, by=by, cc=cc, ws=ws: e.matmul(X.psum[:, by, 0:n], lhsT=ws[:, q8 * 4 + (2 if a < 2 else 3), :], rhs=pr[a], start=(cc % 4 == 0 and a == 0), stop=(cc % 4 == 3 and a == 3)),
                     reads=[rws, rpr[a]], writes=[X.rps[by]])
            if cc % 4 == 3:
                K.op("dve", lambda e, fc=fc, by=by: e.scalar_tensor_tensor(out=sgm, in0=uT[:, fc, 0:n], scalar=col(X, C_SD + fc), in1=X.psum[:, by, 0:n], op0=ALU.mult, op1=ALU.add), reads=[ru, X.rps[by], X.rcols], writes=[rsg])
                K.op("act", lambda e, fc=fc: e.activation(out=yg[:, fc, 0:n], in_=sgm, func=AF.Gelu_apprx_tanh), reads=[rsg], writes=[ryg])
    if typ == 0 and last_prompt:
        K.dma("sp", "s5p", O["s5_p"], X.s5st[:, :, :], reads=[X.rs5st], final=True)
    if typ == 1:
        X.arena_dmas.append(K.dma("sp", "s5o", O["s5_o"], s5o, reads=[rs5o], final=True))
    WG = I["w_glu"].rearrange("(k p) c -> p k c", p=128)
    for p2 in range(2):
        sa, rsa = wslab(X, WG[:, :, p2 * 512:(p2 + 1) * 512], 8, 512)
        sb_, rsb = wslab(X, WG[:, :, 1024 + p2 * 512:1024 + (p2 + 1) * 512], 8, 512)
        for j in range(4):
            ba = bank(X, "glu", [0, 2])
            bb = ba + 1
            mm_fm(X, ba, sa, rsa, j, yg, ryg, n, 8)
            mm_fm(X, bb, sb_, rsb, j, yg, ryg, n, 8)
            K.op("act", lambda e, bb=bb: e.activation(out=sgm, in_=X.psum[:, bb, 0:n], func=AF.Sigmoid), reads=[X.rps[bb]], writes=[rsg])
            K.op("dve", lambda e, ba=ba, p2=p2, j=j: e.tensor_tensor(out=catT[:, 8 + p2 * 4 + j, 0:n], in0=X.psum[:, ba, 0:n], in1=sgm, op=ALU.mult), reads=[X.rps[ba], rsg], writes=[rcat])
    fence(X)
    arena_reset(X, keep)
    fT = aalloc(X, [128, 16, n], F32); rf = Res("f")
    out_proj(X, I["w_out_odd"], n, catT, rcat, fT, rf)
    post_residual(X, 1, 1, n, fT, rf)


def build_all(ngroups_p=8, dbg=False, dbg_cc=0):
    X = build_program()
    K, nc, I, O = X.K, X.nc, X.I, X.O
    X.dbg = dbg
    X.dbg_cc = dbg_cc
    if dbg:
        X.dbgo = {"tq": nc.dram_tensor("dbg_tq", [4, 128, 512], F32, kind="ExternalOutput").ap(),
                  "wz": nc.dram_tensor("dbg_wz", [4, 128, 512], F32, kind="ExternalOutput").ap(),
                  "uT": nc.dram_tensor("dbg_uT", [128, 8, 512], BF16, kind="ExternalOutput").ap()}
    X.ffno = nc.alloc_sbuf_tensor("s_ffno", [128, 2, NFC, 4, 2], F32)
    X.rffno = Res("ffno")
    X.cmaskc = nc.alloc_sbuf_tensor("s_cmaskc", [128, 4, 16], F32)
    cm_in = nc.dram_tensor("poolcorr", [128, 4, 16], F32, kind="ExternalInput").ap()
    setup(X)
    K.dma("sp", "pcorr", X.cmaskc[:, :, :], cm_in, writes=[X.rconst])
    groups = [(g * 512, 512, 0) for g in range(ngroups_p)] + [(NPT, NST, 1)]
    xv = I["xT"].rearrange("(c p) t -> p c t", p=128)
    yv = O["yT"].rearrange("(c p) t -> p c t", p=128)
    for gi, (t0, n, typ) in enumerate(groups):
        lastp = (typ == 0 and gi == ngroups_p - 1)
        K.dma("sp", "xin", X.xT[:, :, 0:n], xv[:, :, t0:t0 + n], writes=[X.rx])
        even_mixer(X, gi, t0, n, typ, lastp)
        conv_ffn(X, 0, gi, n, typ, lastp)
        odd_mixer(X, gi, t0, n, typ, lastp)
        conv_ffn(X, 1, gi, n, typ, lastp)
        K.dma("sp", "yout", yv[:, :, t0:t0 + n], X.xT[:, :, 0:n], reads=[X.rx], final=True)
    K.emit()
    return nc


_NC_CACHE = {}


def _host_consts():
    cm = np.zeros((128, 2, 136), np.float32)
    for typ, Lc in ((0, 64), (1, 32)):
        t = np.arange(128)
        same = (t[:, None] // Lc) == (t[None, :] // Lc)
        cm[:, typ, 0:128] = np.where(same & (t[:, None] > t[None, :]), -1.0 / 16, 0.0)
        for c in range(128 // Lc):
            cm[:, typ, 128 + c] = np.where(t // Lc == c, -1.0 / 16, 0.0)
            cm[:, typ, 132 + c] = np.where(t // Lc == c, 1.0, 0.0)
    MB = np.zeros((128, 4, 2), np.float32)
    for g8 in range(8):
        for q in range(4):
            for gl in range(2):
                if g8 == 2 * q + gl:
                    MB[g8 * 16:(g8 + 1) * 16, q, gl] = 1.0
    MC = np.zeros((128, 2, 4, 8), np.float32)
    for gl in range(2):
        for q in range(4):
            MC[gl * 64:(gl + 1) * 64, 0, q, 2 * q + gl] = 1.0
            MC[gl * 64:(gl + 1) * 64, 1, q, 2 * q + gl] = -1.0
    pc = np.ones((128, 4, 16), np.float32)
    for g in range(4):
        w = 2 << g
        for t in range(16):
            pc[:, g, t] = w / min(t + 1, w)
    return cm, MB, MC, pc


def kernel(**inp):
    f32 = np.float32
    A = {k: np.asarray(v) for k, v in inp.items()}
    if "nc" not in _NC_CACHE:
        _NC_CACHE["nc"] = build_all(8)
    nc = _NC_CACHE["nc"]
    cm, MB, MC, pc = _host_consts()

    def chunkcols(v):
        return np.ascontiguousarray(v.reshape(-1, 128).T)

    cols = np.zeros((128, NCOL), f32)
    norms = [A["norm_mix_pre"], A["norm_mix_post"], A["norm_ffn_pre"], A["norm_ffn_post"]]
    for kind in range(4):
        for l in range(2):
            cols[:, C_NORM + (kind * 2 + l) * 16:C_NORM + (kind * 2 + l + 1) * 16] = chunkcols(norms[kind][l])
    cols[:, C_OG:C_OG + 8] = chunkcols(A["b_out_norm"][0])
    cols[:, C_CS:C_CS + 8] = chunkcols(A["c_scale"][0])
    cols[:, C_SD:C_SD + 8] = chunkcols(A["s5_d"][0])
    for l in range(2):
        for t in range(3):
            cols[:, C_CW + (l * 3 + t) * NFC:C_CW + (l * 3 + t + 1) * NFC] = chunkcols(A["ffn_conv_w"][l, t])
        cols[:, C_CB + l * NFC:C_CB + (l + 1) * NFC] = chunkcols(A["ffn_conv_b"][l])
    ws = A["a_w_s"][0]
    wsT_p = np.ascontiguousarray(ws.transpose(2, 0, 1))
    wsT_s = np.zeros((128, 8, 128), f32)
    bs_s = np.zeros((8, 128), f32)
    for s in range(4):
        wsT_s[32 * s:32 * s + 32, :, 32 * s:32 * s + 32] = wsT_p[0:32, :, 0:32]
        bs_s[:, 32 * s:32 * s + 32] = A["a_b_s"][0][:, 0:32]
    a_re, a_im, ldt = A["s5_a_re"][0], A["s5_a_im"][0], A["s5_log_dt"][0]

    def collay(m):
        return np.ascontiguousarray(m.reshape(32, 2, 64).transpose(1, 2, 0).reshape(128, 32))

    ldt2 = np.repeat(ldt[:, None], 64, axis=1)
    s5col = np.stack([collay(a_re), collay(a_im), collay(ldt2)], axis=1).astype(f32)
    s5row = np.stack([a_re.reshape(-1), a_im.reshape(-1), ldt2.reshape(-1)], axis=0).astype(f32)

    def btlay(b):
        return np.ascontiguousarray(b.reshape(8, 8, 64, 16).transpose(1, 3, 0, 2).reshape(128, 8, 64))

    def ctlay(c):
        return np.ascontiguousarray(c.reshape(32, 2, 16, 64).transpose(1, 3, 0, 2).reshape(128, 32, 16))

    Bt = np.stack([btlay(A["s5_b_re"][0]), btlay(A["s5_b_im"][0])]).astype(f32)
    Ct = np.stack([ctlay(A["s5_c_re"][0]), ctlay(A["s5_c_im"][0])]).astype(f32)
    shared = {
        "w_in_even": A["w_in_even"][0], "w_out_even": A["w_out_even"][0], "w_in_odd": A["w_in_odd"][0],
        "w_glu": A["s5_w_glu"][0], "w_out_odd": A["w_out_odd"][0], "w_up": A["ffn_w_up"], "w_down": A["ffn_w_down"],
        "c_map": np.ascontiguousarray(A["c_map"][0].reshape(1024, 256)), "cols": cols, "wsT_p": wsT_p, "wsT_s": wsT_s,
        "bs_p": np.ascontiguousarray(A["a_b_s"][0].reshape(-1)), "bs_s": bs_s.reshape(-1), "v_gain": A["a_v_norm"][0],
        "gbias": A["b_gate_bias"][0], "w_gate": A["b_w_gate"][0], "cmask": cm, "s5col": s5col, "s5row": s5row,
        "Bt": Bt, "Ct": Ct, "MB": MB, "MC": MC, "invcnt": np.zeros(4 * TT, f32), "poolcorr": pc,
    }
    shared = {k: np.ascontiguousarray(v, dtype=f32) for k, v in shared.items()}
    in_maps = []
    for c in range(8):
        xT = np.zeros((D, TT), f32)
        if c < 2:
            xT[:, :NPT] = A["x_prompt"][c].T
        sq = slice(4 * c, 4 * c + 4)
        xT[:, NPT:] = A["x_sample"][sq].reshape(128, D).T
        m = dict(shared)
        m["xT"] = xT
        m["gla_s"] = np.ascontiguousarray(A["state_gla"][0, sq].transpose(0, 2, 1, 3).reshape(4, 128, 1024))
        m["pool_s"] = np.ascontiguousarray(A["state_pool"][0, sq].reshape(4, 15, 8, 128).transpose(3, 2, 0, 1))
        m["s5re_s"] = np.ascontiguousarray(np.stack([collay(A["state_s5_re"][0, 4 * c + s]) for s in range(4)], axis=1))
        m["s5im_s"] = np.ascontiguousarray(np.stack([collay(A["state_s5_im"][0, 4 * c + s]) for s in range(4)], axis=1))
        m["ffn_s"] = np.ascontiguousarray(A["state_ffn_conv"][:, sq].reshape(2, 4, 2, NFC, 128).transpose(4, 0, 3, 1, 2))
        in_maps.append({k: np.ascontiguousarray(v, dtype=f32) for k, v in m.items()})
    res = run_bass_kernel_spmd(nc, in_maps, core_ids=list(range(8)))
    R = res.results
    _NC_CACHE["R"] = R
    y_prompt = np.stack([R[c]["yT"][:, :NPT].T for c in range(2)]).astype(f32)
    y_sample = np.concatenate([R[c]["yT"][:, NPT:].T.reshape(4, 32, D) for c in range(8)]).astype(f32)

    def gl(a):
        return a.reshape(128, 4, 256).transpose(1, 0, 2)

    def uncol(a):
        return a.reshape(2, 64, 32).transpose(2, 0, 1).reshape(64, 64)

    gla_prompt = np.stack([gl(R[c]["gla_p"]) for c in range(2)])[None].astype(f32)
    gla_sample = np.concatenate([np.stack([gl(R[c]["gla_o"][s]) for s in range(4)]) for c in range(8)])[None].astype(f32)
    av_sample = np.concatenate([R[c]["av_o"].reshape(4, 32, 1024) for c in range(8)])[None].astype(f32)
    pool_prompt = np.stack([R[c]["pool_p"].transpose(2, 1, 0).reshape(15, 1024) for c in range(2)])[None].astype(f32)
    pool_sample = np.concatenate([R[c]["pool_o"].transpose(2, 3, 1, 0).reshape(4, 15, 1024) for c in range(8)])[None].astype(f32)
    s5re_p = np.stack([uncol(R[c]["s5_p"][:, 0]) for c in range(2)])[None].astype(f32)
    s5im_p = np.stack([uncol(R[c]["s5_p"][:, 1]) for c in range(2)])[None].astype(f32)
    s5re_s = np.concatenate([np.stack([uncol(R[c]["s5_o"][:, 0, s]) for s in range(4)]) for c in range(8)])[None].astype(f32)
    s5im_s = np.concatenate([np.stack([uncol(R[c]["s5_o"][:, 1, s]) for s in range(4)]) for c in range(8)])[None].astype(f32)
    ffn_prompt = np.stack([np.stack([R[c]["ffn_p"][:, l].transpose(2, 1, 0).reshape(2, DFF) for c in range(2)]) for l in range(2)]).astype(f32)
    ffn_sample = np.stack([np.concatenate([R[c]["ffn_o"][:, l].transpose(2, 3, 1, 0).reshape(4, 2, DFF) for c in range(8)]) for l in range(2)]).astype(f32)
    return (y_prompt, y_sample, gla_prompt, gla_sample, av_sample, pool_prompt, pool_sample,
            s5re_p, s5im_p, s5re_s, s5im_s, ffn_prompt, ffn_sample)
```

```python
import numpy as np
import concourse.bass as bass
import concourse.mybir as mybir
from concourse.bass_utils import run_bass_kernel_spmd

F32 = mybir.dt.float32
BF16 = mybir.dt.bfloat16
I32 = mybir.dt.int32
AF = mybir.ActivationFunctionType
ALU = mybir.AluOpType


class Res:
    __slots__ = ("w", "readers", "name")

    def __init__(self, name=""):
        self.w = None
        self.readers = {}
        self.name = name


class Op:
    __slots__ = ("eng", "fn", "deps", "signal", "val", "sem", "is_dma", "key")

    def __init__(self, eng, fn):
        self.eng = eng
        self.fn = fn
        self.deps = []
        self.signal = False
        self.val = None
        self.sem = None
        self.is_dma = False
        self.key = eng


class Sched:
    ENG = ("pe", "act", "dve", "pool", "sp")

    def __init__(self, nc):
        self.nc = nc
        self.ops = {e: [] for e in self.ENG}
        self.esem = {e: nc.alloc_semaphore("es_" + e) for e in self.ENG}
        self.dma_sems = {}
        self.dma_tot = {}
        self.final = []

    def dsem(self, name):
        if name not in self.dma_sems:
            self.dma_sems[name] = self.nc.alloc_semaphore("ds_" + name)
            self.dma_tot[name] = 0
        return name

    def op(self, eng, fn, reads=(), writes=(), extra=()):
        o = Op(eng, fn)
        deps = []
        for r in reads:
            if r.w is not None:
                deps.append(r.w)
        for w in writes:
            deps.extend(w.readers.values())
            if w.w is not None:
                deps.append(w.w)
        deps.extend(extra)
        seen = set()
        for d in deps:
            if d is o or id(d) in seen:
                continue
            seen.add(id(d))
            if d.eng == "pe" and eng == "pe" and not d.is_dma:
                continue
            if d.eng == eng and eng == "sp" and not d.is_dma:
                continue
            o.deps.append(d)
            d.signal = True
        for r in reads:
            r.readers[o.key] = o
        for w in writes:
            w.w = o
            w.readers = {}
        self.ops[eng].append(o)
        return o

    def dma(self, eng, semname, out, in_, reads=(), writes=(), extra=(), final=False):
        self.dsem(semname)
        self.dma_tot[semname] += 16
        val = self.dma_tot[semname]
        sem = self.dma_sems[semname]

        def fn(e, out=out, in_=in_, sem=sem):
            return e.dma_start(out=out, in_=in_).then_inc(sem, 16)

        o = Op(eng, fn)
        o.is_dma = True
        o.sem = sem
        o.val = val
        o.key = "dma_" + semname
        deps = []
        for r in reads:
            if r.w is not None:
                deps.append(r.w)
        for w in writes:
            deps.extend(w.readers.values())
            if w.w is not None:
                deps.append(w.w)
        deps.extend(extra)
        seen = set()
        for d in deps:
            if id(d) in seen:
                continue
            seen.add(id(d))
            o.deps.append(d)
            d.signal = True
        for r in reads:
            r.readers[o.key] = o
        for w in writes:
            w.w = o
            w.readers = {}
        self.ops[eng].append(o)
        if final:
            self.final.append(o)
        return o

    def finalize(self):
        for e in self.ENG:
            c = 0
            for o in self.ops[e]:
                if o.is_dma:
                    continue
                if o.signal:
                    c += 1
                    o.val = c
                    o.sem = self.esem[e]

    def emit(self):
        self.finalize()
        nc = self.nc
        sched = self

        def run(e, name):
            waited = {}
            for o in sched.ops[name]:
                for d in o.deps:
                    sid = id(d.sem)
                    if waited.get(sid, 0) >= d.val:
                        continue
                    waited[sid] = d.val
                    e.wait_ge(d.sem, d.val)
                ins = o.fn(e)
                if (not o.is_dma) and o.signal:
                    ins.then_inc(o.sem, 1)
            if name == "sp":
                for o in sched.final:
                    sid = id(o.sem)
                    if waited.get(sid, 0) >= o.val:
                        continue
                    waited[sid] = o.val
                    e.wait_ge(o.sem, o.val)

        with nc.Block() as block:
            @block.sync
            def _(e):
                run(e, "sp")

            @block.tensor
            def _(e):
                run(e, "pe")

            @block.scalar
            def _(e):
                run(e, "act")

            @block.vector
            def _(e):
                run(e, "dve")

            @block.gpsimd
            def _(e):
                run(e, "pool")


import math

D = 2048
NPT = 4096
NST = 128
TT = NPT + NST
DFF = 5632
NFC = 44
EPS = 1e-6
PI = math.pi
NSLOT = 2
SLOT_E = 8192
RING_E = 18432
LB = 64

C_NORM = 0
C_OG = 128
C_CS = 136
C_SD = 144
C_CW = 152
C_CB = C_CW + 264
NCOL = C_CB + 88


class Ctx:
    pass


def build_program(debug=False):
    nc = bass.Bass("TRN2", target_bir_lowering=False)
    K = Sched(nc)
    X = Ctx()
    X.nc, X.K = nc, K

    def din(name, shape, dt=F32):
        return nc.dram_tensor(name, list(shape), dt, kind="ExternalInput").ap()

    def dout(name, shape, dt=F32):
        return nc.dram_tensor(name, list(shape), dt, kind="ExternalOutput").ap()

    I = {}
    I["xT"] = din("xT", [D, TT])
    I["w_in_even"] = din("w_in_even", [D, 5136])
    I["w_out_even"] = din("w_out_even", [D, D])
    I["w_in_odd"] = din("w_in_odd", [D, D])
    I["w_glu"] = din("w_glu", [1024, 2048])
    I["w_out_odd"] = din("w_out_odd", [D, D])
    I["w_up"] = din("w_up", [2, D, 2 * DFF])
    I["w_down"] = din("w_down", [2, DFF, D])
    I["c_map"] = din("c_map", [1024, 256])
    I["cols"] = din("cols", [128, NCOL])
    I["wsT_p"] = din("wsT_p", [128, 8, 128])
    I["wsT_s"] = din("wsT_s", [128, 8, 128])
    I["bs_p"] = din("bs_p", [1024])
    I["bs_s"] = din("bs_s", [1024])
    I["v_gain"] = din("v_gain", [1024])
    I["gbias"] = din("gbias", [512])
    I["w_gate"] = din("w_gate", [16, 512])
    I["cmask"] = din("cmask", [128, 2, 136])
    I["gla_s"] = din("gla_s", [4, 128, 1024])
    I["pool_s"] = din("pool_s", [128, 8, 4, 15])
    I["s5re_s"] = din("s5re_s", [128, 4, 32])
    I["s5im_s"] = din("s5im_s", [128, 4, 32])
    I["ffn_s"] = din("ffn_s", [128, 2, NFC, 4, 2])
    I["s5col"] = din("s5col", [128, 3, 32])
    I["s5row"] = din("s5row", [3, 4096])
    I["Bt"] = din("Bt", [2, 128, 8, 64])
    I["Ct"] = din("Ct", [2, 128, 32, 16])
    I["MB"] = din("MB", [128, 4, 2])
    I["MC"] = din("MC", [128, 2, 4, 8])
    I["invcnt"] = din("invcnt", [4 * TT])

    O = {}
    O["yT"] = dout("yT", [D, TT])
    O["gla_p"] = dout("gla_p", [128, 1024])
    O["gla_o"] = dout("gla_o", [4, 128, 1024])
    O["av_o"] = dout("av_o", [128, 1024])
    O["pool_p"] = dout("pool_p", [128, 8, 15])
    O["pool_o"] = dout("pool_o", [128, 8, 4, 15])
    O["s5_p"] = dout("s5_p", [128, 2, 32])
    O["s5_o"] = dout("s5_o", [128, 2, 4, 32])
    O["ffn_p"] = dout("ffn_p", [128, 2, NFC, 2])
    O["ffn_o"] = dout("ffn_o", [128, 2, NFC, 4, 2])
    X.I, X.O = I, O

    dk = dict(kind="ExternalOutput") if debug else {}
    X.ws5 = nc.dram_tensor("ws5", [4, 128, 8 * 4 * 128], BF16, **dk).ap()
    X.ptab = nc.dram_tensor("ptab", [128, 3 * 32 * LB], F32, **dk).ap()

    sb = lambda name, shape, dt: nc.alloc_sbuf_tensor('s_' + name, shape, dt)
    X.ring = sb("wring", [128, RING_E], BF16)
    X.rpos = 0
    X.live = []
    X.wcnt = 0
    X.xT = sb("xTs", [128, 16, 512], F32)
    X.rx = Res("x")
    X.cols = sb("cols", [128, NCOL], F32)
    X.rcols = Res("cols")
    X.ones = sb("ones", [128, 128], BF16)
    X.rones = Res("ones")
    X.wmT = [sb("wmT_p", [128, 8, 128], BF16), sb("wmT_s", [128, 8, 128], BF16)]
    X.rwmT = Res("wmT")
    X.wgate = sb("wgate", [16, 512], BF16)
    X.cmask = sb("cmask", [128, 2, 136], F32)
    X.rconst = Res("const")
    X.S = sb("glaS", [128, 1024], F32)
    X.rS = Res("S")
    X.Sbf = sb("glaSbf", [128, 1024], BF16)
    X.rSbf = Res("Sbf")
    X.phalo = sb("phalo", [128, 8, 15], F32)
    X.rphalo = Res("phalo")
    X.chalo = sb("chalo", [128, 2, NFC, 2], F32)
    X.rchalo = [Res("chalo0"), Res("chalo1")]
    X.s5st = sb("s5st", [128, 2, 32], F32)
    X.rs5st = Res("s5st")
    X.rho = sb("rho", [128, 32], F32)
    X.pend = sb("pend", [128, 3, 32], F32)
    X.pend32 = sb("pend32", [128, 3, 32], F32)
    X.sqb = [sb(f"sqb{i}", [128, 512], BF16) for i in range(2)]
    X.rsqb = [Res("sqb0"), Res("sqb1")]
    X.sqc = 0
    X.nrm = sb("nrm", [128, 2, 512], F32)
    X.rnrm = Res("nrm")
    X.tmpx = [sb(f"tmpx{i}", [128, 512], F32) for i in range(2)]
    X.rtmpx = [Res("tmpx0"), Res("tmpx1")]
    ARENA_F = 27776
    X.arena = sb("arena", [128, ARENA_F], F32)
    X.ARENA_F = ARENA_F
    X.apos = 0
    X.psum = nc.alloc_psum_tensor("ps", [128, 8, 512], F32)
    X.rps = [Res(f"ps{i}") for i in range(8)]
    X.pscnt = {}
    return X


def arena_reset(X, keep=0):
    X.apos = keep


def aalloc(X, shape, dt):
    n = 1
    for s in shape[1:]:
        n *= s
    nf = n if dt == F32 else (n + 1) // 2
    nf = (nf + 7) // 8 * 8
    a0 = X.apos
    X.apos += nf
    assert X.apos <= X.ARENA_F, (X.apos, X.ARENA_F)
    v = X.arena[:, a0:a0 + nf]
    if dt != F32:
        v = v.bitcast(dt)
    v = v[:, 0:n]
    if len(shape) == 3:
        v = v.rearrange("p (a b) -> p a b", b=shape[2])
    elif len(shape) == 4:
        v = v.rearrange("p (a b c) -> p a b c", b=shape[2], c=shape[3])
    return v


def fence(X):
    K = X.K
    last = list(getattr(X, "arena_dmas", []))
    X.arena_dmas = []
    for e in ("pe", "act", "dve", "pool"):
        if K.ops[e]:
            last.append(K.ops[e][-1])
    for e in ("pe", "act", "dve", "pool"):
        K.op(e, (lambda en: (lambda eng: eng.nop()))(e), extra=last)
    X.fence_ops = [K.ops[e][-1] for e in ("pe", "act", "dve", "pool")]


def bank(X, group, banks):
    c = X.pscnt.get(group, 0)
    X.pscnt[group] = c + 1
    return banks[c % len(banks)]


def wslab(X, src_ap, kc, cols, need=0):
    K = X.K
    ne = kc * cols
    if X.rpos + max(ne, need) > RING_E:
        X.rpos = 0
    a0, a1 = X.rpos, X.rpos + ne
    X.rpos = a1
    extra = []
    keep = []
    for (s0, e0, rr) in X.live:
        if s0 < a1 and e0 > a0:
            extra.extend(rr.readers.values())
            if rr.w is not None:
                extra.append(rr.w)
        else:
            keep.append((s0, e0, rr))
    X.live = keep
    r = Res("wring")
    view = X.ring[:, a0:a1].rearrange("p (k c) -> p k c", c=cols)
    sem = f"w{X.wcnt % 8}"
    X.wcnt += 1
    K.dma("pool", sem, view, src_ap, writes=[r], extra=extra)
    X.live.append((a0, a1, r))
    return view, r


def col(X, idx):
    return X.cols[:, idx:idx + 1]


def norm_finish(X, n):
    K = X.K
    ps7 = X.psum[:, 7, 0:n]
    K.op("act", lambda e: e.activation(out=X.nrm[:, 1, 0:n], in_=ps7, func=AF.Sqrt, scale=1.0 / D, bias=X.epsc[:, 0:1]),
         reads=[X.rps[7], X.rconst], writes=[X.rnrm])
    K.op("dve", lambda e: e.reciprocal(out=X.nrm[:, 0, 0:n], in_=X.nrm[:, 1, 0:n]), reads=[X.rnrm], writes=[X.rnrm])


def sq_accum(X, src, rsrc, n, c, nchunks=16):
    K = X.K
    i = X.sqc % 2
    X.sqc += 1
    sq = X.sqb[i]
    K.op("act", lambda e: e.activation(out=sq[:, 0:n], in_=src, func=AF.Square), reads=[rsrc], writes=[X.rsqb[i]])
    K.op("pe", lambda e: e.matmul(X.psum[:, 7, 0:n], lhsT=X.ones[:, :], rhs=sq[:, 0:n], start=(c == 0), stop=(c == nchunks - 1)),
         reads=[X.rsqb[i], X.rones], writes=[X.rps[7]])


def pre_norm(X, layer, kind, n, hT, rh):
    K = X.K
    for c in range(16):
        sq_accum(X, X.xT[:, c, 0:n], X.rx, n, c)
    norm_finish(X, n)
    for c in range(16):
        g = col(X, C_NORM + (kind * 2 + layer) * 16 + c)
        K.op("dve", lambda e, c=c, g=g: e.scalar_tensor_tensor(out=hT[:, c, 0:n], in0=X.xT[:, c, 0:n], scalar=g, in1=X.nrm[:, 0, 0:n], op0=ALU.mult, op1=ALU.mult),
             reads=[X.rx, X.rnrm, X.rcols], writes=[rh])


def post_residual(X, layer, kind, n, fT, rf):
    K = X.K
    norm_finish(X, n)
    for c in range(16):
        g = col(X, C_NORM + (kind * 2 + layer) * 16 + c)
        i = c % 2
        t = X.tmpx[i]
        K.op("dve", lambda e, c=c, g=g, t=t: e.scalar_tensor_tensor(out=t[:, 0:n], in0=fT[:, c, 0:n], scalar=g, in1=X.nrm[:, 0, 0:n], op0=ALU.mult, op1=ALU.mult),
             reads=[rf, X.rnrm, X.rcols], writes=[X.rtmpx[i]])
        K.op("pool", lambda e, c=c, t=t: e.tensor_tensor(out=X.xT[:, c, 0:n], in0=X.xT[:, c, 0:n], in1=t[:, 0:n], op=ALU.add),
             reads=[X.rtmpx[i], X.rx], writes=[X.rx])


def mm_fm(X, b, slab, rslab, j, src, rsrc, n, kc, extra_reads=()):
    K = X.K
    for k in range(kc):
        K.op("pe", lambda e, k=k: e.matmul(X.psum[:, b, 0:n], lhsT=slab[:, k, j * 128:(j + 1) * 128], rhs=src[:, k, 0:n], start=(k == 0), stop=(k == kc - 1)),
             reads=[rslab, rsrc], writes=[X.rps[b]])


def out_proj(X, wsrc, n, catT, rcat, fT, rf, kc=16):
    K = X.K
    wv = wsrc.rearrange("(k p) c -> p k c", p=128)
    cols_per = 256 if kc == 16 else 128
    nsl = 2048 // cols_per
    c = 0
    for s in range(nsl):
        slab, rs = wslab(X, wv[:, :, s * cols_per:(s + 1) * cols_per], kc, cols_per)
        for j in range(cols_per // 128):
            b = bank(X, "main", [0, 1, 2, 3, 4, 5])
            mm_fm(X, b, slab, rs, j, catT, rcat, n, kc)
            K.op("act", lambda e, b=b, c=c: e.copy(out=fT[:, c, 0:n], in_=X.psum[:, b, 0:n]), reads=[X.rps[b]], writes=[rf])
            sq_accum(X, X.psum[:, b, 0:n], X.rps[b], n, c)
            c += 1


def setup(X):
    K, nc, I = X.K, X.nc, X.I
    arena_reset(X)
    X.epsc = nc.alloc_sbuf_tensor("s_epsc", [128, 4], F32)
    K.op("dve", lambda e: e.memset(X.epsc[:, 0:1], EPS), writes=[X.rconst])
    K.op("dve", lambda e: e.memset(X.epsc[:, 1:2], 1.0), writes=[X.rconst])
    K.op("dve", lambda e: e.memset(X.epsc[:, 2:3], 0.0), writes=[X.rconst])
    K.op("dve", lambda e: e.memset(X.ones[:, :], 1.0), writes=[X.rones])
    K.dma("sp", "cols", X.cols[:, :], I["cols"], writes=[X.rcols])
    K.dma("sp", "cmask", X.cmask[:, :, :], I["cmask"], writes=[X.rconst])
    t0 = aalloc(X, [128, 8, 128], F32)
    t1 = aalloc(X, [128, 8, 128], F32)
    tg = aalloc(X, [128, 512], F32)
    r0, r1, rg = Res(), Res(), Res()
    K.dma("sp", "su0", t0, I["wsT_p"], writes=[r0])
    K.dma("sp", "su1", t1, I["wsT_s"], writes=[r1])
    K.dma("sp", "su2", tg[0:16, :], I["w_gate"], writes=[rg])
    K.op("dve", lambda e: e.memset(t0[64:128, :, 0:64], 0.0), reads=[r0], writes=[r0])
    K.op("dve", lambda e: e.tensor_copy(out=X.wmT[0][:, :, :], in_=t0), reads=[r0], writes=[X.rwmT])
    K.op("dve", lambda e: e.tensor_copy(out=X.wmT[1][:, :, :], in_=t1), reads=[r1], writes=[X.rwmT])
    K.op("dve", lambda e: e.tensor_copy(out=X.wgate[:, :], in_=tg[0:16, :]), reads=[rg], writes=[X.rconst])
    K.op("dve", lambda e: e.memset(X.S[:, :], 0.0), writes=[X.rS])
    K.op("dve", lambda e: e.memset(X.phalo[:, :, :], 0.0), writes=[X.rphalo])
    K.op("dve", lambda e: e.memset(X.chalo[:, 0, :, :], 0.0), writes=[X.rchalo[0]])
    K.op("dve", lambda e: e.memset(X.chalo[:, 1, :, :], 0.0), writes=[X.rchalo[1]])
    K.op("dve", lambda e: e.memset(X.s5st[:, :, :], 0.0), writes=[X.rs5st])
    s5_setup(X)


def s5_params(X, are, aim, ldt, shape, need_f, R=None):
    K = X.K
    R = R if R is not None else Res()
    mk = lambda: aalloc(X, shape, F32)
    dt, rho, th, sn, cs, y, kf, m = mk(), mk(), mk(), mk(), mk(), mk(), mk(), mk()
    ki = aalloc(X, shape, F32).bitcast(I32)
    rd = [X.rconst]
    K.op("act", lambda e: e.activation(out=dt, in_=ldt, func=AF.Exp), reads=[R] + rd, writes=[R])
    K.op("dve", lambda e: e.tensor_tensor(out=rho, in0=are, in1=dt, op=ALU.mult), reads=[R], writes=[R])
    K.op("act", lambda e: e.activation(out=rho, in_=rho, func=AF.Exp), reads=[R], writes=[R])
    K.op("dve", lambda e: e.tensor_tensor(out=th, in0=aim, in1=dt, op=ALU.mult), reads=[R], writes=[R])

    def sin_of(dst, shift):
        K.op("dve", lambda e: e.tensor_scalar(out=y, in0=th, scalar1=shift, scalar2=None, op0=ALU.add), reads=[R], writes=[R])
        K.op("dve", lambda e: e.tensor_scalar(out=ki, in0=y, scalar1=1.0 / (2 * PI), scalar2=None, op0=ALU.mult), reads=[R], writes=[R])
        K.op("dve", lambda e: e.tensor_copy(out=kf, in_=ki), reads=[R], writes=[R])
        K.op("dve", lambda e: e.scalar_tensor_tensor(out=y, in0=kf, scalar=-2 * PI, in1=y, op0=ALU.mult, op1=ALU.add), reads=[R], writes=[R])
        K.op("dve", lambda e: e.tensor_scalar(out=m, in0=y, scalar1=PI, scalar2=None, op0=ALU.is_gt), reads=[R], writes=[R])
        K.op("dve", lambda e: e.scalar_tensor_tensor(out=y, in0=m, scalar=-2 * PI, in1=y, op0=ALU.mult, op1=ALU.add), reads=[R], writes=[R])
        K.op("dve", lambda e: e.tensor_scalar(out=m, in0=y, scalar1=-PI, scalar2=None, op0=ALU.is_lt), reads=[R], writes=[R])
        K.op("dve", lambda e: e.scalar_tensor_tensor(out=y, in0=m, scalar=2 * PI, in1=y, op0=ALU.mult, op1=ALU.add), reads=[R], writes=[R])
        K.op("dve", lambda e: e.tensor_scalar(out=y, in0=y, scalar1=PI, scalar2=-PI, op0=ALU.min, op1=ALU.max), reads=[R], writes=[R])
        K.op("act", lambda e: e.activation(out=dst, in_=y, func=AF.Sin), reads=[R], writes=[R])

    sin_of(sn, 0.0)
    sin_of(cs, PI / 2)
    out = dict(rho=rho, sn=sn, cs=cs, R=R)
    if need_f:
        lbr, lbi, fr, fi = mk(), mk(), mk(), mk()
        tt = lambda o, a, b, op: K.op("dve", lambda e: e.tensor_tensor(out=o, in0=a, in1=b, op=op), reads=[R], writes=[R])
        tt(lbr, rho, cs, ALU.mult)
        K.op("dve", lambda e: e.tensor_scalar(out=lbr, in0=lbr, scalar1=-1.0, scalar2=None, op0=ALU.add), reads=[R], writes=[R])
        tt(lbi, rho, sn, ALU.mult)
        tt(fr, lbr, are, ALU.mult)
        tt(y, lbi, aim, ALU.mult)
        tt(fr, fr, y, ALU.add)
        tt(fi, lbi, are, ALU.mult)
        tt(y, lbr, aim, ALU.mult)
        tt(fi, fi, y, ALU.subtract)
        tt(kf, are, are, ALU.mult)
        tt(m, aim, aim, ALU.mult)
        tt(kf, kf, m, ALU.add)
        K.op("dve", lambda e: e.reciprocal(out=kf, in_=kf), reads=[R], writes=[R])
        tt(fr, fr, kf, ALU.mult)
        tt(fi, fi, kf, ALU.mult)
        out.update(fr=fr, fi=fi)
    return out


def s5_setup(X):
    K, nc, I = X.K, X.nc, X.I
    arena_reset(X, 4096)
    pc = aalloc(X, [128, 3, 32], F32)
    rpc = Res()
    K.dma("sp", "su3", pc, I["s5col"], writes=[rpc])
    P = s5_params(X, pc[:, 0, :], pc[:, 1, :], pc[:, 2, :], [128, 32], False, rpc)
    K.op("dve", lambda e: e.tensor_copy(out=X.rho[:, :], in_=P["rho"]), reads=[P["R"], rpc], writes=[X.rconst])
    tab = aalloc(X, [128, 3, 32, LB], F32)
    rt = P["R"]
    K.op("dve", lambda e: e.tensor_copy(out=tab[:, 0, :, 0], in_=P["cs"]), reads=[rt], writes=[rt])
    K.op("dve", lambda e: e.tensor_copy(out=tab[:, 1, :, 0], in_=P["sn"]), reads=[rt], writes=[rt])
    ta = aalloc(X, [128, 32, 32], F32)
    tb = aalloc(X, [128, 32, 32], F32)
    j = 1
    while j < LB:
        cr = tab[:, 0, :, j - 1:j].to_broadcast([128, 32, j])
        ci = tab[:, 1, :, j - 1:j].to_broadcast([128, 32, j])
        pr, pi = tab[:, 0, :, 0:j], tab[:, 1, :, 0:j]
        tt = lambda o, a, b, op: K.op("dve", lambda e: e.tensor_tensor(out=o, in0=a, in1=b, op=op), reads=[rt], writes=[rt])
        tt(ta[:, :, 0:j], pr, cr, ALU.mult)
        tt(tb[:, :, 0:j], pi, ci, ALU.mult)
        tt(tab[:, 0, :, j:2 * j], ta[:, :, 0:j], tb[:, :, 0:j], ALU.subtract)
        tt(ta[:, :, 0:j], pr, ci, ALU.mult)
        tt(tb[:, :, 0:j], pi, cr, ALU.mult)
        tt(tab[:, 1, :, j:2 * j], ta[:, :, 0:j], tb[:, :, 0:j], ALU.add)
        j *= 2
    K.op("dve", lambda e: e.tensor_scalar(out=tab[:, 2, :, :], in0=tab[:, 1, :, :], scalar1=-1.0, scalar2=None, op0=ALU.mult), reads=[rt], writes=[rt])
    for a in range(3):
        K.op("dve", lambda e, a=a: e.tensor_copy(out=X.pend[:, a, :], in_=tab[:, a, :, LB - 1]), reads=[rt], writes=[X.rconst])
        K.op("dve", lambda e, a=a: e.tensor_copy(out=X.pend32[:, a, :], in_=tab[:, a, :, 31]), reads=[rt], writes=[X.rconst])
    X.ptab_w = K.dma("sp", "ptab", X.ptab, tab.rearrange("p a c l -> p (a c l)"), reads=[rt])
    X.arena_dmas = [X.ptab_w]
    fence(X)
    arena_reset(X, 4096)
    Bt = aalloc(X, [128, 2, 8, 64], F32)
    Ct = aalloc(X, [128, 2, 32, 16], F32)
    MB = aalloc(X, [128, 4, 2], F32)
    MC = aalloc(X, [128, 2, 4, 8], F32)
    rB = Res()
    K.dma("sp", "su4", Bt, I["Bt"].rearrange("a p f q -> p a f q"), writes=[rB], extra=X.fence_ops)
    K.dma("sp", "su5", Ct, I["Ct"].rearrange("a p c h -> p a c h"), writes=[rB], extra=X.fence_ops)
    K.dma("sp", "su6", MB, I["MB"], writes=[rB], extra=X.fence_ops)
    K.dma("sp", "su7", MC, I["MC"], writes=[rB], extra=X.fence_ops)
    base = X.apos
    X.ws5_w = []
    for slab_i in range(4):
        if slab_i > 0:
            X.arena_dmas.append(X.ws5_w[-1])
            fence(X)
        arena_reset(X, base)
        row = aalloc(X, [128, 3, 1024], F32)
        rrow = Res()
        for a in range(3):
            K.dma("sp", "su8", row[:, a, :], I["s5row"][a, slab_i * 1024:(slab_i + 1) * 1024].partition_broadcast(128), writes=[rrow],
                  extra=X.fence_ops)
        R0 = Res()
        K.op("dve", lambda e, row=row: e.tensor_copy(out=row[:, 2, 0:1], in_=row[:, 2, 0:1]), reads=[rrow, rB], writes=[R0])
        Pm = s5_params(X, row[:, 0, :], row[:, 1, :], row[:, 2, :], [128, 1024], True, R0)
        R = Pm["R"]
        fr = Pm["fr"].rearrange("p (c g q) -> p c g q", g=2, q=64)
        fi = Pm["fi"].rearrange("p (c g q) -> p c g q", g=2, q=64)
        stg = aalloc(X, [128, 8, 4, 128], BF16)
        t1 = aalloc(X, [128, 4, 2, 64], F32)
        t2 = aalloc(X, [128, 4, 2, 64], F32)
        rs = Res()
        for half in range(2):
            ccl0 = half * 4
            cc0 = slab_i * 8 + ccl0
            fc = cc0 // 4
            fre = fr[:, ccl0:ccl0 + 4, :, :]
            fim = fi[:, ccl0:ccl0 + 4, :, :]
            bre = Bt[:, 0, fc, :].unsqueeze(1).unsqueeze(1).to_broadcast([128, 4, 2, 64])
            bim = Bt[:, 1, fc, :].unsqueeze(1).unsqueeze(1).to_broadcast([128, 4, 2, 64])
            mb = MB[:, :, :].unsqueeze(3).to_broadcast([128, 4, 2, 64])
            tt = lambda o, a, b, op, R=R, rs=rs: K.op("dve", lambda e: e.tensor_tensor(out=o, in0=a, in1=b, op=op), reads=[R, rB, rs], writes=[rs])
            dst = lambda arr, half=half, stg=stg: stg[:, half * 4:half * 4 + 4, arr, :].rearrange("p c (g q) -> p c g q", g=2)
            tt(t1, fre, bre, ALU.mult)
            tt(t2, fim, bim, ALU.mult)
            tt(t1, t1, t2, ALU.subtract)
            tt(dst(0), t1, mb, ALU.mult)
            tt(t1, fre, bim, ALU.mult)
            tt(t2, fim, bre, ALU.mult)
            tt(t1, t1, t2, ALU.add)
            tt(dst(1), t1, mb, ALU.mult)
            for arr, (a, sgn) in enumerate([(0, 0), (1, 1)]):
                cin = Ct[:, a, cc0:cc0 + 4, :].unsqueeze(2).to_broadcast([128, 4, 8, 16])
                mc = MC[:, sgn, :, :].unsqueeze(3).to_broadcast([128, 4, 8, 16])
                d = stg[:, half * 4:half * 4 + 4, 2 + arr, :].rearrange("p c (g h) -> p c g h", g=8)
                tt(d, cin, mc, ALU.mult)
        X.ws5_w.append(K.dma("sp", "ws5", X.ws5[slab_i], stg.rearrange("p c a m -> p (c a m)"), reads=[rs]))
    X.arena_dmas.append(X.ws5_w[-1])
    fence(X)


def even_mixer(X, gi, t0, n, typ, last_prompt):
    K, nc, I, O = X.K, X.nc, X.I, X.O
    nt = n // 128
    nch = 2 if typ == 0 else 4
    Lc = 128 // nch
    fence(X)
    arena_reset(X)
    catT = aalloc(X, [128, 16, n], BF16); rcat = Res("cat")
    X.cat_keep = X.apos
    hT = aalloc(X, [128, 16, n], BF16); rh = Res("h")
    auT = aalloc(X, [128, 8, n], BF16); rau = Res()
    avbf = aalloc(X, [128, nt, 1024], BF16); ravbf = Res()
    avg = aalloc(X, [128, 1024], F32); ravg = Res()
    qT = aalloc(X, [128, 4, n], BF16); rq = Res()
    k32 = aalloc(X, [128, nt, 512], F32); rk = Res()
    vbf = aalloc(X, [128, nt, 1024], BF16); rv = Res()
    sr = aalloc(X, [128, 8, n], BF16); rsr = Res()
    glr = aalloc(X, [128, n], BF16); rglr = Res()
    sp = X.tmpx[0][:, :]; rsp = X.rtmpx[0]
    exr = X.tmpx[1][:, :]; rexr = X.rtmpx[1]
    kd = [aalloc(X, [128, 512], BF16) for _ in range(2)]; rkd = [Res(), Res()]
    osq = aalloc(X, [128, 8, 128], BF16); rosq = Res()
    otmp = aalloc(X, [128, 8, 128], F32); rot = Res()
    bst = aalloc(X, [128, 8, 128], F32)
    vgt = aalloc(X, [128, 1024], F32)
    gbt = aalloc(X, [128, 512], F32)
    st = aalloc(X, [128, 32], F32); rst = Res()
    dec = aalloc(X, [128, 4, 4], F32); rdec = Res()
    rnrm2 = X.rnrm
    nrm2 = X.nrm
    rbt = Res()
    fo = X.fence_ops
    K.dma("sp", "e0", bst.rearrange("p h i -> p (h i)"), I["bs_p" if typ == 0 else "bs_s"].partition_broadcast(128), writes=[rbt], extra=fo)
    K.dma("sp", "e1", vgt, I["v_gain"].partition_broadcast(128), writes=[rbt], extra=fo)
    K.dma("sp", "e2", gbt, I["gbias"].partition_broadcast(128), writes=[rbt], extra=fo)

    pre_norm(X, 0, 0, n, hT, rh)
    W = I["w_in_even"].rearrange("(k p) c -> p k c", p=128)
    MB_ = [0, 1, 2, 3, 4, 5]

    def fm_slab(c0, ncols, evac):
        slab, rs = wslab(X, W[:, :, c0:c0 + ncols], 16, ncols)
        for j in range(max(1, ncols // 128)):
            b = bank(X, "main", MB_)
            if ncols >= 128:
                mm_fm(X, b, slab, rs, j, hT, rh, n, 16)
            else:
                for k in range(16):
                    K.op("pe", lambda e, k=k, b=b: e.matmul(X.psum[0:ncols, b, 0:n], lhsT=slab[:, k, 0:ncols], rhs=hT[:, k, 0:n], start=(k == 0), stop=(k == 15)),
                         reads=[rs, rh], writes=[X.rps[b]])
            evac(b, c0 // 128 + j if ncols >= 128 else 0)

    def tm_group(b, slab, rs, tt):
        for k in range(16):
            K.op("pe", lambda e, k=k: e.matmul(X.psum[:, b, 0:512], lhsT=hT[:, k, tt * 128:(tt + 1) * 128], rhs=slab[:, k, 0:512], start=(k == 0), stop=(k == 15)),
                 reads=[rs, rh], writes=[X.rps[b]])

    for s in range(4):
        fm_slab(s * 256, 256, lambda b, c: K.op("act", lambda e: e.activation(out=auT[:, c, 0:n], in_=X.psum[:, b, 0:n], func=AF.Gelu_apprx_tanh), reads=[X.rps[b]], writes=[rau]))
    sl0, rs0 = wslab(X, W[:, :, 1024:1536], 16, 512, need=16384)
    sl1, rs1 = wslab(X, W[:, :, 1536:2048], 16, 512)
    for tt in range(nt):
        for hh, (sl, rs) in enumerate([(sl0, rs0), (sl1, rs1)]):
            b = bank(X, "main", MB_)
            tm_group(b, sl, rs, tt)
            K.op("act", lambda e, b=b, hh=hh: e.activation(out=avg[:, hh * 512:(hh + 1) * 512], in_=X.psum[:, b, 0:512], func=AF.Gelu_apprx_tanh), reads=[X.rps[b]], writes=[ravg])
        for hh in range(2):
            K.op("dve", lambda e, hh=hh: e.bn_stats(out=st[:, hh * 6:(hh + 1) * 6], in_=avg[:, hh * 512:(hh + 1) * 512]), reads=[ravg], writes=[rst])
        K.op("dve", lambda e: e.bn_aggr(out=st[:, 12:14], in_=st[:, 0:12]), reads=[rst], writes=[rst])
        K.op("act", lambda e: e.activation(out=st[:, 14:15], in_=st[:, 13:14], func=AF.Sqrt, bias=X.epsc[:, 0:1]), reads=[rst, X.rconst], writes=[rst])
        K.op("dve", lambda e: e.reciprocal(out=st[:, 15:16], in_=st[:, 14:15]), reads=[rst], writes=[rst])
        K.op("dve", lambda e: e.tensor_scalar(out=avg[:, :], in0=avg[:, :], scalar1=st[:, 12:13], scalar2=st[:, 15:16], op0=ALU.subtract, op1=ALU.mult), reads=[rst, ravg], writes=[ravg])
        K.op("dve", lambda e: e.tensor_tensor(out=avg[:, :], in0=avg[:, :], in1=vgt, op=ALU.mult), reads=[ravg, rbt], writes=[ravg])
        K.op("act", lambda e, tt=tt: e.copy(out=avbf[:, tt, :], in_=avg[:, :]), reads=[ravg], writes=[ravbf])
        if typ == 1:
            X.arena_dmas.append(K.dma("sp", "avo", O["av_o"], avg[:, :], reads=[ravg], final=True))
    for s in range(2):
      fm_slab(2048 + s * 256, 256, lambda b, c: K.op("act", lambda e: e.activation(out=qT[:, c - 16, 0:n], in_=X.psum[:, b, 0:n], func=AF.Copy, scale=128.0 ** -0.5), reads=[X.rps[b]], writes=[rq]))
    sl, rs = wslab(X, W[:, :, 2560:3072], 16, 512)
    for tt in range(nt):
        b = bank(X, "main", MB_)
        tm_group(b, sl, rs, tt)
        K.op("act", lambda e, b=b, tt=tt: e.copy(out=k32[:, tt, :], in_=X.psum[:, b, 0:512]), reads=[X.rps[b]], writes=[rk])
    for hh in range(2):
        sl, rs = wslab(X, W[:, :, 3072 + hh * 512:3584 + hh * 512], 16, 512)
        for tt in range(nt):
            b = bank(X, "main", MB_)
            tm_group(b, sl, rs, tt)
            K.op("act", lambda e, b=b, tt=tt, hh=hh: e.copy(out=vbf[:, tt, hh * 512:(hh + 1) * 512], in_=X.psum[:, b, 0:512]), reads=[X.rps[b]], writes=[rv])
    for s in range(4):
        fm_slab(4096 + s * 256, 256, lambda b, c: K.op("act", lambda e: e.activation(out=sr[:, c - 32, 0:n], in_=X.psum[:, b, 0:n], func=AF.Silu), reads=[X.rps[b]], writes=[rsr]))
    fm_slab(5120, 16, lambda b, c: K.op("act", lambda e: e.copy(out=glr[0:16, 0:n], in_=X.psum[0:16, b, 0:n]), reads=[X.rps[b]], writes=[rglr]))

    for tt in range(nt):
        for hp in range(2):
            b = bank(X, "main", MB_)
            for h4 in range(4):
                h = hp * 4 + h4
                K.op("pe", lambda e, b=b, h=h, h4=h4, tt=tt: e.matmul(X.psum[:, b, h4 * 128:(h4 + 1) * 128], lhsT=avbf[:, tt, h * 128:(h + 1) * 128], rhs=X.wmT[typ][:, h, :], start=True, stop=True),
                     reads=[ravbf, X.rwmT], writes=[X.rps[b]])
            K.op("dve", lambda e, b=b, hp=hp: e.tensor_tensor(out=otmp[:, 0:4, :], in0=X.psum[:, b, :].rearrange("p (a i) -> p a i", i=128), in1=bst[:, hp * 4:hp * 4 + 4, :], op=ALU.add),
                 reads=[X.rps[b], rbt], writes=[rot])
            K.op("dve", lambda e, hp=hp, tt=tt: e.tensor_tensor(out=catT[:, hp * 4:hp * 4 + 4, tt * 128:(tt + 1) * 128], in0=otmp[:, 0:4, :], in1=auT[:, hp * 4:hp * 4 + 4, tt * 128:(tt + 1) * 128], op=ALU.mult),
                 reads=[rot, rau], writes=[rcat])

    MBg = [0, 1, 2, 3]
    mrev = X.cmask[:, typ, 0:128]
    cind = X.cmask[:, typ, 128:132]
    cmk = X.cmask[:, typ, 132:136]
    for tt in range(nt):
        tsl = slice(tt * 128, (tt + 1) * 128)
        bz = bank(X, "gla", MBg)
        K.op("pe", lambda e, bz=bz, tsl=tsl: e.matmul(X.psum[:, bz, 0:512], lhsT=glr[0:16, tsl], rhs=X.wgate[0:16, :], start=True, stop=True),
             reads=[rglr, X.rconst], writes=[X.rps[bz]])
        K.op("dve", lambda e, bz=bz: e.tensor_tensor(out=sp, in0=X.psum[:, bz, 0:512], in1=gbt, op=ALU.add), reads=[X.rps[bz], rbt], writes=[rsp])
        K.op("act", lambda e: e.activation(out=sp, in_=sp, func=AF.Exp, scale=-1.0), reads=[rsp], writes=[rsp])
        K.op("act", lambda e: e.activation(out=sp, in_=sp, func=AF.Ln, bias=X.epsc[:, 1:2]), reads=[rsp, X.rconst], writes=[rsp])
        br = bank(X, "gla", MBg)
        K.op("pe", lambda e, br=br: e.matmul(X.psum[:, br, 0:512], lhsT=mrev, rhs=sp, start=True, stop=True), reads=[rsp, X.rconst], writes=[X.rps[br]])
        K.op("act", lambda e, br=br: e.activation(out=exr, in_=X.psum[:, br, 0:512], func=AF.Exp), reads=[X.rps[br]], writes=[rexr])
        bt_ = bank(X, "gla", MBg)
        for h in range(4):
            K.op("pe", lambda e, bt_=bt_, h=h: e.matmul(X.psum[:, bt_, h * 4:h * 4 + 4], lhsT=sp[:, h * 128:(h + 1) * 128], rhs=cind, start=True, stop=True),
                 reads=[rsp, X.rconst], writes=[X.rps[bt_]])
        K.op("act", lambda e, bt_=bt_: e.activation(out=dec.rearrange("p a b -> p (a b)"), in_=X.psum[:, bt_, 0:16], func=AF.Exp), reads=[X.rps[bt_]], writes=[rdec])
        bo = [4, 5]
        for c in range(nch):
            i = c % 2
            K.op("dve", lambda e, c=c, i=i, tt=tt: e.scalar_tensor_tensor(out=kd[i], in0=exr, scalar=cmk[:, c:c + 1], in1=k32[:, tt, :], op0=ALU.mult, op1=ALU.mult),
                 reads=[rexr, rk, X.rconst], writes=[rkd[i]])
            if typ == 1:
                seq = tt * 4 + c
                K.dma("sp", "gls", X.S[:, :], I["gla_s"][seq], writes=[X.rS])
            bs_ = [bank(X, "gla", MBg), bank(X, "gla", MBg)]
            for h in range(4):
                K.op("pe", lambda e, h=h, i=i, tt=tt, bs_=bs_: e.matmul(X.psum[:, bs_[h // 2], (h % 2) * 256:(h % 2 + 1) * 256], lhsT=kd[i][:, h * 128:(h + 1) * 128], rhs=vbf[:, tt, h * 256:(h + 1) * 256], start=True, stop=True),
                     reads=[rkd[i], rv], writes=[X.rps[bs_[h // 2]]])
            for h in range(4):
                K.op("dve", lambda e, h=h, c=c, bs_=bs_: e.scalar_tensor_tensor(out=X.S[:, h * 256:(h + 1) * 256], in0=X.S[:, h * 256:(h + 1) * 256], scalar=dec[:, h, c:c + 1], in1=X.psum[:, bs_[h // 2], (h % 2) * 256:(h % 2 + 1) * 256], op0=ALU.mult, op1=ALU.add),
                     reads=[X.rS, rdec, X.rps[bs_[h // 2]]], writes=[X.rS])
            K.op("act", lambda e: e.copy(out=X.Sbf[:, :], in_=X.S[:, :]), reads=[X.rS], writes=[X.rSbf])
            if typ == 1:
                K.dma("sp", "glo", O["gla_o"][seq], X.S[:, :], reads=[X.rS], final=True)
            for e8 in range(8):
                h = e8 // 2
                K.op("pe", lambda e, e8=e8, h=h, c=c, tt=tt: e.matmul(X.psum[:, bo[e8 // 4], (e8 % 4) * 128 + c * Lc:(e8 % 4) * 128 + (c + 1) * Lc], lhsT=X.Sbf[:, e8 * 128:(e8 + 1) * 128], rhs=qT[:, h, tt * 128 + c * Lc:tt * 128 + (c + 1) * Lc], start=True, stop=True),
                     reads=[X.rSbf, rq], writes=[X.rps[bo[e8 // 4]]])
        for a in range(2):
            K.op("act", lambda e, a=a: e.activation(out=osq[:, a * 4:(a + 1) * 4, :], in_=X.psum[:, bo[a], :].rearrange("p (a i) -> p a i", i=128), func=AF.Square), reads=[X.rps[bo[a]]], writes=[rosq])
        bn = 6
        for h in range(4):
            for u in range(2):
                K.op("pe", lambda e, h=h, u=u, bn=bn: e.matmul(X.psum[:, bn, h * 128:(h + 1) * 128], lhsT=X.ones[:, :], rhs=osq[:, 2 * h + u, :], start=(u == 0), stop=(u == 1)),
                     reads=[rosq, X.rones], writes=[X.rps[bn]])
        K.op("act", lambda e, bn=bn: e.activation(out=nrm2[:, 1, :], in_=X.psum[:, bn, 0:512], func=AF.Sqrt, scale=1.0 / 256, bias=X.epsc[:, 0:1]), reads=[X.rps[bn], X.rconst], writes=[rnrm2])
        K.op("dve", lambda e: e.reciprocal(out=nrm2[:, 0, :], in_=nrm2[:, 1, :]), reads=[rnrm2], writes=[rnrm2])
        for e8 in range(8):
            h = e8 // 2
            K.op("dve", lambda e, e8=e8, h=h: e.scalar_tensor_tensor(out=otmp[:, e8, :], in0=X.psum[:, bo[e8 // 4], (e8 % 4) * 128:(e8 % 4 + 1) * 128], scalar=col(X, C_OG + e8), in1=nrm2[:, 0, h * 128:(h + 1) * 128], op0=ALU.mult, op1=ALU.mult),
                 reads=[X.rps[bo[e8 // 4]], rnrm2, X.rcols], writes=[rot])
        K.op("dve", lambda e, tsl=tsl: e.tensor_tensor(out=catT[:, 8:16, tsl], in0=otmp[:, :, :], in1=sr[:, :, tsl], op=ALU.mult), reads=[rot, rsr], writes=[rcat])
    if last_prompt:
        K.dma("sp", "glp", O["gla_p"], X.S[:, :], reads=[X.rS], final=True)
    fence(X)
    arena_reset(X, X.cat_keep)
    fT = aalloc(X, [128, 16, n], F32); rf = Res("f")
    out_proj(X, I["w_out_even"], n, catT, rcat, fT, rf)
    post_residual(X, 0, 1, n, fT, rf)


def conv_ffn(X, layer, gi, n, typ, last_prompt):
    K, nc, I, O = X.K, X.nc, X.I, X.O
    nseg = 1 if typ == 0 else 4
    L = n // nseg
    fence(X)
    arena_reset(X)
    actT = aalloc(X, [128, NFC, n], BF16); ract = Res("act")
    keep = X.apos
    hT = aalloc(X, [128, 16, n], BF16); rh = Res("h")
    gb = [aalloc(X, [128, nseg, L + 2], F32) for _ in range(2)]; rgb = [Res(), Res()]
    tc_ = [aalloc(X, [128, nseg, L], F32) for _ in range(2)]; rtc = [Res(), Res()]
    fo = X.fence_ops
    if typ == 1:
        fst = aalloc(X, [128, NFC, 4, 2], F32); rfst = Res()
        K.dma("sp", "f0", fst, I["ffn_s"][:, layer], writes=[rfst], extra=fo)
    pre_norm(X, layer, 2, n, hT, rh)
    W = I["w_up"][layer].rearrange("(k p) c -> p k c", p=128)
    cw = lambda t, f: col(X, C_CW + (layer * 3 + t) * NFC + f)
    cb = lambda f: col(X, C_CB + layer * NFC + f)
    for s in range(22):
        sg, rsg = wslab(X, W[:, :, s * 256:(s + 1) * 256], 16, 256, need=8192)
        sv, rsv = wslab(X, W[:, :, DFF + s * 256:DFF + (s + 1) * 256], 16, 256)
        for j in range(2):
            f = s * 2 + j
            i = f % 2
            bg = bank(X, "ffg", [0, 2, 4])
            bv = bg + 1
            mm_fm(X, bg, sg, rsg, j, hT, rh, n, 16)
            mm_fm(X, bv, sv, rsv, j, hT, rh, n, 16)
            g3 = gb[i]
            K.op("act", lambda e, bg=bg, g3=g3: e.copy(out=g3[:, :, 2:L + 2], in_=X.psum[:, bg, 0:n].rearrange("p (s l) -> p s l", l=L)), reads=[X.rps[bg]], writes=[rgb[i]])
            if typ == 0:
                K.op("pool", lambda e, g3=g3, f=f: e.tensor_copy(out=g3[:, 0, 0:2], in_=X.chalo[:, layer, f, :]), reads=[X.rchalo[layer]], writes=[rgb[i]])
                K.op("pool", lambda e, g3=g3, f=f: e.tensor_copy(out=X.chalo[:, layer, f, :], in_=g3[:, 0, L:L + 2]), reads=[rgb[i]], writes=[X.rchalo[layer]])
            else:
                K.op("pool", lambda e, g3=g3, f=f: e.tensor_copy(out=g3[:, :, 0:2], in_=fst[:, f, :, :]), reads=[rfst], writes=[rgb[i]])
                K.op("pool", lambda e, g3=g3, f=f: e.tensor_copy(out=X.ffno[:, layer, f, :, :], in_=g3[:, :, L:L + 2]), reads=[rgb[i]], writes=[X.rffno])
            t3 = tc_[i]
            K.op("act", lambda e, g3=g3, t3=t3, f=f: e.activation(out=t3, in_=g3[:, :, 0:L], func=AF.Identity, scale=cw(0, f), bias=cb(f)), reads=[rgb[i], X.rcols], writes=[rtc[i]])
            K.op("dve", lambda e, g3=g3, t3=t3, f=f: e.scalar_tensor_tensor(out=t3, in0=g3[:, :, 1:L + 1], scalar=cw(1, f), in1=t3, op0=ALU.mult, op1=ALU.add), reads=[rgb[i], rtc[i], X.rcols], writes=[rtc[i]])
            K.op("dve", lambda e, g3=g3, t3=t3, f=f: e.scalar_tensor_tensor(out=t3, in0=g3[:, :, 2:L + 2], scalar=cw(2, f), in1=t3, op0=ALU.mult, op1=ALU.add), reads=[rgb[i], rtc[i], X.rcols], writes=[rtc[i]])
            K.op("act", lambda e, t3=t3: e.activation(out=t3, in_=t3, func=AF.Gelu_apprx_tanh), reads=[rtc[i]], writes=[rtc[i]])
            K.op("dve", lambda e, t3=t3, f=f, bv=bv: e.tensor_tensor(out=actT[:, f, 0:n], in0=t3.rearrange("p s l -> p (s l)"), in1=X.psum[:, bv, 0:n], op=ALU.mult), reads=[rtc[i], X.rps[bv]], writes=[ract])
    if typ == 0 and last_prompt:
        K.dma("sp", "ffp", O["ffn_p"][:, layer], X.chalo[:, layer, :, :], reads=[X.rchalo[layer]], final=True)
    if typ == 1:
        K.dma("sp", "ffo", O["ffn_o"][:, layer], X.ffno[:, layer], reads=[X.rffno], final=True)
    fence(X)
    arena_reset(X, keep)
    fT = aalloc(X, [128, 16, n], F32); rf = Res("f")
    out_proj(X, I["w_down"][layer], n, actT, ract, fT, rf, kc=NFC)
    post_residual(X, layer, 3, n, fT, rf)


def odd_mixer(X, gi, t0, n, typ, last_prompt):
    K, nc, I, O = X.K, X.nc, X.I, X.O
    nseg = 1 if typ == 0 else 4
    L = n // nseg
    Lb = LB if typ == 0 else 32
    nblk = n // Lb
    Wd = 15 + L
    fence(X)
    arena_reset(X)
    catT = aalloc(X, [128, 16, n], BF16); rcat = Res("cat")
    keep = X.apos
    hT = aalloc(X, [128, 16, n], BF16); rh = Res("h")
    cbf = [aalloc(X, [128, nseg, Wd], F32) for _ in range(2)]; rcb = [Res(), Res()]
    pt = [aalloc(X, [128, nseg, Wd], F32) for _ in range(2)]; rpt = [Res(), Res()]
    dlt = aalloc(X, [128, 8, n], BF16); rdl = Res()
    uT = aalloc(X, [128, 8, n], BF16); ru = Res()
    tab = aalloc(X, [128, 3, 32, LB], F32); rtab = Res()
    tq = [X.tmpx[0][:, 0:n], X.tmpx[1][:, 0:n]] + [aalloc(X, [128, n], F32) for _ in range(2)]; rtq = [X.rtmpx[0], X.rtmpx[1], Res(), Res()]
    wz = [aalloc(X, [128, n], F32) for _ in range(4)]; rwz = [Res() for _ in range(4)]
    pr = [aalloc(X, [128, n], BF16) for _ in range(4)]; rpr = [Res() for _ in range(4)]
    yg = aalloc(X, [128, 8, n], BF16); ryg = Res()
    sgm = aalloc(X, [128, n], F32); rsg = Res()
    se = aalloc(X, [128, 8], F32); rse = Res()
    fo = X.fence_ops
    K.dma("sp", "o0", tab.rearrange("p a c l -> p (a c l)"), X.ptab, writes=[rtab], extra=fo + [X.ptab_w])
    if typ == 1:
        pst = aalloc(X, [128, 8, 4, 15], F32); rpst = Res()
        K.dma("sp", "o1", pst, I["pool_s"], writes=[rpst], extra=fo)
        s5i = aalloc(X, [128, 2, 4, 32], F32); rs5i = Res()
        K.dma("sp", "o2", s5i[:, 0], I["s5re_s"], writes=[rs5i], extra=fo)
        K.dma("sp", "o2", s5i[:, 1], I["s5im_s"], writes=[rs5i], extra=fo)
        s5o = aalloc(X, [128, 2, 4, 32], F32); rs5o = Res()
        pso = aalloc(X, [128, 8, 4, 15], F32); rpso = Res()
    pre_norm(X, 1, 0, n, hT, rh)
    W = I["w_in_odd"].rearrange("(k p) c -> p k c", p=128)
    MB_ = [0, 1, 2, 3]
    for s in range(4):
        slab, rs = wslab(X, W[:, :, s * 256:(s + 1) * 256], 16, 256)
        for j4 in range(2):
            j = s * 2 + j4
            g = j // 2
            b = bank(X, "odd", MB_)
            mm_fm(X, b, slab, rs, j4, hT, rh, n, 16)
            i = j % 2
            cb3 = cbf[i]
            K.op("act", lambda e, b=b, cb3=cb3: e.copy(out=cb3[:, :, 15:Wd], in_=X.psum[:, b, 0:n].rearrange("p (s l) -> p s l", l=L)), reads=[X.rps[b]], writes=[rcb[i]])
            if typ == 0:
                K.op("pool", lambda e, cb3=cb3, j=j: e.tensor_copy(out=cb3[:, 0, 0:15], in_=X.phalo[:, j, :]), reads=[X.rphalo], writes=[rcb[i]])
                K.op("pool", lambda e, cb3=cb3, j=j: e.tensor_copy(out=X.phalo[:, j, :], in_=cb3[:, 0, L:L + 15]), reads=[rcb[i]], writes=[X.rphalo])
            else:
                K.op("pool", lambda e, cb3=cb3, j=j: e.tensor_copy(out=cb3[:, :, 0:15], in_=pst[:, j, :, :]), reads=[rpst], writes=[rcb[i]])
                K.op("pool", lambda e, cb3=cb3, j=j: e.tensor_copy(out=pso[:, j, :, :], in_=cb3[:, :, L:L + 15]), reads=[rcb[i]], writes=[rpso])
            src, rsrc = cb3, rcb[i]
            lo = 0
            for step in range(g + 1):
                d = 1 << step
                dst, rdst = pt[step % 2], rpt[step % 2]
                K.op("pool", lambda e, src=src, dst=dst, d=d, lo=lo: e.tensor_tensor(out=dst[:, :, lo + d:Wd], in0=src[:, :, lo + d:Wd], in1=src[:, :, lo:Wd - d], op=ALU.add),
                     reads=[rsrc], writes=[rdst])
                src, rsrc = dst, rdst
                lo += d
            w = 2 << g
            if typ == 0 and t0 == 0:
                K.op("dve", lambda e, src=src, g=g: e.tensor_tensor(out=src[:, 0, 15:31], in0=src[:, 0, 15:31], in1=X.cmaskc[:, g, :], op=ALU.mult), reads=[rsrc, X.rconst], writes=[rsrc])
            K.op("dve", lambda e, src=src, cb3=cb3, j=j, w=w: e.scalar_tensor_tensor(out=dlt[:, j, 0:n].rearrange("p (s l) -> p s l", l=L), in0=src[:, :, 15:Wd], scalar=1.0 / w, in1=cb3[:, :, 15:Wd], op0=ALU.mult, op1=ALU.subtract),
                 reads=[rsrc, rcb[i]], writes=[rdl])
    cms, rcm = wslab(X, I["c_map"].rearrange("(j p) d -> p j d", p=128), 8, 256)
    for j in range(8):
        g, hf = j // 2, j % 2
        b = bank(X, "odd", MB_)
        for kk in range(2):
            K.op("pe", lambda e, b=b, g=g, kk=kk, hf=hf: e.matmul(X.psum[:, b, 0:n], lhsT=cms[:, g * 2 + kk, hf * 128:(hf + 1) * 128], rhs=dlt[:, g * 2 + kk, 0:n], start=(kk == 0), stop=(kk == 1)),
                 reads=[rcm, rdl], writes=[X.rps[b]])
        K.op("act", lambda e, b=b, j=j: e.activation(out=catT[:, j, 0:n], in_=X.psum[:, b, 0:n], func=AF.Identity, scale=col(X, C_CS + j)), reads=[X.rps[b], X.rcols], writes=[rcat])
    if typ == 0 and last_prompt:
        K.dma("sp", "plp", O["pool_p"], X.phalo[:, :, :], reads=[X.rphalo], final=True)
    if typ == 1:
        X.arena_dmas.append(K.dma("sp", "plo", O["pool_o"], pso, reads=[rpso], final=True))
    for s in range(4):
        slab, rs = wslab(X, W[:, :, 1024 + s * 256:1024 + (s + 1) * 256], 16, 256)
        for j4 in range(2):
            j = s * 2 + j4
            b = bank(X, "odd", MB_)
            mm_fm(X, b, slab, rs, j4, hT, rh, n, 16)
            K.op("act", lambda e, b=b, j=j: e.copy(out=uT[:, j, 0:n], in_=X.psum[:, b, 0:n]), reads=[X.rps[b]], writes=[ru])
    pend = X.pend if typ == 0 else X.pend32
    v3 = lambda ap: ap.rearrange("p (b l) -> p b l", l=Lb)
    for sl in range(4):
        ws, rws = wslab(X, X.ws5[sl].rearrange("p (k c) -> p k c", c=128), 32, 128)
        if sl == 0:
            pass
        for q8 in range(8):
            cc = sl * 8 + q8
            fc = cc // 4
            by = 4 + (fc % 2)
            bre, bim = (0, 1) if cc % 2 == 0 else (2, 3)
            K.op("pe", lambda e, bre=bre, q8=q8, fc=fc, ws=ws: e.matmul(X.psum[:, bre, 0:n], lhsT=ws[:, q8 * 4 + 0, :], rhs=uT[:, fc, 0:n], start=True, stop=True), reads=[rws, ru], writes=[X.rps[bre]], extra=X.ws5_w)
            K.op("pe", lambda e, bim=bim, q8=q8, fc=fc, ws=ws: e.matmul(X.psum[:, bim, 0:n], lhsT=ws[:, q8 * 4 + 1, :], rhs=uT[:, fc, 0:n], start=True, stop=True), reads=[rws, ru], writes=[X.rps[bim]])
            Pre = tab[:, 0, cc, 0:Lb].unsqueeze(1).to_broadcast([128, nblk, Lb])
            Pim = tab[:, 1, cc, 0:Lb].unsqueeze(1).to_broadcast([128, nblk, Lb])
            PimN = tab[:, 2, cc, 0:Lb].unsqueeze(1).to_broadcast([128, nblk, Lb])
            pre3 = v3(X.psum[:, bre, 0:n]); pim3 = v3(X.psum[:, bim, 0:n])
            K.op("dve", lambda e, pre3=pre3, Pre=Pre: e.tensor_tensor(out=v3(tq[0]), in0=pre3, in1=Pre, op=ALU.mult), reads=[X.rps[bre], rtab], writes=[rtq[0]])
            K.op("dve", lambda e, pim3=pim3, Pim=Pim: e.tensor_tensor(out=v3(tq[1]), in0=pim3, in1=Pim, op=ALU.mult), reads=[X.rps[bim], rtab], writes=[rtq[1]])
            K.op("pool", lambda e: e.tensor_tensor(out=wz[0], in0=tq[0], in1=tq[1], op=ALU.add), reads=[rtq[0], rtq[1]], writes=[rwz[0]])
            K.op("dve", lambda e, pim3=pim3, Pre=Pre: e.tensor_tensor(out=v3(tq[2]), in0=pim3, in1=Pre, op=ALU.mult), reads=[X.rps[bim], rtab], writes=[rtq[2]])
            K.op("dve", lambda e, pre3=pre3, Pim=Pim: e.tensor_tensor(out=v3(tq[3]), in0=pre3, in1=Pim, op=ALU.mult), reads=[X.rps[bre], rtab], writes=[rtq[3]])
            K.op("pool", lambda e: e.tensor_tensor(out=wz[1], in0=tq[2], in1=tq[3], op=ALU.subtract), reads=[rtq[2], rtq[3]], writes=[rwz[1]])
            if X.dbg and gi == 0 and cc == X.dbg_cc:
                for ii in range(4):
                    K.dma("sp", "dbgq", X.dbgo["tq"][ii], tq[ii], reads=[rtq[ii]], final=True)
                for ii in range(2):
                    K.dma("sp", "dbgw", X.dbgo["wz"][ii], wz[ii], reads=[rwz[ii]], final=True)
            rho_b = X.rho[:, cc:cc + 1].to_broadcast([128, Lb])
            for bl in range(nblk):
                c0 = bl * Lb
                if typ == 0:
                    ire, iim = X.s5st[:, 0, cc:cc + 1], X.s5st[:, 1, cc:cc + 1]
                    rinit = X.rs5st
                    ore, oim, rout = ire, iim, X.rs5st
                else:
                    ire, iim = s5i[:, 0, bl, cc:cc + 1], s5i[:, 1, bl, cc:cc + 1]
                    rinit = rs5i
                    ore, oim, rout = s5o[:, 0, bl, cc:cc + 1], s5o[:, 1, bl, cc:cc + 1], rs5o
                K.op("dve", lambda e, c0=c0, ire=ire, rho_b=rho_b: e.tensor_tensor_scan(out=wz[2][:, c0:c0 + Lb], data0=rho_b, data1=wz[0][:, c0:c0 + Lb], initial=ire, op0=ALU.mult, op1=ALU.add),
                     reads=[rwz[0], rinit, X.rconst], writes=[rwz[2]])
                K.op("dve", lambda e, c0=c0, iim=iim, rho_b=rho_b: e.tensor_tensor_scan(out=wz[3][:, c0:c0 + Lb], data0=rho_b, data1=wz[1][:, c0:c0 + Lb], initial=iim, op0=ALU.mult, op1=ALU.add),
                     reads=[rwz[1], rinit, X.rconst], writes=[rwz[3]])
                zre_e, zim_e = wz[2][:, c0 + Lb - 1:c0 + Lb], wz[3][:, c0 + Lb - 1:c0 + Lb]
                K.op("dve", lambda e, zre_e=zre_e, cc=cc: e.tensor_scalar(out=se[:, 0:1], in0=zre_e, scalar1=pend[:, 0, cc:cc + 1], scalar2=None, op0=ALU.mult), reads=[rwz[2], X.rconst], writes=[rse])
                K.op("dve", lambda e, zre_e=zre_e, cc=cc: e.tensor_scalar(out=se[:, 1:2], in0=zre_e, scalar1=pend[:, 1, cc:cc + 1], scalar2=None, op0=ALU.mult), reads=[rwz[2], X.rconst], writes=[rse])
                K.op("dve", lambda e, zim_e=zim_e, cc=cc, ore=ore: e.scalar_tensor_tensor(out=ore, in0=zim_e, scalar=pend[:, 2, cc:cc + 1], in1=se[:, 0:1], op0=ALU.mult, op1=ALU.add), reads=[rwz[3], rse, X.rconst], writes=[rout])
                K.op("dve", lambda e, zim_e=zim_e, cc=cc, oim=oim: e.scalar_tensor_tensor(out=oim, in0=zim_e, scalar=pend[:, 0, cc:cc + 1], in1=se[:, 1:2], op0=ALU.mult, op1=ALU.add), reads=[rwz[3], rse, X.rconst], writes=[rout])
            if X.dbg and gi == 0 and cc == X.dbg_cc:
                for ii in range(2, 4):
                    K.dma("sp", "dbgw", X.dbgo["wz"][ii], wz[ii], reads=[rwz[ii]], final=True)
                K.dma("sp", "dbgu", X.dbgo["uT"], uT, reads=[ru], final=True)
            K.op("dve", lambda e, Pre=Pre: e.tensor_tensor(out=v3(pr[0]), in0=v3(wz[2]), in1=Pre, op=ALU.mult), reads=[rwz[2], rtab], writes=[rpr[0]])
            K.op("dve", lambda e, PimN=PimN: e.tensor_tensor(out=v3(pr[1]), in0=v3(wz[3]), in1=PimN, op=ALU.mult), reads=[rwz[3], rtab], writes=[rpr[1]])
            K.op("pool", lambda e, Pim=Pim: e.tensor_tensor(out=v3(pr[2]), in0=v3(wz[2]), in1=Pim, op=ALU.mult), reads=[rwz[2], rtab], writes=[rpr[2]])
            K.op("pool", lambda e, Pre=Pre: e.tensor_tensor(out=v3(pr[3]), in0=v3(wz[3]), in1=Pre, op=ALU.mult), reads=[rwz[3], rtab], writes=[rpr[3]])
            for a in range(4):
                K.op("pe", lambda e, a=a, q8=q8, by=by, cc=cc, ws=ws: e.matmul(X.psum[:, by, 0:n], lhsT=ws[:, q8 * 4 + (2 if a < 2 else 3), :], rhs=pr[a], start=(cc % 4 == 0 and a == 0), stop=(cc % 4 == 3 and a == 3)),
                     reads=[rws, rpr[a]], writes=[X.rps[by]])
            if cc % 4 == 3:
                K.op("dve", lambda e, fc=fc, by=by: e.scalar_tensor_tensor(out=sgm, in0=uT[:, fc, 0:n], scalar=col(X, C_SD + fc), in1=X.psum[:, by, 0:n], op0=ALU.mult, op1=ALU.add), reads=[ru, X.rps[by], X.rcols], writes=[rsg])
                K.op("act", lambda e, fc=fc: e.activation(out=yg[:, fc, 0:n], in_=sgm, func=AF.Gelu_apprx_tanh), reads=[rsg], writes=[ryg])
    if typ == 0 and last_prompt:
        K.dma("sp", "s5p", O["s5_p"], X.s5st[:, :, :], reads=[X.rs5st], final=True)
    if typ == 1:
        X.arena_dmas.append(K.dma("sp", "s5o", O["s5_o"], s5o, reads=[rs5o], final=True))
    WG = I["w_glu"].rearrange("(k p) c -> p k c", p=128)
    for p2 in range(2):
        sa, rsa = wslab(X, WG[:, :, p2 * 512:(p2 + 1) * 512], 8, 512, need=8192)
        sb_, rsb = wslab(X, WG[:, :, 1024 + p2 * 512:1024 + (p2 + 1) * 512], 8, 512)
        for j in range(4):
            ba = bank(X, "glu", [0, 2])
            bb = ba + 1
            mm_fm(X, ba, sa, rsa, j, yg, ryg, n, 8)
            mm_fm(X, bb, sb_, rsb, j, yg, ryg, n, 8)
            K.op("act", lambda e, bb=bb: e.activation(out=sgm, in_=X.psum[:, bb, 0:n], func=AF.Sigmoid), reads=[X.rps[bb]], writes=[rsg])
            K.op("dve", lambda e, ba=ba, p2=p2, j=j: e.tensor_tensor(out=catT[:, 8 + p2 * 4 + j, 0:n], in0=X.psum[:, ba, 0:n], in1=sgm, op=ALU.mult), reads=[X.rps[ba], rsg], writes=[rcat])
    fence(X)
    arena_reset(X, keep)
    fT = aalloc(X, [128, 16, n], F32); rf = Res("f")
    out_proj(X, I["w_out_odd"], n, catT, rcat, fT, rf)
    post_residual(X, 1, 1, n, fT, rf)


def build_all(ngroups_p=8, dbg=False, dbg_cc=0):
    X = build_program()
    K, nc, I, O = X.K, X.nc, X.I, X.O
    X.dbg = dbg
    X.dbg_cc = dbg_cc
    if dbg:
        X.dbgo = {"tq": nc.dram_tensor("dbg_tq", [4, 128, 512], F32, kind="ExternalOutput").ap(),
                  "wz": nc.dram_tensor("dbg_wz", [4, 128, 512], F32, kind="ExternalOutput").ap(),
                  "uT": nc.dram_tensor("dbg_uT", [128, 8, 512], BF16, kind="ExternalOutput").ap()}
    X.ffno = nc.alloc_sbuf_tensor("s_ffno", [128, 2, NFC, 4, 2], F32)
    X.rffno = Res("ffno")
    X.cmaskc = nc.alloc_sbuf_tensor("s_cmaskc", [128, 4, 16], F32)
    cm_in = nc.dram_tensor("poolcorr", [128, 4, 16], F32, kind="ExternalInput").ap()
    setup(X)
    K.dma("sp", "pcorr", X.cmaskc[:, :, :], cm_in, writes=[X.rconst])
    groups = [(g * 512, 512, 0) for g in range(ngroups_p)] + [(NPT, NST, 1)]
    xv = I["xT"].rearrange("(c p) t -> p c t", p=128)
    yv = O["yT"].rearrange("(c p) t -> p c t", p=128)
    for gi, (t0, n, typ) in enumerate(groups):
        lastp = (typ == 0 and gi == ngroups_p - 1)
        K.dma("sp", "xin", X.xT[:, :, 0:n], xv[:, :, t0:t0 + n], writes=[X.rx])
        even_mixer(X, gi, t0, n, typ, lastp)
        conv_ffn(X, 0, gi, n, typ, lastp)
        odd_mixer(X, gi, t0, n, typ, lastp)
        conv_ffn(X, 1, gi, n, typ, lastp)
        K.dma("sp", "yout", yv[:, :, t0:t0 + n], X.xT[:, :, 0:n], reads=[X.rx], final=True)
    K.emit()
    return nc


_NC_CACHE = {}


def _host_consts():
    cm = np.zeros((128, 2, 136), np.float32)
    for typ, Lc in ((0, 64), (1, 32)):
        t = np.arange(128)
        same = (t[:, None] // Lc) == (t[None, :] // Lc)
        cm[:, typ, 0:128] = np.where(same & (t[:, None] > t[None, :]), -1.0 / 16, 0.0)
        for c in range(128 // Lc):
            cm[:, typ, 128 + c] = np.where(t // Lc == c, -1.0 / 16, 0.0)
            cm[:, typ, 132 + c] = np.where(t // Lc == c, 1.0, 0.0)
    MB = np.zeros((128, 4, 2), np.float32)
    for g8 in range(8):
        for q in range(4):
            for gl in range(2):
                if g8 == 2 * q + gl:
                    MB[g8 * 16:(g8 + 1) * 16, q, gl] = 1.0
    MC = np.zeros((128, 2, 4, 8), np.float32)
    for gl in range(2):
        for q in range(4):
            MC[gl * 64:(gl + 1) * 64, 0, q, 2 * q + gl] = 1.0
            MC[gl * 64:(gl + 1) * 64, 1, q, 2 * q + gl] = -1.0
    pc = np.ones((128, 4, 16), np.float32)
    for g in range(4):
        w = 2 << g
        for t in range(16):
            pc[:, g, t] = w / min(t + 1, w)
    return cm, MB, MC, pc


def kernel(**inp):
    f32 = np.float32
    A = {k: np.asarray(v) for k, v in inp.items()}
    if "nc" not in _NC_CACHE:
        _NC_CACHE["nc"] = build_all(8)
    nc = _NC_CACHE["nc"]
    cm, MB, MC, pc = _host_consts()

    def chunkcols(v):
        return np.ascontiguousarray(v.reshape(-1, 128).T)

    cols = np.zeros((128, NCOL), f32)
    norms = [A["norm_mix_pre"], A["norm_mix_post"], A["norm_ffn_pre"], A["norm_ffn_post"]]
    for kind in range(4):
        for l in range(2):
            cols[:, C_NORM + (kind * 2 + l) * 16:C_NORM + (kind * 2 + l + 1) * 16] = chunkcols(norms[kind][l])
    cols[:, C_OG:C_OG + 8] = chunkcols(A["b_out_norm"][0])
    cols[:, C_CS:C_CS + 8] = chunkcols(A["c_scale"][0])
    cols[:, C_SD:C_SD + 8] = chunkcols(A["s5_d"][0])
    for l in range(2):
        for t in range(3):
            cols[:, C_CW + (l * 3 + t) * NFC:C_CW + (l * 3 + t + 1) * NFC] = chunkcols(A["ffn_conv_w"][l, t])
        cols[:, C_CB + l * NFC:C_CB + (l + 1) * NFC] = chunkcols(A["ffn_conv_b"][l])
    ws = A["a_w_s"][0]
    wsT_p = np.ascontiguousarray(ws.transpose(2, 0, 1))
    wsT_s = np.zeros((128, 8, 128), f32)
    bs_s = np.zeros((8, 128), f32)
    for s in range(4):
        wsT_s[32 * s:32 * s + 32, :, 32 * s:32 * s + 32] = wsT_p[0:32, :, 0:32]
        bs_s[:, 32 * s:32 * s + 32] = A["a_b_s"][0][:, 0:32]
    a_re, a_im, ldt = A["s5_a_re"][0], A["s5_a_im"][0], A["s5_log_dt"][0]

    def collay(m):
        return np.ascontiguousarray(m.reshape(32, 2, 64).transpose(1, 2, 0).reshape(128, 32))

    ldt2 = np.repeat(ldt[:, None], 64, axis=1)
    s5col = np.stack([collay(a_re), collay(a_im), collay(ldt2)], axis=1).astype(f32)
    s5row = np.stack([a_re.reshape(-1), a_im.reshape(-1), ldt2.reshape(-1)], axis=0).astype(f32)

    def btlay(b):
        return np.ascontiguousarray(b.reshape(8, 8, 64, 16).transpose(1, 3, 0, 2).reshape(128, 8, 64))

    def ctlay(c):
        return np.ascontiguousarray(c.reshape(32, 2, 16, 64).transpose(1, 3, 0, 2).reshape(128, 32, 16))

    Bt = np.stack([btlay(A["s5_b_re"][0]), btlay(A["s5_b_im"][0])]).astype(f32)
    Ct = np.stack([ctlay(A["s5_c_re"][0]), ctlay(A["s5_c_im"][0])]).astype(f32)
    shared = {
        "w_in_even": A["w_in_even"][0], "w_out_even": A["w_out_even"][0], "w_in_odd": A["w_in_odd"][0],
        "w_glu": A["s5_w_glu"][0], "w_out_odd": A["w_out_odd"][0], "w_up": A["ffn_w_up"], "w_down": A["ffn_w_down"],
        "c_map": np.ascontiguousarray(A["c_map"][0].reshape(1024, 256)), "cols": cols, "wsT_p": wsT_p, "wsT_s": wsT_s,
        "bs_p": np.ascontiguousarray(A["a_b_s"][0].reshape(-1)), "bs_s": bs_s.reshape(-1), "v_gain": A["a_v_norm"][0],
        "gbias": A["b_gate_bias"][0], "w_gate": A["b_w_gate"][0], "cmask": cm, "s5col": s5col, "s5row": s5row,
        "Bt": Bt, "Ct": Ct, "MB": MB, "MC": MC, "invcnt": np.zeros(4 * TT, f32), "poolcorr": pc,
    }
    shared = {k: np.ascontiguousarray(v, dtype=f32) for k, v in shared.items()}
    in_maps = []
    for c in range(8):
        xT = np.zeros((D, TT), f32)
        if c < 2:
            xT[:, :NPT] = A["x_prompt"][c].T
        sq = slice(4 * c, 4 * c + 4)
        xT[:, NPT:] = A["x_sample"][sq].reshape(128, D).T
        m = dict(shared)
        m["xT"] = xT
        m["gla_s"] = np.ascontiguousarray(A["state_gla"][0, sq].transpose(0, 2, 1, 3).reshape(4, 128, 1024))
        m["pool_s"] = np.ascontiguousarray(A["state_pool"][0, sq].reshape(4, 15, 8, 128).transpose(3, 2, 0, 1))
        m["s5re_s"] = np.ascontiguousarray(np.stack([collay(A["state_s5_re"][0, 4 * c + s]) for s in range(4)], axis=1))
        m["s5im_s"] = np.ascontiguousarray(np.stack([collay(A["state_s5_im"][0, 4 * c + s]) for s in range(4)], axis=1))
        m["ffn_s"] = np.ascontiguousarray(A["state_ffn_conv"][:, sq].reshape(2, 4, 2, NFC, 128).transpose(4, 0, 3, 1, 2))
        in_maps.append({k: np.ascontiguousarray(v, dtype=f32) for k, v in m.items()})
    res = run_bass_kernel_spmd(nc, in_maps, core_ids=list(range(8)))
    R = res.results
    _NC_CACHE["R"] = R
    y_prompt = np.stack([R[c]["yT"][:, :NPT].T for c in range(2)]).astype(f32)
    y_sample = np.concatenate([R[c]["yT"][:, NPT:].T.reshape(4, 32, D) for c in range(8)]).astype(f32)

    def gl(a):
        return a.reshape(128, 4, 256).transpose(1, 0, 2)

    def uncol(a):
        return a.reshape(2, 64, 32).transpose(2, 0, 1).reshape(64, 64)

    gla_prompt = np.stack([gl(R[c]["gla_p"]) for c in range(2)])[None].astype(f32)
    gla_sample = np.concatenate([np.stack([gl(R[c]["gla_o"][s]) for s in range(4)]) for c in range(8)])[None].astype(f32)
    av_sample = np.concatenate([R[c]["av_o"].reshape(4, 32, 1024) for c in range(8)])[None].astype(f32)
    pool_prompt = np.stack([R[c]["pool_p"].transpose(2, 1, 0).reshape(15, 1024) for c in range(2)])[None].astype(f32)
    pool_sample = np.concatenate([R[c]["pool_o"].transpose(2, 3, 1, 0).reshape(4, 15, 1024) for c in range(8)])[None].astype(f32)
    s5re_p = np.stack([uncol(R[c]["s5_p"][:, 0]) for c in range(2)])[None].astype(f32)
    s5im_p = np.stack([uncol(R[c]["s5_p"][:, 1]) for c in range(2)])[None].astype(f32)
    s5re_s = np.concatenate([np.stack([uncol(R[c]["s5_o"][:, 0, s]) for s in range(4)]) for c in range(8)])[None].astype(f32)
    s5im_s = np.concatenate([np.stack([uncol(R[c]["s5_o"][:, 1, s]) for s in range(4)]) for c in range(8)])[None].astype(f32)
    ffn_prompt = np.stack([np.stack([R[c]["ffn_p"][:, l].transpose(2, 1, 0).reshape(2, DFF) for c in range(2)]) for l in range(2)]).astype(f32)
    ffn_sample = np.stack([np.concatenate([R[c]["ffn_o"][:, l].transpose(2, 3, 1, 0).reshape(4, 2, DFF) for c in range(8)]) for l in range(2)]).astype(f32)
    return (y_prompt, y_sample, gla_prompt, gla_sample, av_sample, pool_prompt, pool_sample,
            s5re_p, s5im_p, s5re_s, s5im_s, ffn_prompt, ffn_sample)
```

```python
import numpy as np
import concourse.bass as bass
import concourse.mybir as mybir
from concourse.bass_utils import run_bass_kernel_spmd

F32 = mybir.dt.float32
BF16 = mybir.dt.bfloat16
I32 = mybir.dt.int32
AF = mybir.ActivationFunctionType
ALU = mybir.AluOpType


class Res:
    __slots__ = ("w", "readers", "name")

    def __init__(self, name=""):
        self.w = None
        self.readers = {}
        self.name = name


class Op:
    __slots__ = ("eng", "fn", "deps", "signal", "val", "sem", "is_dma", "key")

    def __init__(self, eng, fn):
        self.eng = eng
        self.fn = fn
        self.deps = []
        self.signal = False
        self.val = None
        self.sem = None
        self.is_dma = False
        self.key = eng


class Sched:
    ENG = ("pe", "act", "dve", "pool", "sp")

    def __init__(self, nc):
        self.nc = nc
        self.ops = {e: [] for e in self.ENG}
        self.esem = {e: nc.alloc_semaphore("es_" + e) for e in self.ENG}
        self.dma_sems = {}
        self.dma_tot = {}
        self.final = []

    def dsem(self, name):
        if name not in self.dma_sems:
            self.dma_sems[name] = self.nc.alloc_semaphore("ds_" + name)
            self.dma_tot[name] = 0
        return name

    def op(self, eng, fn, reads=(), writes=(), extra=()):
        o = Op(eng, fn)
        deps = []
        for r in reads:
            if r.w is not None:
                deps.append(r.w)
        for w in writes:
            deps.extend(w.readers.values())
            if w.w is not None:
                deps.append(w.w)
        deps.extend(extra)
        seen = set()
        for d in deps:
            if d is o or id(d) in seen:
                continue
            seen.add(id(d))
            if d.eng == "pe" and eng == "pe" and not d.is_dma:
                continue
            if d.eng == eng and eng == "sp" and not d.is_dma:
                continue
            o.deps.append(d)
            d.signal = True
        for r in reads:
            r.readers[o.key] = o
        for w in writes:
            w.w = o
            w.readers = {}
        self.ops[eng].append(o)
        return o

    def dma(self, eng, semname, out, in_, reads=(), writes=(), extra=(), final=False):
        self.dsem(semname)
        self.dma_tot[semname] += 16
        val = self.dma_tot[semname]
        sem = self.dma_sems[semname]

        def fn(e, out=out, in_=in_, sem=sem):
            return e.dma_start(out=out, in_=in_).then_inc(sem, 16)

        o = Op(eng, fn)
        o.is_dma = True
        o.sem = sem
        o.val = val
        o.key = "dma_" + semname
        deps = []
        for r in reads:
            if r.w is not None:
                deps.append(r.w)
        for w in writes:
            deps.extend(w.readers.values())
            if w.w is not None:
                deps.append(w.w)
        deps.extend(extra)
        seen = set()
        for d in deps:
            if id(d) in seen:
                continue
            seen.add(id(d))
            o.deps.append(d)
            d.signal = True
        for r in reads:
            r.readers[o.key] = o
        for w in writes:
            w.w = o
            w.readers = {}
        self.ops[eng].append(o)
        if final:
            self.final.append(o)
        return o

    def finalize(self):
        for e in self.ENG:
            c = 0
            for o in self.ops[e]:
                if o.is_dma:
                    continue
                if o.signal:
                    c += 1
                    o.val = c
                    o.sem = self.esem[e]

    def emit(self):
        self.finalize()
        nc = self.nc
        sched = self

        def run(e, name):
            waited = {}
            for o in sched.ops[name]:
                for d in o.deps:
                    sid = id(d.sem)
                    if waited.get(sid, 0) >= d.val:
                        continue
                    waited[sid] = d.val
                    e.wait_ge(d.sem, d.val)
                ins = o.fn(e)
                if (not o.is_dma) and o.signal:
                    ins.then_inc(o.sem, 1)
            if name == "sp":
                for o in sched.final:
                    sid = id(o.sem)
                    if waited.get(sid, 0) >= o.val:
                        continue
                    waited[sid] = o.val
                    e.wait_ge(o.sem, o.val)

        with nc.Block() as block:
            @block.sync
            def _(e):
                run(e, "sp")

            @block.tensor
            def _(e):
                run(e, "pe")

            @block.scalar
            def _(e):
                run(e, "act")

            @block.vector
            def _(e):
                run(e, "dve")

            @block.gpsimd
            def _(e):
                run(e, "pool")


import math

D = 2048
NPT = 4096
NST = 128
TT = NPT + NST
DFF = 5632
NFC = 44
EPS = 1e-6
PI = math.pi
NSLOT = 2
SLOT_E = 8192
RING_E = 18432
LB = 64

C_NORM = 0
C_OG = 128
C_CS = 136
C_SD = 144
C_CW = 152
C_CB = C_CW + 264
NCOL = C_CB + 88


class Ctx:
    pass


def build_program(debug=False):
    nc = bass.Bass("TRN2", target_bir_lowering=False)
    K = Sched(nc)
    X = Ctx()
    X.nc, X.K = nc, K

    def din(name, shape, dt=F32):
        return nc.dram_tensor(name, list(shape), dt, kind="ExternalInput").ap()

    def dout(name, shape, dt=F32):
        return nc.dram_tensor(name, list(shape), dt, kind="ExternalOutput").ap()

    I = {}
    I["xT"] = din("xT", [D, TT])
    I["w_in_even"] = din("w_in_even", [D, 5136])
    I["w_out_even"] = din("w_out_even", [D, D])
    I["w_in_odd"] = din("w_in_odd", [D, D])
    I["w_glu"] = din("w_glu", [1024, 2048])
    I["w_out_odd"] = din("w_out_odd", [D, D])
    I["w_up"] = din("w_up", [2, D, 2 * DFF])
    I["w_down"] = din("w_down", [2, DFF, D])
    I["c_map"] = din("c_map", [1024, 256])
    I["cols"] = din("cols", [128, NCOL])
    I["wsT_p"] = din("wsT_p", [128, 8, 128])
    I["wsT_s"] = din("wsT_s", [128, 8, 128])
    I["bs_p"] = din("bs_p", [1024])
    I["bs_s"] = din("bs_s", [1024])
    I["v_gain"] = din("v_gain", [1024])
    I["gbias"] = din("gbias", [512])
    I["w_gate"] = din("w_gate", [16, 512])
    I["cmask"] = din("cmask", [128, 2, 136])
    I["gla_s"] = din("gla_s", [4, 128, 1024])
    I["pool_s"] = din("pool_s", [128, 8, 4, 15])
    I["s5re_s"] = din("s5re_s", [128, 4, 32])
    I["s5im_s"] = din("s5im_s", [128, 4, 32])
    I["ffn_s"] = din("ffn_s", [128, 2, NFC, 4, 2])
    I["s5col"] = din("s5col", [128, 3, 32])
    I["s5row"] = din("s5row", [3, 4096])
    I["Bt"] = din("Bt", [2, 128, 8, 64])
    I["Ct"] = din("Ct", [2, 128, 32, 16])
    I["MB"] = din("MB", [128, 4, 2])
    I["MC"] = din("MC", [128, 2, 4, 8])
    I["invcnt"] = din("invcnt", [4 * TT])

    O = {}
    O["yT"] = dout("yT", [D, TT])
    O["gla_p"] = dout("gla_p", [128, 1024])
    O["gla_o"] = dout("gla_o", [4, 128, 1024])
    O["av_o"] = dout("av_o", [128, 1024])
    O["pool_p"] = dout("pool_p", [128, 8, 15])
    O["pool_o"] = dout("pool_o", [128, 8, 4, 15])
    O["s5_p"] = dout("s5_p", [128, 2, 32])
    O["s5_o"] = dout("s5_o", [128, 2, 4, 32])
    O["ffn_p"] = dout("ffn_p", [128, 2, NFC, 2])
    O["ffn_o"] = dout("ffn_o", [128, 2, NFC, 4, 2])
    X.I, X.O = I, O

    dk = dict(kind="ExternalOutput") if debug else {}
    X.ws5 = nc.dram_tensor("ws5", [4, 128, 8 * 4 * 128], BF16, **dk).ap()
    X.ptab = nc.dram_tensor("ptab", [128, 3 * 32 * LB], F32, **dk).ap()

    sb = lambda name, shape, dt: nc.alloc_sbuf_tensor('s_' + name, shape, dt)
    X.ring = sb("wring", [128, RING_E], BF16)
    X.rpos = 0
    X.live = []
    X.wcnt = 0
    X.xT = sb("xTs", [128, 16, 512], F32)
    X.rx = Res("x")
    X.cols = sb("cols", [128, NCOL], F32)
    X.rcols = Res("cols")
    X.ones = sb("ones", [128, 128], BF16)
    X.rones = Res("ones")
    X.wmT = [sb("wmT_p", [128, 8, 128], BF16), sb("wmT_s", [128, 8, 128], BF16)]
    X.rwmT = Res("wmT")
    X.wgate = sb("wgate", [16, 512], BF16)
    X.cmask = sb("cmask", [128, 2, 136], F32)
    X.rconst = Res("const")
    X.S = sb("glaS", [128, 1024], F32)
    X.rS = Res("S")
    X.Sbf = sb("glaSbf", [128, 1024], BF16)
    X.rSbf = Res("Sbf")
    X.phalo = sb("phalo", [128, 8, 15], F32)
    X.rphalo = Res("phalo")
    X.chalo = sb("chalo", [128, 2, NFC, 2], F32)
    X.rchalo = [Res("chalo0"), Res("chalo1")]
    X.s5st = sb("s5st", [128, 2, 32], F32)
    X.rs5st = Res("s5st")
    X.rho = sb("rho", [128, 32], F32)
    X.pend = sb("pend", [128, 4, 32], F32)
    X.pend32 = sb("pend32", [128, 4, 32], F32)
    X.sqb = [sb(f"sqb{i}", [128, 512], BF16) for i in range(2)]
    X.rsqb = [Res("sqb0"), Res("sqb1")]
    X.sqc = 0
    X.nrm = sb("nrm", [128, 2, 512], F32)
    X.rnrm = Res("nrm")
    X.tmpx = [sb(f"tmpx{i}", [128, 512], F32) for i in range(2)]
    X.rtmpx = [Res("tmpx0"), Res("tmpx1")]
    ARENA_F = 27776
    X.arena = sb("arena", [128, ARENA_F], F32)
    X.ARENA_F = ARENA_F
    X.apos = 0
    X.psum = nc.alloc_psum_tensor("ps", [128, 8, 512], F32)
    X.rps = [Res(f"ps{i}") for i in range(8)]
    X.pscnt = {}
    return X


def arena_reset(X, keep=0):
    X.apos = keep


def aalloc(X, shape, dt):
    n = 1
    for s in shape[1:]:
        n *= s
    nf = n if dt == F32 else (n + 1) // 2
    nf = (nf + 7) // 8 * 8
    a0 = X.apos
    X.apos += nf
    assert X.apos <= X.ARENA_F, (X.apos, X.ARENA_F)
    v = X.arena[:, a0:a0 + nf]
    if dt != F32:
        v = v.bitcast(dt)
    v = v[:, 0:n]
    if len(shape) == 3:
        v = v.rearrange("p (a b) -> p a b", b=shape[2])
    elif len(shape) == 4:
        v = v.rearrange("p (a b c) -> p a b c", b=shape[2], c=shape[3])
    return v


def fence(X):
    K = X.K
    last = list(getattr(X, "arena_dmas", []))
    X.arena_dmas = []
    for e in ("pe", "act", "dve", "pool"):
        if K.ops[e]:
            last.append(K.ops[e][-1])
    for e in ("pe", "act", "dve", "pool"):
        K.op(e, (lambda en: (lambda eng: eng.nop()))(e), extra=last)
    X.fence_ops = [K.ops[e][-1] for e in ("pe", "act", "dve", "pool")]


def bank(X, group, banks):
    c = X.pscnt.get(group, 0)
    X.pscnt[group] = c + 1
    return banks[c % len(banks)]


def wslab(X, src_ap, kc, cols, need=0):
    K = X.K
    ne = kc * cols
    if X.rpos + max(ne, need) > RING_E:
        X.rpos = 0
    a0, a1 = X.rpos, X.rpos + ne
    X.rpos = a1
    extra = []
    keep = []
    for (s0, e0, rr) in X.live:
        if s0 < a1 and e0 > a0:
            extra.extend(rr.readers.values())
            if rr.w is not None:
                extra.append(rr.w)
        else:
            keep.append((s0, e0, rr))
    X.live = keep
    r = Res("wring")
    view = X.ring[:, a0:a1].rearrange("p (k c) -> p k c", c=cols)
    sem = f"w{X.wcnt % 8}"
    X.wcnt += 1
    K.dma("pool", sem, view, src_ap, writes=[r], extra=extra)
    X.live.append((a0, a1, r))
    return view, r


def col(X, idx):
    return X.cols[:, idx:idx + 1]


def norm_finish(X, n):
    K = X.K
    ps7 = X.psum[:, 7, 0:n]
    K.op("act", lambda e: e.activation(out=X.nrm[:, 1, 0:n], in_=ps7, func=AF.Sqrt, scale=1.0 / D, bias=X.epsc[:, 0:1]),
         reads=[X.rps[7], X.rconst], writes=[X.rnrm])
    K.op("dve", lambda e: e.reciprocal(out=X.nrm[:, 0, 0:n], in_=X.nrm[:, 1, 0:n]), reads=[X.rnrm], writes=[X.rnrm])


def sq_accum(X, src, rsrc, n, c, nchunks=16):
    K = X.K
    i = X.sqc % 2
    X.sqc += 1
    sq = X.sqb[i]
    K.op("act", lambda e: e.activation(out=sq[:, 0:n], in_=src, func=AF.Square), reads=[rsrc], writes=[X.rsqb[i]])
    K.op("pe", lambda e: e.matmul(X.psum[:, 7, 0:n], lhsT=X.ones[:, :], rhs=sq[:, 0:n], start=(c == 0), stop=(c == nchunks - 1)),
         reads=[X.rsqb[i], X.rones], writes=[X.rps[7]])


def pre_norm(X, layer, kind, n, hT, rh):
    K = X.K
    for c in range(16):
        sq_accum(X, X.xT[:, c, 0:n], X.rx, n, c)
    norm_finish(X, n)
    for c in range(16):
        g = col(X, C_NORM + (kind * 2 + layer) * 16 + c)
        K.op("dve", lambda e, c=c, g=g: e.scalar_tensor_tensor(out=hT[:, c, 0:n], in0=X.xT[:, c, 0:n], scalar=g, in1=X.nrm[:, 0, 0:n], op0=ALU.mult, op1=ALU.mult),
             reads=[X.rx, X.rnrm, X.rcols], writes=[rh])


def post_residual(X, layer, kind, n, fT, rf):
    K = X.K
    norm_finish(X, n)
    for c in range(16):
        g = col(X, C_NORM + (kind * 2 + layer) * 16 + c)
        i = c % 2
        t = X.tmpx[i]
        K.op("dve", lambda e, c=c, g=g, t=t: e.scalar_tensor_tensor(out=t[:, 0:n], in0=fT[:, c, 0:n], scalar=g, in1=X.nrm[:, 0, 0:n], op0=ALU.mult, op1=ALU.mult),
             reads=[rf, X.rnrm, X.rcols], writes=[X.rtmpx[i]])
        K.op("pool", lambda e, c=c, t=t: e.tensor_tensor(out=X.xT[:, c, 0:n], in0=X.xT[:, c, 0:n], in1=t[:, 0:n], op=ALU.add),
             reads=[X.rtmpx[i], X.rx], writes=[X.rx])


def mm_fm(X, b, slab, rslab, j, src, rsrc, n, kc, extra_reads=()):
    K = X.K
    for k in range(kc):
        K.op("pe", lambda e, k=k: e.matmul(X.psum[:, b, 0:n], lhsT=slab[:, k, j * 128:(j + 1) * 128], rhs=src[:, k, 0:n], start=(k == 0), stop=(k == kc - 1)),
             reads=[rslab, rsrc], writes=[X.rps[b]])


def out_proj(X, wsrc, n, catT, rcat, fT, rf, kc=16):
    K = X.K
    wv = wsrc.rearrange("(k p) c -> p k c", p=128)
    cols_per = 256 if kc == 16 else 128
    nsl = 2048 // cols_per
    c = 0
    for s in range(nsl):
        slab, rs = wslab(X, wv[:, :, s * cols_per:(s + 1) * cols_per], kc, cols_per)
        for j in range(cols_per // 128):
            b = bank(X, "main", [0, 1, 2, 3, 4, 5])
            mm_fm(X, b, slab, rs, j, catT, rcat, n, kc)
            K.op("act", lambda e, b=b, c=c: e.copy(out=fT[:, c, 0:n], in_=X.psum[:, b, 0:n]), reads=[X.rps[b]], writes=[rf])
            sq_accum(X, X.psum[:, b, 0:n], X.rps[b], n, c)
            c += 1


def setup(X):
    K, nc, I = X.K, X.nc, X.I
    arena_reset(X)
    X.epsc = nc.alloc_sbuf_tensor("s_epsc", [128, 4], F32)
    K.op("dve", lambda e: e.memset(X.epsc[:, 0:1], EPS), writes=[X.rconst])
    K.op("dve", lambda e: e.memset(X.epsc[:, 1:2], 1.0), writes=[X.rconst])
    K.op("dve", lambda e: e.memset(X.epsc[:, 2:3], 0.0), writes=[X.rconst])
    K.op("dve", lambda e: e.memset(X.ones[:, :], 1.0), writes=[X.rones])
    K.dma("sp", "cols", X.cols[:, :], I["cols"], writes=[X.rcols])
    K.dma("sp", "cmask", X.cmask[:, :, :], I["cmask"], writes=[X.rconst])
    t0 = aalloc(X, [128, 8, 128], F32)
    t1 = aalloc(X, [128, 8, 128], F32)
    tg = aalloc(X, [128, 512], F32)
    r0, r1, rg = Res(), Res(), Res()
    K.dma("sp", "su0", t0, I["wsT_p"], writes=[r0])
    K.dma("sp", "su1", t1, I["wsT_s"], writes=[r1])
    K.dma("sp", "su2", tg[0:16, :], I["w_gate"], writes=[rg])
    K.op("dve", lambda e: e.memset(t0[64:128, :, 0:64], 0.0), reads=[r0], writes=[r0])
    K.op("dve", lambda e: e.tensor_copy(out=X.wmT[0][:, :, :], in_=t0), reads=[r0], writes=[X.rwmT])
    K.op("dve", lambda e: e.tensor_copy(out=X.wmT[1][:, :, :], in_=t1), reads=[r1], writes=[X.rwmT])
    K.op("dve", lambda e: e.tensor_copy(out=X.wgate[:, :], in_=tg[0:16, :]), reads=[rg], writes=[X.rconst])
    K.op("dve", lambda e: e.memset(X.S[:, :], 0.0), writes=[X.rS])
    K.op("dve", lambda e: e.memset(X.phalo[:, :, :], 0.0), writes=[X.rphalo])
    K.op("dve", lambda e: e.memset(X.chalo[:, 0, :, :], 0.0), writes=[X.rchalo[0]])
    K.op("dve", lambda e: e.memset(X.chalo[:, 1, :, :], 0.0), writes=[X.rchalo[1]])
    K.op("dve", lambda e: e.memset(X.s5st[:, :, :], 0.0), writes=[X.rs5st])
    s5_setup(X)


def s5_params(X, are, aim, ldt, shape, need_f, R=None):
    K = X.K
    R = R if R is not None else Res()
    mk = lambda: aalloc(X, shape, F32)
    dt, rho, th, sn, cs, y, kf, m = mk(), mk(), mk(), mk(), mk(), mk(), mk(), mk()
    ki = aalloc(X, shape, F32).bitcast(I32)
    rd = [X.rconst]
    K.op("act", lambda e: e.activation(out=dt, in_=ldt, func=AF.Exp), reads=[R] + rd, writes=[R])
    K.op("dve", lambda e: e.tensor_tensor(out=rho, in0=are, in1=dt, op=ALU.mult), reads=[R], writes=[R])
    K.op("act", lambda e: e.activation(out=rho, in_=rho, func=AF.Exp), reads=[R], writes=[R])
    K.op("dve", lambda e: e.tensor_tensor(out=th, in0=aim, in1=dt, op=ALU.mult), reads=[R], writes=[R])

    def sin_of(dst, shift):
        K.op("dve", lambda e: e.tensor_scalar(out=y, in0=th, scalar1=shift, scalar2=None, op0=ALU.add), reads=[R], writes=[R])
        K.op("dve", lambda e: e.tensor_scalar(out=ki, in0=y, scalar1=1.0 / (2 * PI), scalar2=None, op0=ALU.mult), reads=[R], writes=[R])
        K.op("dve", lambda e: e.tensor_copy(out=kf, in_=ki), reads=[R], writes=[R])
        K.op("dve", lambda e: e.scalar_tensor_tensor(out=y, in0=kf, scalar=-2 * PI, in1=y, op0=ALU.mult, op1=ALU.add), reads=[R], writes=[R])
        K.op("dve", lambda e: e.tensor_scalar(out=m, in0=y, scalar1=PI, scalar2=None, op0=ALU.is_gt), reads=[R], writes=[R])
        K.op("dve", lambda e: e.scalar_tensor_tensor(out=y, in0=m, scalar=-2 * PI, in1=y, op0=ALU.mult, op1=ALU.add), reads=[R], writes=[R])
        K.op("dve", lambda e: e.tensor_scalar(out=m, in0=y, scalar1=-PI, scalar2=None, op0=ALU.is_lt), reads=[R], writes=[R])
        K.op("dve", lambda e: e.scalar_tensor_tensor(out=y, in0=m, scalar=2 * PI, in1=y, op0=ALU.mult, op1=ALU.add), reads=[R], writes=[R])
        K.op("dve", lambda e: e.tensor_scalar(out=y, in0=y, scalar1=PI, scalar2=-PI, op0=ALU.min, op1=ALU.max), reads=[R], writes=[R])
        K.op("act", lambda e: e.activation(out=dst, in_=y, func=AF.Sin), reads=[R], writes=[R])

    sin_of(sn, 0.0)
    sin_of(cs, PI / 2)
    out = dict(rho=rho, sn=sn, cs=cs, R=R)
    if need_f:
        lbr, lbi, fr, fi = mk(), mk(), mk(), mk()
        tt = lambda o, a, b, op: K.op("dve", lambda e: e.tensor_tensor(out=o, in0=a, in1=b, op=op), reads=[R], writes=[R])
        tt(lbr, rho, cs, ALU.mult)
        K.op("dve", lambda e: e.tensor_scalar(out=lbr, in0=lbr, scalar1=-1.0, scalar2=None, op0=ALU.add), reads=[R], writes=[R])
        tt(lbi, rho, sn, ALU.mult)
        tt(fr, lbr, are, ALU.mult)
        tt(y, lbi, aim, ALU.mult)
        tt(fr, fr, y, ALU.add)
        tt(fi, lbi, are, ALU.mult)
        tt(y, lbr, aim, ALU.mult)
        tt(fi, fi, y, ALU.subtract)
        tt(kf, are, are, ALU.mult)
        tt(m, aim, aim, ALU.mult)
        tt(kf, kf, m, ALU.add)
        K.op("dve", lambda e: e.reciprocal(out=kf, in_=kf), reads=[R], writes=[R])
        tt(fr, fr, kf, ALU.mult)
        tt(fi, fi, kf, ALU.mult)
        out.update(fr=fr, fi=fi)
    return out


def s5_setup(X):
    K, nc, I = X.K, X.nc, X.I
    arena_reset(X, 4096)
    pc = aalloc(X, [128, 3, 32], F32)
    rpc = Res()
    K.dma("sp", "su3", pc, I["s5col"], writes=[rpc])
    P = s5_params(X, pc[:, 0, :], pc[:, 1, :], pc[:, 2, :], [128, 32], False, rpc)
    K.op("dve", lambda e: e.tensor_copy(out=X.rho[:, :], in_=P["rho"]), reads=[P["R"], rpc], writes=[X.rconst])
    tab = aalloc(X, [128, 3, 32, LB], F32)
    rt = P["R"]
    K.op("dve", lambda e: e.tensor_copy(out=tab[:, 0, :, 0], in_=P["cs"]), reads=[rt], writes=[rt])
    K.op("dve", lambda e: e.tensor_copy(out=tab[:, 1, :, 0], in_=P["sn"]), reads=[rt], writes=[rt])
    ta = aalloc(X, [128, 32, 32], F32)
    tb = aalloc(X, [128, 32, 32], F32)
    j = 1
    while j < LB:
        cr = tab[:, 0, :, j - 1:j].to_broadcast([128, 32, j])
        ci = tab[:, 1, :, j - 1:j].to_broadcast([128, 32, j])
        pr, pi = tab[:, 0, :, 0:j], tab[:, 1, :, 0:j]
        tt = lambda o, a, b, op: K.op("dve", lambda e: e.tensor_tensor(out=o, in0=a, in1=b, op=op), reads=[rt], writes=[rt])
        tt(ta[:, :, 0:j], pr, cr, ALU.mult)
        tt(tb[:, :, 0:j], pi, ci, ALU.mult)
        tt(tab[:, 0, :, j:2 * j], ta[:, :, 0:j], tb[:, :, 0:j], ALU.subtract)
        tt(ta[:, :, 0:j], pr, ci, ALU.mult)
        tt(tb[:, :, 0:j], pi, cr, ALU.mult)
        tt(tab[:, 1, :, j:2 * j], ta[:, :, 0:j], tb[:, :, 0:j], ALU.add)
        j *= 2
    K.op("dve", lambda e: e.tensor_scalar(out=tab[:, 2, :, :], in0=tab[:, 1, :, :], scalar1=-1.0, scalar2=None, op0=ALU.mult), reads=[rt], writes=[rt])
    for a in range(4):
        K.op("dve", lambda e, a=a: e.tensor_copy(out=X.pend[:, a, :], in_=tab[:, a % 3, :, LB - 1]), reads=[rt], writes=[X.rconst])
        K.op("dve", lambda e, a=a: e.tensor_copy(out=X.pend32[:, a, :], in_=tab[:, a % 3, :, 31]), reads=[rt], writes=[X.rconst])
    X.ptab_w = K.dma("sp", "ptab", X.ptab, tab.rearrange("p a c l -> p (a c l)"), reads=[rt])
    X.arena_dmas = [X.ptab_w]
    fence(X)
    arena_reset(X, 4096)
    Bt = aalloc(X, [128, 2, 8, 64], F32)
    Ct = aalloc(X, [128, 2, 32, 16], F32)
    MB = aalloc(X, [128, 4, 2], F32)
    MC = aalloc(X, [128, 2, 4, 8], F32)
    rB = Res()
    K.dma("sp", "su4", Bt, I["Bt"].rearrange("a p f q -> p a f q"), writes=[rB], extra=X.fence_ops)
    K.dma("sp", "su5", Ct, I["Ct"].rearrange("a p c h -> p a c h"), writes=[rB], extra=X.fence_ops)
    K.dma("sp", "su6", MB, I["MB"], writes=[rB], extra=X.fence_ops)
    K.dma("sp", "su7", MC, I["MC"], writes=[rB], extra=X.fence_ops)
    base = X.apos
    X.ws5_w = []
    for slab_i in range(4):
        if slab_i > 0:
            X.arena_dmas.append(X.ws5_w[-1])
            fence(X)
        arena_reset(X, base)
        row = aalloc(X, [128, 3, 1024], F32)
        rrow = Res()
        for a in range(3):
            K.dma("sp", "su8", row[:, a, :], I["s5row"][a, slab_i * 1024:(slab_i + 1) * 1024].partition_broadcast(128), writes=[rrow],
                  extra=X.fence_ops)
        R0 = Res()
        K.op("dve", lambda e, row=row: e.tensor_copy(out=row[:, 2, 0:1], in_=row[:, 2, 0:1]), reads=[rrow, rB], writes=[R0])
        Pm = s5_params(X, row[:, 0, :], row[:, 1, :], row[:, 2, :], [128, 1024], True, R0)
        R = Pm["R"]
        fr = Pm["fr"].rearrange("p (c g q) -> p c g q", g=2, q=64)
        fi = Pm["fi"].rearrange("p (c g q) -> p c g q", g=2, q=64)
        stg = aalloc(X, [128, 8, 4, 128], BF16)
        t1 = aalloc(X, [128, 4, 2, 64], F32)
        t2 = aalloc(X, [128, 4, 2, 64], F32)
        rs = Res()
        for half in range(2):
            ccl0 = half * 4
            cc0 = slab_i * 8 + ccl0
            fc = cc0 // 4
            fre = fr[:, ccl0:ccl0 + 4, :, :]
            fim = fi[:, ccl0:ccl0 + 4, :, :]
            bre = Bt[:, 0, fc, :].unsqueeze(1).unsqueeze(1).to_broadcast([128, 4, 2, 64])
            bim = Bt[:, 1, fc, :].unsqueeze(1).unsqueeze(1).to_broadcast([128, 4, 2, 64])
            mb = MB[:, :, :].unsqueeze(3).to_broadcast([128, 4, 2, 64])
            tt = lambda o, a, b, op, R=R, rs=rs: K.op("dve", lambda e: e.tensor_tensor(out=o, in0=a, in1=b, op=op), reads=[R, rB, rs], writes=[rs])
            dst = lambda arr, half=half, stg=stg: stg[:, half * 4:half * 4 + 4, arr, :].rearrange("p c (g q) -> p c g q", g=2)
            tt(t1, fre, bre, ALU.mult)
            tt(t2, fim, bim, ALU.mult)
            tt(t1, t1, t2, ALU.subtract)
            tt(dst(0), t1, mb, ALU.mult)
            tt(t1, fre, bim, ALU.mult)
            tt(t2, fim, bre, ALU.mult)
            tt(t1, t1, t2, ALU.add)
            tt(dst(1), t1, mb, ALU.mult)
            for arr, (a, sgn) in enumerate([(0, 0), (1, 1)]):
                cin = Ct[:, a, cc0:cc0 + 4, :].unsqueeze(2).to_broadcast([128, 4, 8, 16])
                mc = MC[:, sgn, :, :].unsqueeze(3).to_broadcast([128, 4, 8, 16])
                d = stg[:, half * 4:half * 4 + 4, 2 + arr, :].rearrange("p c (g h) -> p c g h", g=8)
                tt(d, cin, mc, ALU.mult)
        X.ws5_w.append(K.dma("sp", "ws5", X.ws5[slab_i], stg.rearrange("p c a m -> p (c a m)"), reads=[rs]))
    X.arena_dmas.append(X.ws5_w[-1])
    fence(X)


def even_mixer(X, gi, t0, n, typ, last_prompt):
    K, nc, I, O = X.K, X.nc, X.I, X.O
    nt = n // 128
    nch = 2 if typ == 0 else 4
    Lc = 128 // nch
    fence(X)
    arena_reset(X)
    catT = aalloc(X, [128, 16, n], BF16); rcat = Res("cat")
    X.cat_keep = X.apos
    hT = aalloc(X, [128, 16, n], BF16); rh = Res("h")
    auT = aalloc(X, [128, 8, n], BF16); rau = Res()
    avbf = aalloc(X, [128, nt, 1024], BF16); ravbf = Res()
    avg = aalloc(X, [128, 1024], F32); ravg = Res()
    qT = aalloc(X, [128, 4, n], BF16); rq = Res()
    k32 = aalloc(X, [128, nt, 512], F32); rk = Res()
    vbf = aalloc(X, [128, nt, 1024], BF16); rv = Res()
    sr = aalloc(X, [128, 8, n], BF16); rsr = Res()
    glr = aalloc(X, [128, n], BF16); rglr = Res()
    sp = X.tmpx[0][:, :]; rsp = X.rtmpx[0]
    exr = X.tmpx[1][:, :]; rexr = X.rtmpx[1]
    kd = [aalloc(X, [128, 512], BF16) for _ in range(2)]; rkd = [Res(), Res()]
    osq = aalloc(X, [128, 8, 128], BF16); rosq = Res()
    otmp = aalloc(X, [128, 8, 128], F32); rot = Res()
    bst = aalloc(X, [128, 8, 128], F32)
    vgt = aalloc(X, [128, 1024], F32)
    gbt = aalloc(X, [128, 512], F32)
    st = aalloc(X, [128, 32], F32); rst = Res()
    dec = aalloc(X, [128, 4, 4], F32); rdec = Res()
    rnrm2 = X.rnrm
    nrm2 = X.nrm
    rbt = Res()
    fo = X.fence_ops
    K.dma("sp", "e0", bst.rearrange("p h i -> p (h i)"), I["bs_p" if typ == 0 else "bs_s"].partition_broadcast(128), writes=[rbt], extra=fo)
    K.dma("sp", "e1", vgt, I["v_gain"].partition_broadcast(128), writes=[rbt], extra=fo)
    K.dma("sp", "e2", gbt, I["gbias"].partition_broadcast(128), writes=[rbt], extra=fo)

    pre_norm(X, 0, 0, n, hT, rh)
    W = I["w_in_even"].rearrange("(k p) c -> p k c", p=128)
    MB_ = [0, 1, 2, 3, 4, 5]

    def fm_slab(c0, ncols, evac):
        slab, rs = wslab(X, W[:, :, c0:c0 + ncols], 16, ncols)
        for j in range(max(1, ncols // 128)):
            b = bank(X, "main", MB_)
            if ncols >= 128:
                mm_fm(X, b, slab, rs, j, hT, rh, n, 16)
            else:
                for k in range(16):
                    K.op("pe", lambda e, k=k, b=b: e.matmul(X.psum[0:ncols, b, 0:n], lhsT=slab[:, k, 0:ncols], rhs=hT[:, k, 0:n], start=(k == 0), stop=(k == 15)),
                         reads=[rs, rh], writes=[X.rps[b]])
            evac(b, c0 // 128 + j if ncols >= 128 else 0)

    def tm_group(b, slab, rs, tt):
        for k in range(16):
            K.op("pe", lambda e, k=k: e.matmul(X.psum[:, b, 0:512], lhsT=hT[:, k, tt * 128:(tt + 1) * 128], rhs=slab[:, k, 0:512], start=(k == 0), stop=(k == 15)),
                 reads=[rs, rh], writes=[X.rps[b]])

    for s in range(4):
        fm_slab(s * 256, 256, lambda b, c: K.op("act", lambda e: e.activation(out=auT[:, c, 0:n], in_=X.psum[:, b, 0:n], func=AF.Gelu_apprx_tanh), reads=[X.rps[b]], writes=[rau]))
    sl0, rs0 = wslab(X, W[:, :, 1024:1536], 16, 512, need=16384)
    sl1, rs1 = wslab(X, W[:, :, 1536:2048], 16, 512)
    for tt in range(nt):
        for hh, (sl, rs) in enumerate([(sl0, rs0), (sl1, rs1)]):
            b = bank(X, "main", MB_)
            tm_group(b, sl, rs, tt)
            K.op("act", lambda e, b=b, hh=hh: e.activation(out=avg[:, hh * 512:(hh + 1) * 512], in_=X.psum[:, b, 0:512], func=AF.Gelu_apprx_tanh), reads=[X.rps[b]], writes=[ravg])
        for hh in range(2):
            K.op("dve", lambda e, hh=hh: e.bn_stats(out=st[:, hh * 6:(hh + 1) * 6], in_=avg[:, hh * 512:(hh + 1) * 512]), reads=[ravg], writes=[rst])
        K.op("dve", lambda e: e.bn_aggr(out=st[:, 12:14], in_=st[:, 0:12]), reads=[rst], writes=[rst])
        K.op("act", lambda e: e.activation(out=st[:, 14:15], in_=st[:, 13:14], func=AF.Sqrt, bias=X.epsc[:, 0:1]), reads=[rst, X.rconst], writes=[rst])
        K.op("dve", lambda e: e.reciprocal(out=st[:, 15:16], in_=st[:, 14:15]), reads=[rst], writes=[rst])
        K.op("dve", lambda e: e.tensor_scalar(out=avg[:, :], in0=avg[:, :], scalar1=st[:, 12:13], scalar2=st[:, 15:16], op0=ALU.subtract, op1=ALU.mult), reads=[rst, ravg], writes=[ravg])
        K.op("dve", lambda e: e.tensor_tensor(out=avg[:, :], in0=avg[:, :], in1=vgt, op=ALU.mult), reads=[ravg, rbt], writes=[ravg])
        K.op("act", lambda e, tt=tt: e.copy(out=avbf[:, tt, :], in_=avg[:, :]), reads=[ravg], writes=[ravbf])
        if typ == 1:
            X.arena_dmas.append(K.dma("sp", "avo", O["av_o"], avg[:, :], reads=[ravg], final=True))
    for s in range(2):
      fm_slab(2048 + s * 256, 256, lambda b, c: K.op("act", lambda e: e.activation(out=qT[:, c - 16, 0:n], in_=X.psum[:, b, 0:n], func=AF.Copy, scale=128.0 ** -0.5), reads=[X.rps[b]], writes=[rq]))
    sl, rs = wslab(X, W[:, :, 2560:3072], 16, 512)
    for tt in range(nt):
        b = bank(X, "main", MB_)
        tm_group(b, sl, rs, tt)
        K.op("act", lambda e, b=b, tt=tt: e.copy(out=k32[:, tt, :], in_=X.psum[:, b, 0:512]), reads=[X.rps[b]], writes=[rk])
    for hh in range(2):
        sl, rs = wslab(X, W[:, :, 3072 + hh * 512:3584 + hh * 512], 16, 512)
        for tt in range(nt):
            b = bank(X, "main", MB_)
            tm_group(b, sl, rs, tt)
            K.op("act", lambda e, b=b, tt=tt, hh=hh: e.copy(out=vbf[:, tt, hh * 512:(hh + 1) * 512], in_=X.psum[:, b, 0:512]), reads=[X.rps[b]], writes=[rv])
    for s in range(4):
        fm_slab(4096 + s * 256, 256, lambda b, c: K.op("act", lambda e: e.activation(out=sr[:, c - 32, 0:n], in_=X.psum[:, b, 0:n], func=AF.Silu), reads=[X.rps[b]], writes=[rsr]))
    fm_slab(5120, 16, lambda b, c: K.op("act", lambda e: e.copy(out=glr[0:16, 0:n], in_=X.psum[0:16, b, 0:n]), reads=[X.rps[b]], writes=[rglr]))

    for tt in range(nt):
        for hp in range(2):
            b = bank(X, "main", MB_)
            for h4 in range(4):
                h = hp * 4 + h4
                K.op("pe", lambda e, b=b, h=h, h4=h4, tt=tt: e.matmul(X.psum[:, b, h4 * 128:(h4 + 1) * 128], lhsT=avbf[:, tt, h * 128:(h + 1) * 128], rhs=X.wmT[typ][:, h, :], start=True, stop=True),
                     reads=[ravbf, X.rwmT], writes=[X.rps[b]])
            K.op("dve", lambda e, b=b, hp=hp: e.tensor_tensor(out=otmp[:, 0:4, :], in0=X.psum[:, b, :].rearrange("p (a i) -> p a i", i=128), in1=bst[:, hp * 4:hp * 4 + 4, :], op=ALU.add),
                 reads=[X.rps[b], rbt], writes=[rot])
            K.op("dve", lambda e, hp=hp, tt=tt: e.tensor_tensor(out=catT[:, hp * 4:hp * 4 + 4, tt * 128:(tt + 1) * 128], in0=otmp[:, 0:4, :], in1=auT[:, hp * 4:hp * 4 + 4, tt * 128:(tt + 1) * 128], op=ALU.mult),
                 reads=[rot, rau], writes=[rcat])

    MBg = [0, 1, 2, 3]
    mrev = X.cmask[:, typ, 0:128]
    cind = X.cmask[:, typ, 128:132]
    cmk = X.cmask[:, typ, 132:136]
    for tt in range(nt):
        tsl = slice(tt * 128, (tt + 1) * 128)
        bz = bank(X, "gla", MBg)
        K.op("pe", lambda e, bz=bz, tsl=tsl: e.matmul(X.psum[:, bz, 0:512], lhsT=glr[0:16, tsl], rhs=X.wgate[0:16, :], start=True, stop=True),
             reads=[rglr, X.rconst], writes=[X.rps[bz]])
        K.op("dve", lambda e, bz=bz: e.tensor_tensor(out=sp, in0=X.psum[:, bz, 0:512], in1=gbt, op=ALU.add), reads=[X.rps[bz], rbt], writes=[rsp])
        K.op("act", lambda e: e.activation(out=sp, in_=sp, func=AF.Exp, scale=-1.0), reads=[rsp], writes=[rsp])
        K.op("act", lambda e: e.activation(out=sp, in_=sp, func=AF.Ln, bias=X.epsc[:, 1:2]), reads=[rsp, X.rconst], writes=[rsp])
        br = bank(X, "gla", MBg)
        K.op("pe", lambda e, br=br: e.matmul(X.psum[:, br, 0:512], lhsT=mrev, rhs=sp, start=True, stop=True), reads=[rsp, X.rconst], writes=[X.rps[br]])
        K.op("act", lambda e, br=br: e.activation(out=exr, in_=X.psum[:, br, 0:512], func=AF.Exp), reads=[X.rps[br]], writes=[rexr])
        bt_ = bank(X, "gla", MBg)
        for h in range(4):
            K.op("pe", lambda e, bt_=bt_, h=h: e.matmul(X.psum[:, bt_, h * 4:h * 4 + 4], lhsT=sp[:, h * 128:(h + 1) * 128], rhs=cind, start=True, stop=True),
                 reads=[rsp, X.rconst], writes=[X.rps[bt_]])
        K.op("act", lambda e, bt_=bt_: e.activation(out=dec.rearrange("p a b -> p (a b)"), in_=X.psum[:, bt_, 0:16], func=AF.Exp), reads=[X.rps[bt_]], writes=[rdec])
        bo = [4, 5]
        for c in range(nch):
            i = c % 2
            K.op("dve", lambda e, c=c, i=i, tt=tt: e.scalar_tensor_tensor(out=kd[i], in0=exr, scalar=cmk[:, c:c + 1], in1=k32[:, tt, :], op0=ALU.mult, op1=ALU.mult),
                 reads=[rexr, rk, X.rconst], writes=[rkd[i]])
            if typ == 1:
                seq = tt * 4 + c
                K.dma("sp", "gls", X.S[:, :], I["gla_s"][seq], writes=[X.rS])
            bs_ = [bank(X, "gla", MBg), bank(X, "gla", MBg)]
            for h in range(4):
                K.op("pe", lambda e, h=h, i=i, tt=tt, bs_=bs_: e.matmul(X.psum[:, bs_[h // 2], (h % 2) * 256:(h % 2 + 1) * 256], lhsT=kd[i][:, h * 128:(h + 1) * 128], rhs=vbf[:, tt, h * 256:(h + 1) * 256], start=True, stop=True),
                     reads=[rkd[i], rv], writes=[X.rps[bs_[h // 2]]])
            for h in range(4):
                K.op("dve", lambda e, h=h, c=c, bs_=bs_: e.scalar_tensor_tensor(out=X.S[:, h * 256:(h + 1) * 256], in0=X.S[:, h * 256:(h + 1) * 256], scalar=dec[:, h, c:c + 1], in1=X.psum[:, bs_[h // 2], (h % 2) * 256:(h % 2 + 1) * 256], op0=ALU.mult, op1=ALU.add),
                     reads=[X.rS, rdec, X.rps[bs_[h // 2]]], writes=[X.rS])
            K.op("act", lambda e: e.copy(out=X.Sbf[:, :], in_=X.S[:, :]), reads=[X.rS], writes=[X.rSbf])
            if typ == 1:
                K.dma("sp", "glo", O["gla_o"][seq], X.S[:, :], reads=[X.rS], final=True)
            for e8 in range(8):
                h = e8 // 2
                K.op("pe", lambda e, e8=e8, h=h, c=c, tt=tt: e.matmul(X.psum[:, bo[e8 // 4], (e8 % 4) * 128 + c * Lc:(e8 % 4) * 128 + (c + 1) * Lc], lhsT=X.Sbf[:, e8 * 128:(e8 + 1) * 128], rhs=qT[:, h, tt * 128 + c * Lc:tt * 128 + (c + 1) * Lc], start=True, stop=True),
                     reads=[X.rSbf, rq], writes=[X.rps[bo[e8 // 4]]])
        for a in range(2):
            K.op("act", lambda e, a=a: e.activation(out=osq[:, a * 4:(a + 1) * 4, :], in_=X.psum[:, bo[a], :].rearrange("p (a i) -> p a i", i=128), func=AF.Square), reads=[X.rps[bo[a]]], writes=[rosq])
        bn = 6
        for h in range(4):
            for u in range(2):
                K.op("pe", lambda e, h=h, u=u, bn=bn: e.matmul(X.psum[:, bn, h * 128:(h + 1) * 128], lhsT=X.ones[:, :], rhs=osq[:, 2 * h + u, :], start=(u == 0), stop=(u == 1)),
                     reads=[rosq, X.rones], writes=[X.rps[bn]])
        K.op("act", lambda e, bn=bn: e.activation(out=nrm2[:, 1, :], in_=X.psum[:, bn, 0:512], func=AF.Sqrt, scale=1.0 / 256, bias=X.epsc[:, 0:1]), reads=[X.rps[bn], X.rconst], writes=[rnrm2])
        K.op("dve", lambda e: e.reciprocal(out=nrm2[:, 0, :], in_=nrm2[:, 1, :]), reads=[rnrm2], writes=[rnrm2])
        for e8 in range(8):
            h = e8 // 2
            K.op("dve", lambda e, e8=e8, h=h: e.scalar_tensor_tensor(out=otmp[:, e8, :], in0=X.psum[:, bo[e8 // 4], (e8 % 4) * 128:(e8 % 4 + 1) * 128], scalar=col(X, C_OG + e8), in1=nrm2[:, 0, h * 128:(h + 1) * 128], op0=ALU.mult, op1=ALU.mult),
                 reads=[X.rps[bo[e8 // 4]], rnrm2, X.rcols], writes=[rot])
        K.op("dve", lambda e, tsl=tsl: e.tensor_tensor(out=catT[:, 8:16, tsl], in0=otmp[:, :, :], in1=sr[:, :, tsl], op=ALU.mult), reads=[rot, rsr], writes=[rcat])
    if last_prompt:
        K.dma("sp", "glp", O["gla_p"], X.S[:, :], reads=[X.rS], final=True)
    fence(X)
    arena_reset(X, X.cat_keep)
    fT = aalloc(X, [128, 16, n], F32); rf = Res("f")
    out_proj(X, I["w_out_even"], n, catT, rcat, fT, rf)
    post_residual(X, 0, 1, n, fT, rf)


def conv_ffn(X, layer, gi, n, typ, last_prompt):
    K, nc, I, O = X.K, X.nc, X.I, X.O
    nseg = 1 if typ == 0 else 4
    L = n // nseg
    fence(X)
    arena_reset(X)
    actT = aalloc(X, [128, NFC, n], BF16); ract = Res("act")
    keep = X.apos
    hT = aalloc(X, [128, 16, n], BF16); rh = Res("h")
    gb = [aalloc(X, [128, nseg, L + 2], F32) for _ in range(2)]; rgb = [Res(), Res()]
    tc_ = [aalloc(X, [128, nseg, L], F32) for _ in range(2)]; rtc = [Res(), Res()]
    fo = X.fence_ops
    if typ == 1:
        fst = aalloc(X, [128, NFC, 4, 2], F32); rfst = Res()
        K.dma("sp", "f0", fst, I["ffn_s"][:, layer], writes=[rfst], extra=fo)
    pre_norm(X, layer, 2, n, hT, rh)
    W = I["w_up"][layer].rearrange("(k p) c -> p k c", p=128)
    cw = lambda t, f: col(X, C_CW + (layer * 3 + t) * NFC + f)
    cb = lambda f: col(X, C_CB + layer * NFC + f)
    for s in range(22):
        sg, rsg = wslab(X, W[:, :, s * 256:(s + 1) * 256], 16, 256, need=8192)
        sv, rsv = wslab(X, W[:, :, DFF + s * 256:DFF + (s + 1) * 256], 16, 256)
        for j in range(2):
            f = s * 2 + j
            i = f % 2
            bg = bank(X, "ffg", [0, 2, 4])
            bv = bg + 1
            mm_fm(X, bg, sg, rsg, j, hT, rh, n, 16)
            mm_fm(X, bv, sv, rsv, j, hT, rh, n, 16)
            g3 = gb[i]
            K.op("act", lambda e, bg=bg, g3=g3: e.copy(out=g3[:, :, 2:L + 2], in_=X.psum[:, bg, 0:n].rearrange("p (s l) -> p s l", l=L)), reads=[X.rps[bg]], writes=[rgb[i]])
            if typ == 0:
                K.op("pool", lambda e, g3=g3, f=f: e.tensor_copy(out=g3[:, 0, 0:2], in_=X.chalo[:, layer, f, :]), reads=[X.rchalo[layer]], writes=[rgb[i]])
                K.op("pool", lambda e, g3=g3, f=f: e.tensor_copy(out=X.chalo[:, layer, f, :], in_=g3[:, 0, L:L + 2]), reads=[rgb[i]], writes=[X.rchalo[layer]])
            else:
                K.op("pool", lambda e, g3=g3, f=f: e.tensor_copy(out=g3[:, :, 0:2], in_=fst[:, f, :, :]), reads=[rfst], writes=[rgb[i]])
                K.op("pool", lambda e, g3=g3, f=f: e.tensor_copy(out=X.ffno[:, layer, f, :, :], in_=g3[:, :, L:L + 2]), reads=[rgb[i]], writes=[X.rffno])
            t3 = tc_[i]
            K.op("act", lambda e, g3=g3, t3=t3, f=f: e.activation(out=t3, in_=g3[:, :, 0:L], func=AF.Identity, scale=cw(0, f), bias=cb(f)), reads=[rgb[i], X.rcols], writes=[rtc[i]])
            K.op("dve", lambda e, g3=g3, t3=t3, f=f: e.scalar_tensor_tensor(out=t3, in0=g3[:, :, 1:L + 1], scalar=cw(1, f), in1=t3, op0=ALU.mult, op1=ALU.add), reads=[rgb[i], rtc[i], X.rcols], writes=[rtc[i]])
            K.op("dve", lambda e, g3=g3, t3=t3, f=f: e.scalar_tensor_tensor(out=t3, in0=g3[:, :, 2:L + 2], scalar=cw(2, f), in1=t3, op0=ALU.mult, op1=ALU.add), reads=[rgb[i], rtc[i], X.rcols], writes=[rtc[i]])
            K.op("act", lambda e, t3=t3: e.activation(out=t3, in_=t3, func=AF.Gelu_apprx_tanh), reads=[rtc[i]], writes=[rtc[i]])
            K.op("dve", lambda e, t3=t3, f=f, bv=bv: e.tensor_tensor(out=actT[:, f, 0:n], in0=t3.rearrange("p s l -> p (s l)"), in1=X.psum[:, bv, 0:n], op=ALU.mult), reads=[rtc[i], X.rps[bv]], writes=[ract])
    if typ == 0 and last_prompt:
        K.dma("sp", "ffp", O["ffn_p"][:, layer], X.chalo[:, layer, :, :], reads=[X.rchalo[layer]], final=True)
    if typ == 1:
        K.dma("sp", "ffo", O["ffn_o"][:, layer], X.ffno[:, layer], reads=[X.rffno], final=True)
    fence(X)
    arena_reset(X, keep)
    fT = aalloc(X, [128, 16, n], F32); rf = Res("f")
    out_proj(X, I["w_down"][layer], n, actT, ract, fT, rf, kc=NFC)
    post_residual(X, layer, 3, n, fT, rf)


def odd_mixer(X, gi, t0, n, typ, last_prompt):
    K, nc, I, O = X.K, X.nc, X.I, X.O
    nseg = 1 if typ == 0 else 4
    L = n // nseg
    Lb = LB if typ == 0 else 32
    nblk = n // Lb
    Wd = 15 + L
    fence(X)
    arena_reset(X)
    catT = aalloc(X, [128, 16, n], BF16); rcat = Res("cat")
    keep = X.apos
    hT = aalloc(X, [128, 16, n], BF16); rh = Res("h")
    cbf = [aalloc(X, [128, nseg, Wd], F32) for _ in range(2)]; rcb = [Res(), Res()]
    pt = [aalloc(X, [128, nseg, Wd], F32) for _ in range(2)]; rpt = [Res(), Res()]
    dlt = aalloc(X, [128, 8, n], BF16); rdl = Res()
    uT = aalloc(X, [128, 8, n], BF16); ru = Res()
    tab = aalloc(X, [128, 3, 32, LB], F32); rtab = Res()
    tq = [X.tmpx[0][:, 0:n], X.tmpx[1][:, 0:n]] + [aalloc(X, [128, n], F32) for _ in range(2)]; rtq = [X.rtmpx[0], X.rtmpx[1], Res(), Res()]
    wz = [aalloc(X, [128, n], F32) for _ in range(4)]; rwz = [Res() for _ in range(4)]
    pr = [aalloc(X, [128, n], BF16) for _ in range(4)]; rpr = [Res() for _ in range(4)]
    yg = aalloc(X, [128, 8, n], BF16); ryg = Res()
    sgm = aalloc(X, [128, n], F32); rsg = Res()
    se = aalloc(X, [128, 8], F32); rse = Res()
    fo = X.fence_ops
    K.dma("sp", "o0", tab.rearrange("p a c l -> p (a c l)"), X.ptab, writes=[rtab], extra=fo + [X.ptab_w])
    if typ == 1:
        pst = aalloc(X, [128, 8, 4, 15], F32); rpst = Res()
        K.dma("sp", "o1", pst, I["pool_s"], writes=[rpst], extra=fo)
        s5i = aalloc(X, [128, 2, 4, 32], F32); rs5i = Res()
        K.dma("sp", "o2", s5i[:, 0], I["s5re_s"], writes=[rs5i], extra=fo)
        K.dma("sp", "o2", s5i[:, 1], I["s5im_s"], writes=[rs5i], extra=fo)
        s5o = aalloc(X, [128, 2, 4, 32], F32); rs5o = Res()
        pso = aalloc(X, [128, 8, 4, 15], F32); rpso = Res()
    pre_norm(X, 1, 0, n, hT, rh)
    W = I["w_in_odd"].rearrange("(k p) c -> p k c", p=128)
    MB_ = [0, 1, 2, 3]
    for s in range(4):
        slab, rs = wslab(X, W[:, :, s * 256:(s + 1) * 256], 16, 256)
        for j4 in range(2):
            j = s * 2 + j4
            g = j // 2
            b = bank(X, "odd", MB_)
            mm_fm(X, b, slab, rs, j4, hT, rh, n, 16)
            i = j % 2
            cb3 = cbf[i]
            K.op("act", lambda e, b=b, cb3=cb3: e.copy(out=cb3[:, :, 15:Wd], in_=X.psum[:, b, 0:n].rearrange("p (s l) -> p s l", l=L)), reads=[X.rps[b]], writes=[rcb[i]])
            if typ == 0:
                K.op("pool", lambda e, cb3=cb3, j=j: e.tensor_copy(out=cb3[:, 0, 0:15], in_=X.phalo[:, j, :]), reads=[X.rphalo], writes=[rcb[i]])
                K.op("pool", lambda e, cb3=cb3, j=j: e.tensor_copy(out=X.phalo[:, j, :], in_=cb3[:, 0, L:L + 15]), reads=[rcb[i]], writes=[X.rphalo])
            else:
                K.op("pool", lambda e, cb3=cb3, j=j: e.tensor_copy(out=cb3[:, :, 0:15], in_=pst[:, j, :, :]), reads=[rpst], writes=[rcb[i]])
                K.op("pool", lambda e, cb3=cb3, j=j: e.tensor_copy(out=pso[:, j, :, :], in_=cb3[:, :, L:L + 15]), reads=[rcb[i]], writes=[rpso])
            src, rsrc = cb3, rcb[i]
            lo = 0
            for step in range(g + 1):
                d = 1 << step
                dst, rdst = pt[step % 2], rpt[step % 2]
                K.op("pool", lambda e, src=src, dst=dst, d=d, lo=lo: e.tensor_tensor(out=dst[:, :, lo + d:Wd], in0=src[:, :, lo + d:Wd], in1=src[:, :, lo:Wd - d], op=ALU.add),
                     reads=[rsrc], writes=[rdst])
                src, rsrc = dst, rdst
                lo += d
            w = 2 << g
            if typ == 0 and t0 == 0:
                K.op("dve", lambda e, src=src, g=g: e.tensor_tensor(out=src[:, 0, 15:31], in0=src[:, 0, 15:31], in1=X.cmaskc[:, g, :], op=ALU.mult), reads=[rsrc, X.rconst], writes=[rsrc])
            K.op("dve", lambda e, src=src, cb3=cb3, j=j, w=w: e.scalar_tensor_tensor(out=dlt[:, j, 0:n].rearrange("p (s l) -> p s l", l=L), in0=src[:, :, 15:Wd], scalar=1.0 / w, in1=cb3[:, :, 15:Wd], op0=ALU.mult, op1=ALU.subtract),
                 reads=[rsrc, rcb[i]], writes=[rdl])
    cms, rcm = wslab(X, I["c_map"].rearrange("(j p) d -> p j d", p=128), 8, 256)
    for j in range(8):
        g, hf = j // 2, j % 2
        b = bank(X, "odd", MB_)
        for kk in range(2):
            K.op("pe", lambda e, b=b, g=g, kk=kk, hf=hf: e.matmul(X.psum[:, b, 0:n], lhsT=cms[:, g * 2 + kk, hf * 128:(hf + 1) * 128], rhs=dlt[:, g * 2 + kk, 0:n], start=(kk == 0), stop=(kk == 1)),
                 reads=[rcm, rdl], writes=[X.rps[b]])
        K.op("act", lambda e, b=b, j=j: e.activation(out=catT[:, j, 0:n], in_=X.psum[:, b, 0:n], func=AF.Identity, scale=col(X, C_CS + j)), reads=[X.rps[b], X.rcols], writes=[rcat])
    if typ == 0 and last_prompt:
        K.dma("sp", "plp", O["pool_p"], X.phalo[:, :, :], reads=[X.rphalo], final=True)
    if typ == 1:
        X.arena_dmas.append(K.dma("sp", "plo", O["pool_o"], pso, reads=[rpso], final=True))
    for s in range(4):
        slab, rs = wslab(X, W[:, :, 1024 + s * 256:1024 + (s + 1) * 256], 16, 256)
        for j4 in range(2):
            j = s * 2 + j4
            b = bank(X, "odd", MB_)
            mm_fm(X, b, slab, rs, j4, hT, rh, n, 16)
            K.op("act", lambda e, b=b, j=j: e.copy(out=uT[:, j, 0:n], in_=X.psum[:, b, 0:n]), reads=[X.rps[b]], writes=[ru])
    pend = X.pend if typ == 0 else X.pend32
    v3 = lambda ap: ap.rearrange("p (b l) -> p b l", l=Lb)
    for sl in range(4):
        ws, rws = wslab(X, X.ws5[sl].rearrange("p (k c) -> p k c", c=128), 32, 128)
        if sl == 0:
            pass
        for q8 in range(8):
            cc = sl * 8 + q8
            fc = cc // 4
            by = 4 + (fc % 2)
            bre, bim = (0, 1) if cc % 2 == 0 else (2, 3)
            K.op("pe", lambda e, bre=bre, q8=q8, fc=fc, ws=ws: e.matmul(X.psum[:, bre, 0:n], lhsT=ws[:, q8 * 4 + 0, :], rhs=uT[:, fc, 0:n], start=True, stop=True), reads=[rws, ru], writes=[X.rps[bre]], extra=X.ws5_w)
            K.op("pe", lambda e, bim=bim, q8=q8, fc=fc, ws=ws: e.matmul(X.psum[:, bim, 0:n], lhsT=ws[:, q8 * 4 + 1, :], rhs=uT[:, fc, 0:n], start=True, stop=True), reads=[rws, ru], writes=[X.rps[bim]])
            Pre = tab[:, 0, cc, 0:Lb].unsqueeze(1).to_broadcast([128, nblk, Lb])
            Pim = tab[:, 1, cc, 0:Lb].unsqueeze(1).to_broadcast([128, nblk, Lb])
            PimN = tab[:, 2, cc, 0:Lb].unsqueeze(1).to_broadcast([128, nblk, Lb])
            pre3 = v3(X.psum[:, bre, 0:n]); pim3 = v3(X.psum[:, bim, 0:n])
            K.op("dve", lambda e, pre3=pre3, Pre=Pre: e.tensor_tensor(out=v3(tq[0]), in0=pre3, in1=Pre, op=ALU.mult), reads=[X.rps[bre], rtab], writes=[rtq[0]])
            K.op("dve", lambda e, pim3=pim3, Pim=Pim: e.tensor_tensor(out=v3(tq[1]), in0=pim3, in1=Pim, op=ALU.mult), reads=[X.rps[bim], rtab], writes=[rtq[1]])
            K.op("pool", lambda e: e.tensor_tensor(out=wz[0], in0=tq[0], in1=tq[1], op=ALU.add), reads=[rtq[0], rtq[1]], writes=[rwz[0]])
            K.op("dve", lambda e, pim3=pim3, Pre=Pre: e.tensor_tensor(out=v3(tq[2]), in0=pim3, in1=Pre, op=ALU.mult), reads=[X.rps[bim], rtab], writes=[rtq[2]])
            K.op("dve", lambda e, pre3=pre3, Pim=Pim: e.tensor_tensor(out=v3(tq[3]), in0=pre3, in1=Pim, op=ALU.mult), reads=[X.rps[bre], rtab], writes=[rtq[3]])
            K.op("pool", lambda e: e.tensor_tensor(out=wz[1], in0=tq[2], in1=tq[3], op=ALU.subtract), reads=[rtq[2], rtq[3]], writes=[rwz[1]])
            if X.dbg and gi == 0 and cc == X.dbg_cc:
                for ii in range(4):
                    K.dma("sp", "dbgq", X.dbgo["tq"][ii], tq[ii], reads=[rtq[ii]], final=True)
                for ii in range(2):
                    K.dma("sp", "dbgw", X.dbgo["wz"][ii], wz[ii], reads=[rwz[ii]], final=True)
            rho_b = X.rho[:, cc:cc + 1].to_broadcast([128, Lb])
            for bl in range(nblk):
                c0 = bl * Lb
                if typ == 0:
                    ire, iim = X.s5st[:, 0, cc:cc + 1], X.s5st[:, 1, cc:cc + 1]
                    rinit = X.rs5st
                    ore, oim, rout = ire, iim, X.rs5st
                else:
                    ire, iim = s5i[:, 0, bl, cc:cc + 1], s5i[:, 1, bl, cc:cc + 1]
                    rinit = rs5i
                    ore, oim, rout = s5o[:, 0, bl, cc:cc + 1], s5o[:, 1, bl, cc:cc + 1], rs5o
                K.op("dve", lambda e, c0=c0, ire=ire, rho_b=rho_b: e.tensor_tensor_scan(out=wz[2][:, c0:c0 + Lb], data0=rho_b, data1=wz[0][:, c0:c0 + Lb], initial=ire, op0=ALU.mult, op1=ALU.add),
                     reads=[rwz[0], rinit, X.rconst], writes=[rwz[2]])
                K.op("dve", lambda e, c0=c0, iim=iim, rho_b=rho_b: e.tensor_tensor_scan(out=wz[3][:, c0:c0 + Lb], data0=rho_b, data1=wz[1][:, c0:c0 + Lb], initial=iim, op0=ALU.mult, op1=ALU.add),
                     reads=[rwz[1], rinit, X.rconst], writes=[rwz[3]])
                zre_e, zim_e = wz[2][:, c0 + Lb - 1:c0 + Lb], wz[3][:, c0 + Lb - 1:c0 + Lb]
                if typ == 0:
                    o2 = X.s5st[:, 0:2, cc]
                else:
                    o2 = s5o[:, 0:2, bl, cc]
                K.op("dve", lambda e, zre_e=zre_e, cc=cc: e.tensor_scalar(out=se[:, 0:2], in0=pend[:, 0:2, cc], scalar1=zre_e, scalar2=None, op0=ALU.mult), reads=[rwz[2], X.rconst], writes=[rse])
                K.op("dve", lambda e, zim_e=zim_e, cc=cc, o2=o2: e.scalar_tensor_tensor(out=o2, in0=pend[:, 2:4, cc], scalar=zim_e, in1=se[:, 0:2], op0=ALU.mult, op1=ALU.add), reads=[rwz[3], rse, X.rconst], writes=[rout])
            if X.dbg and gi == 0 and cc == X.dbg_cc:
                for ii in range(2, 4):
                    K.dma("sp", "dbgw", X.dbgo["wz"][ii], wz[ii], reads=[rwz[ii]], final=True)
                K.dma("sp", "dbgu", X.dbgo["uT"], uT, reads=[ru], final=True)
            K.op("dve", lambda e, Pre=Pre: e.tensor_tensor(out=v3(pr[0]), in0=v3(wz[2]), in1=Pre, op=ALU.mult), reads=[rwz[2], rtab], writes=[rpr[0]])
            K.op("dve", lambda e, PimN=PimN: e.tensor_tensor(out=v3(pr[1]), in0=v3(wz[3]), in1=PimN, op=ALU.mult), reads=[rwz[3], rtab], writes=[rpr[1]])
            K.op("pool", lambda e, Pim=Pim: e.tensor_tensor(out=v3(pr[2]), in0=v3(wz[2]), in1=Pim, op=ALU.mult), reads=[rwz[2], rtab], writes=[rpr[2]])
            K.op("pool", lambda e, Pre=Pre: e.tensor_tensor(out=v3(pr[3]), in0=v3(wz[3]), in1=Pre, op=ALU.mult), reads=[rwz[3], rtab], writes=[rpr[3]])
            for a in range(4):
                K.op("pe", lambda e, a=a, q8=q8, by=by, cc=cc, ws=ws: e.matmul(X.psum[:, by, 0:n], lhsT=ws[:, q8 * 4 + (2 if a < 2 else 3), :], rhs=pr[a], start=(cc % 4 == 0 and a == 0), stop=(cc % 4 == 3 and a == 3)),
                     reads=[rws, rpr[a]], writes=[X.rps[by]])
            if cc % 4 == 3:
                K.op("dve", lambda e, fc=fc, by=by: e.scalar_tensor_tensor(out=sgm, in0=uT[:, fc, 0:n], scalar=col(X, C_SD + fc), in1=X.psum[:, by, 0:n], op0=ALU.mult, op1=ALU.add), reads=[ru, X.rps[by], X.rcols], writes=[rsg])
                K.op("act", lambda e, fc=fc: e.activation(out=yg[:, fc, 0:n], in_=sgm, func=AF.Gelu_apprx_tanh), reads=[rsg], writes=[ryg])
    if typ == 0 and last_prompt:
        K.dma("sp", "s5p", O["s5_p"], X.s5st[:, :, :], reads=[X.rs5st], final=True)
    if typ == 1:
        X.arena_dmas.append(K.dma("sp", "s5o", O["s5_o"], s5o, reads=[rs5o], final=True))
    WG = I["w_glu"].rearrange("(k p) c -> p k c", p=128)
    for p2 in range(2):
        sa, rsa = wslab(X, WG[:, :, p2 * 512:(p2 + 1) * 512], 8, 512, need=8192)
        sb_, rsb = wslab(X, WG[:, :, 1024 + p2 * 512:1024 + (p2 + 1) * 512], 8, 512)
        for j in range(4):
            ba = bank(X, "glu", [0, 2])
            bb = ba + 1
            mm_fm(X, ba, sa, rsa, j, yg, ryg, n, 8)
            mm_fm(X, bb, sb_, rsb, j, yg, ryg, n, 8)
            K.op("act", lambda e, bb=bb: e.activation(out=sgm, in_=X.psum[:, bb, 0:n], func=AF.Sigmoid), reads=[X.rps[bb]], writes=[rsg])
            K.op("dve", lambda e, ba=ba, p2=p2, j=j: e.tensor_tensor(out=catT[:, 8 + p2 * 4 + j, 0:n], in0=X.psum[:, ba, 0:n], in1=sgm, op=ALU.mult), reads=[X.rps[ba], rsg], writes=[rcat])
    fence(X)
    arena_reset(X, keep)
    fT = aalloc(X, [128, 16, n], F32); rf = Res("f")
    out_proj(X, I["w_out_odd"], n, catT, rcat, fT, rf)
    post_residual(X, 1, 1, n, fT, rf)


def build_all(ngroups_p=8, dbg=False, dbg_cc=0):
    X = build_program()
    K, nc, I, O = X.K, X.nc, X.I, X.O
    X.dbg = dbg
    X.dbg_cc = dbg_cc
    if dbg:
        X.dbgo = {"tq": nc.dram_tensor("dbg_tq", [4, 128, 512], F32, kind="ExternalOutput").ap(),
                  "wz": nc.dram_tensor("dbg_wz", [4, 128, 512], F32, kind="ExternalOutput").ap(),
                  "uT": nc.dram_tensor("dbg_uT", [128, 8, 512], BF16, kind="ExternalOutput").ap()}
    X.ffno = nc.alloc_sbuf_tensor("s_ffno", [128, 2, NFC, 4, 2], F32)
    X.rffno = Res("ffno")
    X.cmaskc = nc.alloc_sbuf_tensor("s_cmaskc", [128, 4, 16], F32)
    cm_in = nc.dram_tensor("poolcorr", [128, 4, 16], F32, kind="ExternalInput").ap()
    setup(X)
    K.dma("sp", "pcorr", X.cmaskc[:, :, :], cm_in, writes=[X.rconst])
    groups = [(g * 512, 512, 0) for g in range(ngroups_p)] + [(NPT, NST, 1)]
    xv = I["xT"].rearrange("(c p) t -> p c t", p=128)
    yv = O["yT"].rearrange("(c p) t -> p c t", p=128)
    for gi, (t0, n, typ) in enumerate(groups):
        lastp = (typ == 0 and gi == ngroups_p - 1)
        K.dma("sp", "xin", X.xT[:, :, 0:n], xv[:, :, t0:t0 + n], writes=[X.rx])
        even_mixer(X, gi, t0, n, typ, lastp)
        conv_ffn(X, 0, gi, n, typ, lastp)
        odd_mixer(X, gi, t0, n, typ, lastp)
        conv_ffn(X, 1, gi, n, typ, lastp)
        K.dma("sp", "yout", yv[:, :, t0:t0 + n], X.xT[:, :, 0:n], reads=[X.rx], final=True)
    K.emit()
    return nc


_NC_CACHE = {}


def _host_consts():
    cm = np.zeros((128, 2, 136), np.float32)
    for typ, Lc in ((0, 64), (1, 32)):
        t = np.arange(128)
        same = (t[:, None] // Lc) == (t[None, :] // Lc)
        cm[:, typ, 0:128] = np.where(same & (t[:, None] > t[None, :]), -1.0 / 16, 0.0)
        for c in range(128 // Lc):
            cm[:, typ, 128 + c] = np.where(t // Lc == c, -1.0 / 16, 0.0)
            cm[:, typ, 132 + c] = np.where(t // Lc == c, 1.0, 0.0)
    MB = np.zeros((128, 4, 2), np.float32)
    for g8 in range(8):
        for q in range(4):
            for gl in range(2):
                if g8 == 2 * q + gl:
                    MB[g8 * 16:(g8 + 1) * 16, q, gl] = 1.0
    MC = np.zeros((128, 2, 4, 8), np.float32)
    for gl in range(2):
        for q in range(4):
            MC[gl * 64:(gl + 1) * 64, 0, q, 2 * q + gl] = 1.0
            MC[gl * 64:(gl + 1) * 64, 1, q, 2 * q + gl] = -1.0
    pc = np.ones((128, 4, 16), np.float32)
    for g in range(4):
        w = 2 << g
        for t in range(16):
            pc[:, g, t] = w / min(t + 1, w)
    return cm, MB, MC, pc


def kernel(**inp):
    f32 = np.float32
    A = {k: np.asarray(v) for k, v in inp.items()}
    if "nc" not in _NC_CACHE:
        _NC_CACHE["nc"] = build_all(8)
    nc = _NC_CACHE["nc"]
    cm, MB, MC, pc = _host_consts()

    def chunkcols(v):
        return np.ascontiguousarray(v.reshape(-1, 128).T)

    cols = np.zeros((128, NCOL), f32)
    norms = [A["norm_mix_pre"], A["norm_mix_post"], A["norm_ffn_pre"], A["norm_ffn_post"]]
    for kind in range(4):
        for l in range(2):
            cols[:, C_NORM + (kind * 2 + l) * 16:C_NORM + (kind * 2 + l + 1) * 16] = chunkcols(norms[kind][l])
    cols[:, C_OG:C_OG + 8] = chunkcols(A["b_out_norm"][0])
    cols[:, C_CS:C_CS + 8] = chunkcols(A["c_scale"][0])
    cols[:, C_SD:C_SD + 8] = chunkcols(A["s5_d"][0])
    for l in range(2):
        for t in range(3):
            cols[:, C_CW + (l * 3 + t) * NFC:C_CW + (l * 3 + t + 1) * NFC] = chunkcols(A["ffn_conv_w"][l, t])
        cols[:, C_CB + l * NFC:C_CB + (l + 1) * NFC] = chunkcols(A["ffn_conv_b"][l])
    ws = A["a_w_s"][0]
    wsT_p = np.ascontiguousarray(ws.transpose(2, 0, 1))
    wsT_s = np.zeros((128, 8, 128), f32)
    bs_s = np.zeros((8, 128), f32)
    for s in range(4):
        wsT_s[32 * s:32 * s + 32, :, 32 * s:32 * s + 32] = wsT_p[0:32, :, 0:32]
        bs_s[:, 32 * s:32 * s + 32] = A["a_b_s"][0][:, 0:32]
    a_re, a_im, ldt = A["s5_a_re"][0], A["s5_a_im"][0], A["s5_log_dt"][0]

    def collay(m):
        return np.ascontiguousarray(m.reshape(32, 2, 64).transpose(1, 2, 0).reshape(128, 32))

    ldt2 = np.repeat(ldt[:, None], 64, axis=1)
    s5col = np.stack([collay(a_re), collay(a_im), collay(ldt2)], axis=1).astype(f32)
    s5row = np.stack([a_re.reshape(-1), a_im.reshape(-1), ldt2.reshape(-1)], axis=0).astype(f32)

    def btlay(b):
        return np.ascontiguousarray(b.reshape(8, 8, 64, 16).transpose(1, 3, 0, 2).reshape(128, 8, 64))

    def ctlay(c):
        return np.ascontiguousarray(c.reshape(32, 2, 16, 64).transpose(1, 3, 0, 2).reshape(128, 32, 16))

    Bt = np.stack([btlay(A["s5_b_re"][0]), btlay(A["s5_b_im"][0])]).astype(f32)
    Ct = np.stack([ctlay(A["s5_c_re"][0]), ctlay(A["s5_c_im"][0])]).astype(f32)
    shared = {
        "w_in_even": A["w_in_even"][0], "w_out_even": A["w_out_even"][0], "w_in_odd": A["w_in_odd"][0],
        "w_glu": A["s5_w_glu"][0], "w_out_odd": A["w_out_odd"][0], "w_up": A["ffn_w_up"], "w_down": A["ffn_w_down"],
        "c_map": np.ascontiguousarray(A["c_map"][0].reshape(1024, 256)), "cols": cols, "wsT_p": wsT_p, "wsT_s": wsT_s,
        "bs_p": np.ascontiguousarray(A["a_b_s"][0].reshape(-1)), "bs_s": bs_s.reshape(-1), "v_gain": A["a_v_norm"][0],
        "gbias": A["b_gate_bias"][0], "w_gate": A["b_w_gate"][0], "cmask": cm, "s5col": s5col, "s5row": s5row,
        "Bt": Bt, "Ct": Ct, "MB": MB, "MC": MC, "invcnt": np.zeros(4 * TT, f32), "poolcorr": pc,
    }
    shared = {k: np.ascontiguousarray(v, dtype=f32) for k, v in shared.items()}
    in_maps = []
    for c in range(8):
        xT = np.zeros((D, TT), f32)
        if c < 2:
            xT[:, :NPT] = A["x_prompt"][c].T
        sq = slice(4 * c, 4 * c + 4)
        xT[:, NPT:] = A["x_sample"][sq].reshape(128, D).T
        m = dict(shared)
        m["xT"] = xT
        m["gla_s"] = np.ascontiguousarray(A["state_gla"][0, sq].transpose(0, 2, 1, 3).reshape(4, 128, 1024))
        m["pool_s"] = np.ascontiguousarray(A["state_pool"][0, sq].reshape(4, 15, 8, 128).transpose(3, 2, 0, 1))
        m["s5re_s"] = np.ascontiguousarray(np.stack([collay(A["state_s5_re"][0, 4 * c + s]) for s in range(4)], axis=1))
        m["s5im_s"] = np.ascontiguousarray(np.stack([collay(A["state_s5_im"][0, 4 * c + s]) for s in range(4)], axis=1))
        m["ffn_s"] = np.ascontiguousarray(A["state_ffn_conv"][:, sq].reshape(2, 4, 2, NFC, 128).transpose(4, 0, 3, 1, 2))
        in_maps.append({k: np.ascontiguousarray(v, dtype=f32) for k, v in m.items()})
    res = run_bass_kernel_spmd(nc, in_maps, core_ids=list(range(8)))
    R = res.results
    _NC_CACHE["R"] = R
    y_prompt = np.stack([R[c]["yT"][:, :NPT].T for c in range(2)]).astype(f32)
    y_sample = np.concatenate([R[c]["yT"][:, NPT:].T.reshape(4, 32, D) for c in range(8)]).astype(f32)

    def gl(a):
        return a.reshape(128, 4, 256).transpose(1, 0, 2)

    def uncol(a):
        return a.reshape(2, 64, 32).transpose(2, 0, 1).reshape(64, 64)

    gla_prompt = np.stack([gl(R[c]["gla_p"]) for c in range(2)])[None].astype(f32)
    gla_sample = np.concatenate([np.stack([gl(R[c]["gla_o"][s]) for s in range(4)]) for c in range(8)])[None].astype(f32)
    av_sample = np.concatenate([R[c]["av_o"].reshape(4, 32, 1024) for c in range(8)])[None].astype(f32)
    pool_prompt = np.stack([R[c]["pool_p"].transpose(2, 1, 0).reshape(15, 1024) for c in range(2)])[None].astype(f32)
    pool_sample = np.concatenate([R[c]["pool_o"].transpose(2, 3, 1, 0).reshape(4, 15, 1024) for c in range(8)])[None].astype(f32)
    s5re_p = np.stack([uncol(R[c]["s5_p"][:, 0]) for c in range(2)])[None].astype(f32)
    s5im_p = np.stack([uncol(R[c]["s5_p"][:, 1]) for c in range(2)])[None].astype(f32)
    s5re_s = np.concatenate([np.stack([uncol(R[c]["s5_o"][:, 0, s]) for s in range(4)]) for c in range(8)])[None].astype(f32)
    s5im_s = np.concatenate([np.stack([uncol(R[c]["s5_o"][:, 1, s]) for s in range(4)]) for c in range(8)])[None].astype(f32)
    ffn_prompt = np.stack([np.stack([R[c]["ffn_p"][:, l].transpose(2, 1, 0).reshape(2, DFF) for c in range(2)]) for l in range(2)]).astype(f32)
    ffn_sample = np.stack([np.concatenate([R[c]["ffn_o"][:, l].transpose(2, 3, 1, 0).reshape(4, 2, DFF) for c in range(8)]) for l in range(2)]).astype(f32)
    return (y_prompt, y_sample, gla_prompt, gla_sample, av_sample, pool_prompt, pool_sample,
            s5re_p, s5im_p, s5re_s, s5im_s, ffn_prompt, ffn_sample)
```
